# Optimizing a Trainium2 kernel written in Bass

```python
import math
import jax
import jax.numpy as jnp
from jax import lax
import numpy as np

D_MODEL = 1024
BATCH = 4
SEQ = 4096
DEPTH = 1
DEC_BATCH = 32
DEC_SEQ = 8
PAST_LEN = 16384
PAGE_SIZE = 128

D_FF = 2816
D_SSM = 2 * D_MODEL
SSD_HEAD_DIM = 64
SSD_HEADS = D_SSM // SSD_HEAD_DIM
SSD_GROUPS = 4
D_STATE = 128
D_CONV = 4
CONV_DIM = D_SSM + 2 * SSD_GROUPS * D_STATE
SSD_CHUNK = 128
ATT_HEAD_DIM = 64
ATT_HEADS = D_MODEL // ATT_HEAD_DIM
KV_HEADS = 4
Q_PER_KV = ATT_HEADS // KV_HEADS
ATT_DIM = ATT_HEADS * ATT_HEAD_DIM
KV_DIM = KV_HEADS * ATT_HEAD_DIM
Q_BLOCK = 128
IN_DIM = D_SSM + CONV_DIM + SSD_HEADS + ATT_DIM + 2 * KV_DIM + ATT_HEADS + 2 * D_MODEL
RMS_EPS = 1e-6

kernel_name = 'hybrid_ssd_fox_macaron_step'


def rmsnorm(x, w):
    xf = x.astype(jnp.float32)
    y = xf * lax.rsqrt(jnp.mean(xf * xf, axis=-1, keepdims=True) + RMS_EPS)
    return (y * w.astype(jnp.float32)).astype(x.dtype)


def swiglu(x, w_gate, w_up, w_down):
    return (jax.nn.silu(x @ w_gate) * (x @ w_up)) @ w_down


def split_combined(proj):
    sizes = [D_SSM, CONV_DIM, SSD_HEADS, ATT_DIM, KV_DIM, KV_DIM, ATT_HEADS, D_MODEL, D_MODEL]
    offsets = [int(o) for o in np.cumsum(sizes)[:-1]]
    return jnp.split(proj, offsets, axis=-1)


def causal_conv(xbc, conv_prev, conv_w, conv_b):
    L = xbc.shape[1]
    xa = jnp.concatenate([conv_prev.astype(xbc.dtype), xbc], axis=1)
    y = conv_b + sum(xa[:, j:j + L] * conv_w[j] for j in range(D_CONV))
    return jax.nn.silu(y), xa[:, L:]


def ssd_scan(xh, dt, A, Bm, Cm, h0):
    f32 = jnp.float32
    b, L, nh, hp = xh.shape
    r = nh // SSD_GROUPS
    Q = min(SSD_CHUNK, L)
    pad = (-L) % Q
    x = xh.astype(f32)
    Bf = Bm.astype(f32)
    Cf = Cm.astype(f32)
    if pad:
        pw = ((0, 0), (0, pad), (0, 0), (0, 0))
        x = jnp.pad(x, pw)
        Bf = jnp.pad(Bf, pw)
        Cf = jnp.pad(Cf, pw)
        dt = jnp.pad(dt, ((0, 0), (0, pad), (0, 0)))
    nc = (L + pad) // Q
    x = x.reshape(b, nc, Q, SSD_GROUPS, r, hp)
    dtc = dt.reshape(b, nc, Q, SSD_GROUPS, r)
    Bc = Bf.reshape(b, nc, Q, SSD_GROUPS, D_STATE)
    Cc = Cf.reshape(b, nc, Q, SSD_GROUPS, D_STATE)
    a = dtc * A.reshape(SSD_GROUPS, r)
    acs = jnp.cumsum(a, axis=2)
    xdt = x * dtc[..., None]
    causal = jnp.tril(jnp.ones((Q, Q), dtype=bool))
    seg = acs[:, :, :, None] - acs[:, :, None, :]
    decay = jnp.exp(jnp.where(causal[None, None, :, :, None, None], seg, -jnp.inf))
    cb = jnp.einsum('bclgn,bcsgn->bclsg', Cc, Bc)
    y_diag = jnp.einsum('bclsgr,bcsgrp->bclgrp', cb[..., None] * decay, xdt)
    states = jnp.einsum('bclgn,bclgr,bclgrp->bcgrpn', Bc, jnp.exp(acs[:, :, -1:] - acs), xdt)
    chunk_decay = jnp.exp(acs[:, :, -1])

    def step(h, inp):
        dec, st = inp
        return dec[..., None, None] * h + st, h

    h_init = h0.astype(f32).reshape(b, SSD_GROUPS, r, hp, D_STATE)
    h_fin, h_prev = lax.scan(step, h_init, (jnp.moveaxis(chunk_decay, 1, 0), jnp.moveaxis(states, 1, 0)))
    h_prev = jnp.moveaxis(h_prev, 0, 1)
    y_off = jnp.einsum('bclgn,bcgrpn,bclgr->bclgrp', Cc, h_prev, jnp.exp(acs))
    y = (y_diag + y_off).reshape(b, nc * Q, nh, hp)[:, :L]
    return y, h_fin.reshape(b, nh, hp, D_STATE).astype(h0.dtype)


def ssd_branch(z, xbc, dt_raw, conv_prev, h0, p):
    b, L, _ = z.shape
    xc, conv_new = causal_conv(xbc, conv_prev, p['conv_w'], p['conv_b'])
    xs, Bm, Cm = jnp.split(xc, [D_SSM, D_SSM + SSD_GROUPS * D_STATE], axis=-1)
    xh = xs.reshape(b, L, SSD_HEADS, SSD_HEAD_DIM)
    Bm = Bm.reshape(b, L, SSD_GROUPS, D_STATE)
    Cm = Cm.reshape(b, L, SSD_GROUPS, D_STATE)
    dt = jax.nn.softplus(dt_raw.astype(jnp.float32) + p['dt_bias'].astype(jnp.float32))
    A = -jnp.exp(p['a_log'].astype(jnp.float32))
    y, h_new = ssd_scan(xh, dt, A, Bm, Cm, h0)
    y = y + p['d_skip'].astype(jnp.float32)[:, None] * xh.astype(jnp.float32)
    yg = y.reshape(b, L, D_SSM) * jax.nn.silu(z.astype(jnp.float32))
    yg = yg.reshape(b, L, SSD_GROUPS, D_SSM // SSD_GROUPS)
    yg = yg * lax.rsqrt(jnp.mean(yg * yg, axis=-1, keepdims=True) + RMS_EPS)
    yg = yg.reshape(b, L, D_SSM) * p['ssd_norm'].astype(jnp.float32)
    return yg.astype(z.dtype), h_new, conv_new


def fox_block(q, k, v, c_q, c_k, q_pos):
    s = jnp.einsum('bqkgd,bskd->bkgqs', q, k).astype(jnp.float32) * (ATT_HEAD_DIM ** -0.5)
    bias = jnp.transpose(c_q, (0, 2, 3, 1))[..., :, None] - jnp.transpose(c_k, (0, 2, 3, 1))[..., None, :]
    mask = jnp.arange(k.shape[1])[None, :] <= q_pos[:, None]
    s = jnp.where(mask, s + bias, -jnp.inf)
    prob = jax.nn.softmax(s, axis=-1)
    return jnp.einsum('bkgqs,bskd->bqkgd', prob.astype(v.dtype), v)


def fox_attention(q, k_all, v_all, logf_all):
    b, T = q.shape[:2]
    S = k_all.shape[1]
    past = S - T
    c = jnp.cumsum(logf_all.astype(jnp.float32), axis=1).reshape(b, S, KV_HEADS, Q_PER_KV)
    c_q = c[:, past:]
    qg = q.reshape(b, T, KV_HEADS, Q_PER_KV, ATT_HEAD_DIM)
    pos = past + jnp.arange(T)
    if T > Q_BLOCK and T % Q_BLOCK == 0:
        nb = T // Q_BLOCK
        qb = jnp.moveaxis(qg.reshape(b, nb, Q_BLOCK, KV_HEADS, Q_PER_KV, ATT_HEAD_DIM), 1, 0)
        cb = jnp.moveaxis(c_q.reshape(b, nb, Q_BLOCK, KV_HEADS, Q_PER_KV), 1, 0)
        o = lax.map(lambda blk: fox_block(blk[0], k_all, v_all, blk[1], c, blk[2]),
                    (qb, cb, pos.reshape(nb, Q_BLOCK)))
        o = jnp.moveaxis(o, 0, 1)
    else:
        o = fox_block(qg, k_all, v_all, c_q, c, pos)
    return o.reshape(b, T, ATT_DIM)


def trunk_layer(x, conv_prev, h0, past, p):
    b, L, _ = x.shape
    h = x + 0.5 * swiglu(rmsnorm(x, p['ffn1_norm']), p['ffn1_w_gate'], p['ffn1_w_up'], p['ffn1_w_down'])
    xn = rmsnorm(h, p['mix_norm'])
    z, xbc, dt_raw, q, k, v, f_raw, g_ssd, g_att = split_combined(xn @ p['w_in'])
    y_ssd, h_new, conv_new = ssd_branch(z, xbc, dt_raw, conv_prev, h0, p)
    q = rmsnorm(q.reshape(b, L, ATT_HEADS, ATT_HEAD_DIM), p['q_norm'])
    k = rmsnorm(k.reshape(b, L, KV_HEADS, ATT_HEAD_DIM), p['k_norm'])
    v = v.reshape(b, L, KV_HEADS, ATT_HEAD_DIM)
    logf = jax.nn.log_sigmoid(f_raw.astype(jnp.float32) + p['b_f'].astype(jnp.float32))
    if past is None:
        k_all, v_all, logf_all = k, v, logf
    else:
        k_past, v_past, logf_past = past
        k_all = jnp.concatenate([k_past.astype(k.dtype), k], axis=1)
        v_all = jnp.concatenate([v_past.astype(v.dtype), v], axis=1)
        logf_all = jnp.concatenate([logf_past.astype(jnp.float32), logf], axis=1)
    y_att = fox_attention(q, k_all, v_all, logf_all)
    merged = (jax.nn.sigmoid(g_ssd) * (y_ssd @ p['w_ssd_proj'])
              + jax.nn.sigmoid(g_att) * (y_att @ p['w_attn_proj']))
    h = h + merged @ p['w_out']
    h = h + 0.5 * swiglu(rmsnorm(h, p['ffn2_norm']), p['ffn2_w_gate'], p['ffn2_w_up'], p['ffn2_w_down'])
    return h, (k, v, logf, h_new, conv_new)


def setup_inputs(seed: int = 0) -> dict:
    key = jax.random.key(seed)
    ks = jax.random.split(key, 40)
    f32 = jnp.float32

    def nrm(k, shape, scale):
        return scale * jax.random.normal(k, shape, f32)

    n_pages = PAST_LEN // PAGE_SIZE
    n_used = DEC_BATCH * n_pages
    n_pool = n_used + max(1, n_used // 4)
    u = jax.random.uniform(ks[10], (DEPTH, SSD_HEADS), f32)
    dt0 = jnp.exp(u * (math.log(0.1) - math.log(0.001)) + math.log(0.001))
    return {
        'x_prompt': nrm(ks[0], (BATCH, SEQ, D_MODEL), 1.0),
        'x_sample': nrm(ks[1], (DEC_BATCH, DEC_SEQ, D_MODEL), 1.0),
        'cache_k': nrm(ks[2], (DEPTH, n_pool, PAGE_SIZE, KV_HEADS, ATT_HEAD_DIM), 1.0),
        'cache_v': nrm(ks[3], (DEPTH, n_pool, PAGE_SIZE, KV_HEADS, ATT_HEAD_DIM), 1.0),
        'cache_logf': jax.nn.log_sigmoid(3.0 + nrm(ks[4], (DEPTH, n_pool, PAGE_SIZE, ATT_HEADS), 1.0)),
        'state_ssm': nrm(ks[5], (DEPTH, DEC_BATCH, SSD_HEADS, SSD_HEAD_DIM, D_STATE), 0.1),
        'state_conv': nrm(ks[6], (DEPTH, DEC_BATCH, D_CONV - 1, CONV_DIM), 1.0),
        'page_table': jax.random.permutation(ks[7], n_pool)[:n_used].reshape(DEC_BATCH, n_pages).astype(jnp.int32),
        'ffn1_norm': 1.0 + nrm(ks[8], (DEPTH, D_MODEL), 0.02),
        'ffn1_w_gate': nrm(ks[9], (DEPTH, D_MODEL, D_FF), D_MODEL ** -0.5),
        'ffn1_w_up': nrm(ks[11], (DEPTH, D_MODEL, D_FF), D_MODEL ** -0.5),
        'ffn1_w_down': nrm(ks[12], (DEPTH, D_FF, D_MODEL), D_FF ** -0.5),
        'mix_norm': 1.0 + nrm(ks[13], (DEPTH, D_MODEL), 0.02),
        'w_in': nrm(ks[14], (DEPTH, D_MODEL, IN_DIM), D_MODEL ** -0.5),
        'conv_w': nrm(ks[15], (DEPTH, D_CONV, CONV_DIM), D_CONV ** -0.5),
        'conv_b': nrm(ks[16], (DEPTH, CONV_DIM), 0.01),
        'dt_bias': dt0 + jnp.log(-jnp.expm1(-dt0)),
        'a_log': jnp.log(jax.random.uniform(ks[17], (DEPTH, SSD_HEADS), f32, minval=1.0, maxval=16.0)),
        'd_skip': 1.0 + nrm(ks[18], (DEPTH, SSD_HEADS), 0.1),
        'ssd_norm': 1.0 + nrm(ks[19], (DEPTH, D_SSM), 0.02),
        'q_norm': 1.0 + nrm(ks[20], (DEPTH, ATT_HEAD_DIM), 0.02),
        'k_norm': 1.0 + nrm(ks[21], (DEPTH, ATT_HEAD_DIM), 0.02),
        'b_f': 3.0 + nrm(ks[22], (DEPTH, ATT_HEADS), 0.5),
        'w_ssd_proj': nrm(ks[23], (DEPTH, D_SSM, D_MODEL), D_SSM ** -0.5),
        'w_attn_proj': nrm(ks[24], (DEPTH, ATT_DIM, D_MODEL), ATT_DIM ** -0.5),
        'w_out': nrm(ks[25], (DEPTH, D_MODEL, D_MODEL), D_MODEL ** -0.5),
        'ffn2_norm': 1.0 + nrm(ks[26], (DEPTH, D_MODEL), 0.02),
        'ffn2_w_gate': nrm(ks[27], (DEPTH, D_MODEL, D_FF), D_MODEL ** -0.5),
        'ffn2_w_up': nrm(ks[28], (DEPTH, D_MODEL, D_FF), D_MODEL ** -0.5),
        'ffn2_w_down': nrm(ks[29], (DEPTH, D_FF, D_MODEL), D_FF ** -0.5),
    }


def reference(x_prompt, x_sample, cache_k, cache_v, cache_logf, state_ssm, state_conv, page_table,
              ffn1_norm, ffn1_w_gate, ffn1_w_up, ffn1_w_down, mix_norm, w_in, conv_w, conv_b,
              dt_bias, a_log, d_skip, ssd_norm, q_norm, k_norm, b_f, w_ssd_proj, w_attn_proj, w_out,
              ffn2_norm, ffn2_w_gate, ffn2_w_up, ffn2_w_down):
    dec_batch, n_pages = page_table.shape
    past_len = n_pages * cache_k.shape[2]
    batch = x_prompt.shape[0]
    yp, ys = x_prompt, x_sample
    prompt_states, sample_states = [], []
    for l in range(DEPTH):
        p = {
            'ffn1_norm': ffn1_norm[l], 'ffn1_w_gate': ffn1_w_gate[l], 'ffn1_w_up': ffn1_w_up[l],
            'ffn1_w_down': ffn1_w_down[l], 'mix_norm': mix_norm[l], 'w_in': w_in[l],
            'conv_w': conv_w[l], 'conv_b': conv_b[l], 'dt_bias': dt_bias[l], 'a_log': a_log[l],
            'd_skip': d_skip[l], 'ssd_norm': ssd_norm[l], 'q_norm': q_norm[l], 'k_norm': k_norm[l],
            'b_f': b_f[l], 'w_ssd_proj': w_ssd_proj[l], 'w_attn_proj': w_attn_proj[l], 'w_out': w_out[l],
            'ffn2_norm': ffn2_norm[l], 'ffn2_w_gate': ffn2_w_gate[l], 'ffn2_w_up': ffn2_w_up[l],
            'ffn2_w_down': ffn2_w_down[l],
        }
        conv0 = jnp.zeros((batch, D_CONV - 1, CONV_DIM), x_prompt.dtype)
        h0 = jnp.zeros((batch, SSD_HEADS, SSD_HEAD_DIM, D_STATE), state_ssm.dtype)
        yp, sp = trunk_layer(yp, conv0, h0, None, p)
        k_past = cache_k[l, page_table].reshape(dec_batch, past_len, KV_HEADS, ATT_HEAD_DIM)
        v_past = cache_v[l, page_table].reshape(dec_batch, past_len, KV_HEADS, ATT_HEAD_DIM)
        logf_past = cache_logf[l, page_table].reshape(dec_batch, past_len, ATT_HEADS)
        ys, ss = trunk_layer(ys, state_conv[l], state_ssm[l], (k_past, v_past, logf_past), p)
        prompt_states.append(sp)
        sample_states.append(ss)
    k_p, v_p, logf_p, ssm_p, conv_p = (jnp.stack(t) for t in zip(*prompt_states))
    k_s, v_s, logf_s, ssm_s, conv_s = (jnp.stack(t) for t in zip(*sample_states))
    return (yp, ys, k_p, v_p, logf_p, ssm_p, conv_p, k_s, v_s, logf_s, ssm_s, conv_s)
```

```python
import contextlib
import numpy as np
import concourse.bass as bass
import concourse.mybir as mybir
from concourse.bass_utils import run_bass_kernel_spmd

F32 = mybir.dt.float32
BF16 = mybir.dt.bfloat16
I32 = mybir.dt.int32
AF = mybir.ActivationFunctionType
ALU = mybir.AluOpType

D = 1024
DFF = 2816
NF = 22
NSEQ_S = 4
LS = 8
NCORES = 8


class Res:
    __slots__ = ("name", "w", "r")

    def __init__(self, name=""):
        self.name = name
        self.w = None
        self.r = []


class DmaSlot:
    __slots__ = ("sem", "count", "name")

    def __init__(self, name):
        self.name = name
        self.sem = None
        self.count = 0


ENGS = ("pe", "act", "dve", "pool", "sp")
EPOCH = 30000


class Prog:
    def __init__(self):
        self.ops = {e: [] for e in ENGS}
        self.waited = {e: {} for e in ENGS}
        self.needed = {e: set() for e in ENGS}
        self.slots = []

    def slot(self, name=""):
        s = DmaSlot(name)
        self.slots.append(s)
        return s

    def _deps(self, eng, reads, writes):
        deps = []
        for r in reads:
            if r.w is not None:
                deps.append(r.w)
        for w in writes:
            if w.w is not None:
                deps.append(w.w)
            deps.extend(w.r)
        out = []
        wd = self.waited[eng]
        best = {}
        for d in deps:
            if d[0] == "dma":
                key = ("dma", id(d[1]))
                if key not in best or best[key][2] < d[2]:
                    best[key] = d
            else:
                if d[0] == eng and eng == "pe":
                    continue
                if d[0] not in best or best[d[0]][1] < d[1]:
                    best[d[0]] = d
        for key, d in best.items():
            v = d[2] if d[0] == "dma" else d[1]
            if wd.get(key, 0) >= v:
                continue
            wd[key] = v
            if d[0] != "dma":
                self.needed[d[0]].add(v)
            out.append(d)
        return out

    def op(self, eng, fn, reads=(), writes=()):
        waits = self._deps(eng, reads, writes)
        lst = self.ops[eng]
        lst.append([fn, waits, None, 0])
        tok = (eng, len(lst))
        for r in reads:
            r.r.append(tok)
        for w in writes:
            w.w = tok
            w.r = []
        return tok

    def dma(self, eng, fn, slot, reads=(), writes=()):
        if slot is None:
            key = writes[0] if len(writes) else reads[0]
            if not hasattr(self, "_auto"):
                self._auto = {}
            if id(key) not in self._auto:
                self._auto[id(key)] = self.slot("a" + key.name)
            slot = self._auto[id(key)]
        waits = self._deps(eng, reads, writes)
        slot.count += 16
        self.ops[eng].append([fn, waits, slot, slot.count])
        tok = ("dma", slot, slot.count)
        for r in reads:
            r.r.append(tok)
        for w in writes:
            w.w = tok
            w.r = []
        return tok

    def emit(self, nc, final_engine="sp"):
        with contextlib.ExitStack() as st:
            rank = {}
            nsig = {}
            for e in ENGS:
                flagged = sorted(self.needed[e])
                rank[e] = {idx: i + 1 for i, idx in enumerate(flagged)}
                nsig[e] = len(flagged)
            sems = {}
            for e in ENGS:
                n_ep = max(1, (nsig[e] + EPOCH - 1) // EPOCH)
                sems[e] = [st.enter_context(nc.semaphore(f"s_{e}{k}")) for k in range(n_ep)]
            for s in self.slots:
                if s.count > 0:
                    s.sem = st.enter_context(nc.semaphore(f"d_{s.name}"))
            fin = [("dma", s, s.count) for s in self.slots if s.count > 0]
            self.ops[final_engine].append([None, fin, None, 0])
            block = st.enter_context(nc.Block())

            def run(e):
                def body(eng):
                    for i, (fn, waits, slot, val) in enumerate(self.ops[e]):
                        for d in waits:
                            if d[0] == "dma":
                                eng.wait_ge(d[1].sem, d[2])
                            else:
                                sg = rank[d[0]][d[1]]
                                eng.wait_ge(sems[d[0]][(sg - 1) // EPOCH], (sg - 1) % EPOCH + 1)
                        if fn is None:
                            continue
                        ins = fn(eng)
                        if slot is not None:
                            ins.then_inc(slot.sem, 16)
                        else:
                            sg = rank[e].get(i + 1)
                            if sg is not None:
                                ins.then_inc(sems[e][(sg - 1) // EPOCH], 1)
                return body

            block.tensor(run("pe"))
            block.scalar(run("act"))
            block.vector(run("dve"))
            block.gpsimd(run("pool"))
            block.sync(run("sp"))
        return {e: len(self.ops[e]) for e in ENGS}, nsig


def _kc_tile(w_cols):
    return w_cols.reshape(8, 128, 128).transpose(1, 0, 2).reshape(128, 1024)


def _pad_cols(w, n):
    out = np.zeros((w.shape[0], n), np.float32)
    out[:, : w.shape[1]] = w
    return out


def _ffn_tiles(wg, wu, wd):
    tiles = []
    for j in range(NF):
        tiles.append(_kc_tile(wg[:, j * 128:(j + 1) * 128]))
        tiles.append(_kc_tile(wu[:, j * 128:(j + 1) * 128]))
    wdp = np.concatenate([wd, np.zeros((24 * 128 - DFF, D), np.float32)], 0)
    for m in range(8):
        blk = wdp[:, m * 128:(m + 1) * 128].reshape(24, 128, 128)
        for s in range(3):
            tiles.append(blk[s * 8:(s + 1) * 8].transpose(1, 0, 2).reshape(128, 1024))
    return tiles


O_Z, O_XBC, O_DT, O_Q, O_K, O_V, O_F, O_GS, O_GA = 0, 2048, 5120, 5152, 6176, 6432, 6688, 6704, 7728


def _q_perm_cols():
    cols = []
    for cp in range(2):
        for g in range(4):
            for half in range(2):
                kv = 2 * cp + half
                h = 4 * kv + g
                cols.extend(range(O_Q + h * 64, O_Q + (h + 1) * 64))
    return np.array(cols)


def build_weight_stream(p):
    t = []
    t += _ffn_tiles(p["ffn1_w_gate"], p["ffn1_w_up"], p["ffn1_w_down"])
    w_in = p["w_in"]
    for c in range(2):
        t.append(_kc_tile(w_in[:, O_K + c * 128: O_K + (c + 1) * 128]))
    for c in range(2):
        t.append(_kc_tile(w_in[:, O_V + c * 128: O_V + (c + 1) * 128]))
    t.append(_kc_tile(_pad_cols(w_in[:, O_F:O_F + 16], 128)))
    qc = w_in[:, _q_perm_cols()]
    for c in range(8):
        t.append(_kc_tile(qc[:, c * 128:(c + 1) * 128]))
    for c in range(24):
        t.append(_kc_tile(w_in[:, O_XBC + c * 128: O_XBC + (c + 1) * 128]))
    for c in range(16):
        t.append(_kc_tile(w_in[:, O_Z + c * 128: O_Z + (c + 1) * 128]))
    wsp = p["w_ssd_proj"]
    wap = p["w_attn_proj"]
    for m in range(8):
        blk = wsp[:, m * 128:(m + 1) * 128].reshape(16, 128, 128)
        for s in range(2):
            t.append(blk[s * 8:(s + 1) * 8].transpose(1, 0, 2).reshape(128, 1024))
        t.append(_kc_tile(w_in[:, O_GS + m * 128: O_GS + (m + 1) * 128]))
        ablk = wap[:, m * 128:(m + 1) * 128].reshape(16, 64, 128)
        for s in range(2):
            a = np.zeros((128, 1024), np.float32)
            a[:64] = ablk[s * 8:(s + 1) * 8].transpose(1, 0, 2).reshape(64, 1024)
            t.append(a)
        t.append(_kc_tile(w_in[:, O_GA + m * 128: O_GA + (m + 1) * 128]))
    wo = p["w_out"]
    for m in range(8):
        t.append(_kc_tile(wo[:, m * 128:(m + 1) * 128]))
    t += _ffn_tiles(p["ffn2_w_gate"], p["ffn2_w_up"], p["ffn2_w_down"])
    return np.ascontiguousarray(np.stack(t)).astype(np.float32)


NT = 68 + 37 + 16 + 48 + 8 + 68

CV_N1, CV_NM, CV_N2, CV_SN, CV_CW, CV_CB, CV_QN, CV_KN, CV_DS, CV_BF, CV_NBF = 0, 8, 16, 24, 40, 136, 160, 161, 162, 178, 179
NCV = 180


def build_cvec(p):
    cv = np.zeros((128, NCV), np.float32)
    cv[:, CV_N1:CV_N1 + 8] = p["ffn1_norm"].reshape(8, 128).T
    cv[:, CV_NM:CV_NM + 8] = p["mix_norm"].reshape(8, 128).T
    cv[:, CV_N2:CV_N2 + 8] = p["ffn2_norm"].reshape(8, 128).T
    cv[:, CV_SN:CV_SN + 16] = p["ssd_norm"].reshape(16, 128).T
    cw = p["conv_w"].reshape(4, 24, 128)
    cv[:, CV_CW:CV_CW + 96] = cw.transpose(2, 1, 0).reshape(128, 96)
    cv[:, CV_CB:CV_CB + 24] = p["conv_b"].reshape(24, 128).T
    cv[:, CV_QN] = np.tile(p["q_norm"], 2)
    cv[:, CV_KN] = np.tile(p["k_norm"], 2)
    cv[:, CV_DS:CV_DS + 16] = np.repeat(p["d_skip"], 64).reshape(16, 128).T
    cv[:16, CV_BF] = p["b_f"]
    return cv


def build_program(SEQ, NPG, NPOOL, do_sample=True):
    T = 512
    NB = SEQ // T
    NTIL = SEQ // 128
    TS = NSEQ_S * LS
    nc = bass.Bass("TRN2", target_bir_lowering=False)

    def din(name, shape, dt=F32):
        return nc.dram_tensor(name, shape, dt, kind="ExternalInput").ap()

    def dout(name, shape, dt=F32):
        return nc.dram_tensor(name, shape, dt, kind="ExternalOutput").ap()

    xT_d = din("xT", [D, SEQ])
    xsT_d = din("xsT", [D, TS])
    wst_d = din("wst", [NT, 128, 1024])
    cvec_d = din("cvec", [128, NCV])
    rowc_d = din("rowc", [1, 96])
    wdt_d = din("wdt", [128, 8 * 32])
    ssm0_d = din("ssm0", [NSEQ_S, 128, 2048])
    conv0_d = din("conv0", [NSEQ_S, 128, 24 * 3])
    ck_d = din("cache_k", [NPOOL * 128, 256])
    cvv_d = din("cache_v", [NPOOL * 128, 256])
    clf_d = din("cache_lf", [NPOOL * 128, 16])
    pt_d = din("ptab", [1, NSEQ_S * NPG], I32)

    yT_o = dout("yT", [D, SEQ])
    ysT_o = dout("ysT", [D, TS])
    kT_o = dout("kT", [256, SEQ])
    vT_o = dout("vT", [256, SEQ])
    lfT_o = dout("lfT", [16, SEQ])
    ssm_o = dout("ssm", [128, 2048])
    conv_o = dout("conv", [128, 72])
    ksT_o = dout("ksT", [256, TS])
    vsT_o = dout("vsT", [256, TS])
    lfsT_o = dout("lfsT", [16, TS])
    ssms_o = dout("ssms", [NSEQ_S, 128, 2048])
    convs_o = dout("convs", [NSEQ_S, 128, 72])

    wbf_d = nc.dram_tensor("wbf", [NT, 128, 1024], BF16, kind="Internal").ap()

    P = Prog()
    st = contextlib.ExitStack()
    with st:
        def sb(name, shape, dt=F32):
            return st.enter_context(nc.sbuf_tensor("sb_" + name, shape, dt))

        def pst(name, shape, dt=F32):
            return st.enter_context(nc.psum_tensor("ps_" + name, shape, dt))

        cvec = sb("cvec", [128, NCV]); r_c = Res("const")
        rowc = sb("rowc", [128, 96])
        wdt32 = sb("wdt32", [128, 256]); wdt = sb("wdt", [128, 8, 32], BF16)
        ident = sb("ident", [128, 128], BF16)
        identf = sb("identf", [128, 128], F32)
        ones_d = sb("ones_d", [128, 128], BF16)
        ones_g = sb("ones_g", [128, 128], BF16)
        bd64 = sb("bd64", [128, 128], BF16)
        onesf = sb("onesf", [128, 128], F32)
        tri_f = sb("tri_f", [128, 128], F32)
        ustr_f = sb("ustr_f", [128, 128], F32)
        mask01 = sb("mask01", [128, 128], F32)
        epsc = sb("epsc", [128, 1]); onec = sb("onec", [128, 1])
        A_bc = sb("A_bc", [128, 32]); nbf = sb("nbf", [128, 1])
        s_c = P.slot("const")
        r_c1 = Res("c1"); r_c2 = Res("c2")
        P.dma("sp", lambda e: e.dma_start(out=cvec[:], in_=cvec_d), None, writes=[r_c])
        P.dma("sp", lambda e: e.dma_start(out=rowc[:], in_=rowc_d.partition_broadcast(128)), None, writes=[r_c1])
        P.dma("sp", lambda e: e.dma_start(out=wdt32[:], in_=wdt_d), None, writes=[r_c2])
        r_k = Res("consts2")
        P.op("pool", lambda e: e.memset(identf[:], 1.0), writes=[r_k])
        P.op("pool", lambda e: e.affine_select(out=identf[:], in_=identf[:], pattern=[[-1, 128]], compare_op=ALU.is_equal, fill=0.0, base=0, channel_multiplier=1), writes=[r_k])
        P.op("pool", lambda e: e.tensor_copy(out=ident[:], in_=identf[:]), writes=[r_k])
        P.op("pool", lambda e: e.memset(onesf[:], 1.0), writes=[r_k])
        P.op("pool", lambda e: e.memset(ones_d[:], 1.0 / 1024), writes=[r_k])
        P.op("pool", lambda e: e.memset(ones_g[:], 1.0 / 512), writes=[r_k])
        P.op("pool", lambda e: e.memset(bd64[:], 0.0), writes=[r_k])
        P.op("pool", lambda e: e.memset(bd64[0:64, 0:64], 1.0 / 64), writes=[r_k])
        P.op("pool", lambda e: e.memset(bd64[64:128, 64:128], 1.0 / 64), writes=[r_k])
        P.op("pool", lambda e: e.affine_select(out=tri_f[:], in_=onesf[:], pattern=[[1, 128]], compare_op=ALU.is_ge, fill=0.0, base=0, channel_multiplier=-1), writes=[r_k])
        P.op("pool", lambda e: e.tensor_copy(out=mask01[:], in_=tri_f[:]), writes=[r_k])
        P.op("pool", lambda e: e.affine_select(out=ustr_f[:], in_=onesf[:], pattern=[[-1, 128]], compare_op=ALU.is_gt, fill=0.0, base=0, channel_multiplier=1), writes=[r_k])
        P.op("pool", lambda e: e.memset(epsc[:], 1e-6), writes=[r_k])
        P.op("pool", lambda e: e.memset(onec[:], 1.0), writes=[r_k])
        P.op("act", lambda e: e.activation(out=A_bc[:], in_=rowc[:, 32:64], func=AF.Exp), reads=[r_c1], writes=[r_k])
        P.op("dve", lambda e: e.tensor_scalar(out=A_bc[:], in0=A_bc[:], scalar1=-1.0, scalar2=None, op0=ALU.mult), reads=[r_k], writes=[r_k])
        P.op("dve", lambda e: e.tensor_scalar(out=nbf[:], in0=cvec[:, CV_BF:CV_BF + 1], scalar1=-1.0, scalar2=None, op0=ALU.mult), reads=[r_c], writes=[r_k])
        P.op("dve", lambda e: e.tensor_copy(out=wdt[:].rearrange("p a b -> p (a b)"), in_=wdt32[:]), reads=[r_c2], writes=[r_k])
        CR = [r_c, r_k, r_c1]

        r_wbf = [Res(f"wbf{i}") for i in range(NT)]
        s_pre = [P.slot(f"pre{i}") for i in range(8)]
        for t in range(NT):
            P.dma("pool", lambda e, t=t: e.dma_start(out=wbf_d[t], in_=wst_d[t]), s_pre[(t // 8) % 8], writes=[r_wbf[t]])
        for t in range(NT):
            last = min(NT - 1, (t // 8) * 8 + 7)
            r_wbf[t].w = r_wbf[last].w
        NS = 5
        ring = [sb(f"ring{i}", [128, 1024], BF16) for i in range(NS)]
        r_ring = [Res(f"ring{i}") for i in range(NS)]
        s_ring = [P.slot(f"ring{i}") for i in range(NS)]
        wctr = [0]

        def wtile():
            n = wctr[0]; wctr[0] += 1
            t = n % NT
            s = n % NS
            P.dma("sp", lambda e: e.dma_start(out=ring[s][:], in_=wbf_d[t]), s_ring[s], reads=[r_wbf[t]], writes=[r_ring[s]])
            return ring[s], r_ring[s]

        pAB = pst("pAB", [128, 1024]); pCD = pst("pCD", [128, 1024]); pEF = pst("pEF", [128, 1024])
        pTB = pst("pTB", [128, 2048], BF16)
        bank = {"A": pAB[:, 0:512], "B": pAB[:, 512:1024], "C": pCD[:, 0:512], "D": pCD[:, 512:1024],
                "E": pEF[:, 0:512], "F": pEF[:, 512:1024]}
        r_b = {k: Res("bank" + k) for k in "ABCDEF"}
        r_tb = Res("pTB")

        hT = sb("hT", [128, 8, T]); r_hT = Res("hT")
        xn = sb("xn", [128, 8, T], BF16); r_xn = Res("xn")
        arena = sb("arena", [128, NF, T], BF16)
        r_hid = [Res(f"hid{j}") for j in range(NF)]
        xc = sb("xc", [128, 24, T], BF16); r_xc = [Res(f"xc{c}") for c in range(24)]
        qT = sb("qT", [128, 8, T], BF16); r_qT = Res("qT")
        kTs = sb("kTs", [128, 2, SEQ], BF16); r_kT = Res("kT")
        Vs = sb("Vs", [128, NTIL, 4, 65], BF16); r_V = Res("V")
        cks = sb("cks", [128, NTIL, 16]); r_ck = Res("ck")
        biasb = sb("biasb", [128, NTIL, 16]); r_bias = Res("bias")
        yT = sb("yT", [128, 16, T], BF16); r_yT = [Res(f"yT{c}") for c in range(16)]
        oT = xc
        r_oT = Res("oT")
        merged = qT
        r_mg = Res("merged")
        hst = sb("hst", [128, 2048]); r_hst = Res("hst")
        stg = [sb(f"stg{i}", [128, T]) for i in range(3)]; r_stg = [Res(f"stg{i}") for i in range(3)]
        sqb = sb("sqb", [128, 4, T], BF16); r_sq = [Res(f"sq{i}") for i in range(4)]
        rstd = sb("rstd", [128, T]); r_rstd = Res("rstd")
        cstage = [sb(f"cst{i}", [128, T + 3 * NSEQ_S]) for i in range(2)]; r_cst = [Res(f"cst{i}") for i in range(2)]
        cacc = [sb(f"cacc{i}", [128, T]) for i in range(2)]; r_cacc = [Res(f"cacc{i}") for i in range(2)]
        ccar = sb("ccar", [128, 24, 3 * NSEQ_S]); r_ccar = [Res(f"ccar{c}") for c in range(24)]
        lfT = sb("lfT", [16, T]); r_lf = Res("lfT")
        cT = sb("cT", [16, T]); r_cT = Res("cT")
        ccarry = sb("ccarry", [16, 1]); r_cc = Res("ccarry")
        ones16 = sb("ones16", [16, T])
        dgl = sb("dgl", [16, 16]); r_dgl = Res("dgl")
        cref = sb("cref", [128, 16]); r_cref = Res("cref")
        dtt = sb("dtt", [128, 32]); at = sb("at", [128, 32]); acs = sb("acs", [128, 32]); tot = sb("tot", [128, 32])
        eacs = sb("eacs", [128, 32]); dte = sb("dte", [128, 32]); cdec = sb("cdec", [128, 32])
        r_ss = Res("ssdsmall")
        Dg = sb("Dg", [128, 8, 128]); r_Dg = Res("Dg")
        cbm = sb("cbm", [128, 128]); r_cbm = Res("cbm")
        tmpf = sb("tmpf", [128, 512]); r_tmpf = Res("tmpf")
        def asl(a, b):
            return arena[:, a:b, :].rearrange("p a b -> p (a b)")
        xdt = asl(0, 4); xw = asl(4, 8); ytok = asl(8, 12); hbf = asl(12, 16)
        Btok = asl(16, 17); MT = asl(17, 19); MTb = asl(19, 21); PT = asl(21, 22)
        r_xdt, r_xw, r_ytok, r_hbf, r_Btok, r_MT, r_MTb, r_PT = (Res(n) for n in ("xdt", "xw", "ytok", "hbf", "Btok", "MT", "MTb", "PT"))
        sTt = sb("sTt", [128, 512]); r_sTt = Res("sTt")
        bcs = stg[2][0:64, :]; r_bcs = r_stg[2]
        rec = tmpf; r_rec = r_tmpf

        s_in = P.slot("xin"); s_o = [P.slot(f"out{i}") for i in range(6)]
        P.op("pool", lambda e: e.memset(Vs[:].rearrange("p a b c -> p (a b c)"), 1.0), writes=[r_V])
        P.op("pool", lambda e: e.memset(ones16[:], 1.0), writes=[r_k])

        def mm(out, lhsT, rhs, start, stop, reads, writes):
            P.op("pe", lambda e: e.matmul(out, lhsT=lhsT, rhs=rhs, start=start, stop=stop), reads=reads, writes=writes)

        def rms_rstd(ps_ms, r_ps, n):
            P.op("act", lambda e: e.activation(out=rstd[:, :n], in_=ps_ms, func=AF.Ln, bias=epsc[:], scale=1.0), reads=[r_ps] + CR, writes=[r_rstd])
            P.op("act", lambda e: e.activation(out=rstd[:, :n], in_=rstd[:, :n], func=AF.Exp, scale=-0.5), reads=[r_rstd], writes=[r_rstd])

        def norm_to_xn(n, cvo):
            for c in range(8):
                P.op("act", lambda e, c=c: e.activation(out=sqb[:, c % 4, :n], in_=hT[:, c, :n], func=AF.Square), reads=[r_hT], writes=[r_sq[c % 4]])
                mm(bank["E"][:, :n], ones_d[:], sqb[:, c % 4, :n], c == 0, c == 7, [r_sq[c % 4]] + CR, [r_b["E"]])
            rms_rstd(bank["E"][:, :n], r_b["E"], n)
            for c in range(8):
                P.op("dve", lambda e, c=c: e.scalar_tensor_tensor(out=xn[:, c, :n], in0=hT[:, c, :n], scalar=cvec[:, cvo + c:cvo + c + 1], in1=rstd[:, :n], op0=ALU.mult, op1=ALU.mult),
                     reads=[r_hT, r_rstd] + CR, writes=[r_xn])

        def proj8(bk, n, w, rw, src=None, rsrc=None):
            for c in range(8):
                mm(bank[bk][:, :n], w[:, c * 128:(c + 1) * 128], xn[:, c, :n], c == 0, c == 7, [rw, r_xn], [r_b[bk]])

        def ffn(n, final_out=None):
            for j in range(NF):
                wg, rg = wtile(); wu, ru = wtile()
                pg, pu = ("A", "B") if j % 2 == 0 else ("C", "D")
                proj8(pg, n, wg, rg); proj8(pu, n, wu, ru)
                si = j % 2
                P.op("act", lambda e, pg=pg, si=si: e.activation(out=stg[si][:, :n], in_=bank[pg][:, :n], func=AF.Silu), reads=[r_b[pg]], writes=[r_stg[si]])
                P.op("dve", lambda e, pu=pu, si=si, j=j: e.tensor_tensor(out=arena[:, j, :n], in0=bank[pu][:, :n], in1=stg[si][:, :n], op=ALU.mult),
                     reads=[r_b[pu], r_stg[si]], writes=[r_hid[j]])
            for m in range(8):
                tl = [wtile() for _ in range(3)]
                pb = "AB"[m % 2]
                for kc in range(NF):
                    w, r = tl[kc // 8]
                    mm(bank[pb][:, :n], w[:, (kc % 8) * 128:(kc % 8 + 1) * 128], arena[:, kc, :n], kc == 0, kc == NF - 1, [r, r_hid[kc]], [r_b[pb]])
                P.op("dve", lambda e, m=m, pb=pb: e.scalar_tensor_tensor(out=hT[:, m, :n], in0=bank[pb][:, :n], scalar=0.5, in1=hT[:, m, :n], op0=ALU.mult, op1=ALU.add),
                     reads=[r_b[pb], r_hT], writes=[r_hT])

        def qknorm(bk, n, wcol, scale, out_bf, r_out, out_f32=None, r_f32=None, si=0, f32_view=None):
            P.op("act", lambda e: e.activation(out=stg[si][:, :n], in_=bank[bk][:, :n], func=AF.Copy), reads=[r_b[bk]], writes=[r_stg[si]])
            P.op("act", lambda e: e.activation(out=sqb[:, si, :n], in_=stg[si][:, :n], func=AF.Square), reads=[r_stg[si]], writes=[r_sq[si]])
            pb = "EF"[si]
            mm(bank[pb][:, :n], bd64[:], sqb[:, si, :n], True, True, [r_sq[si]] + CR, [r_b[pb]])
            rms_rstd(bank[pb][:, :n], r_b[pb], n)
            if out_f32 is not None:
                P.op("dve", lambda e: e.scalar_tensor_tensor(out=out_f32, in0=stg[si][:, :n], scalar=cvec[:, wcol:wcol + 1], in1=rstd[:, :n], op0=ALU.mult, op1=ALU.mult),
                     reads=[r_stg[si], r_rstd] + CR, writes=[r_f32])
                P.op("act", lambda e: e.activation(out=out_bf, in_=(out_f32 if f32_view is None else f32_view), func=AF.Copy, scale=scale), reads=[r_f32], writes=[r_out])
            else:
                P.op("dve", lambda e: e.scalar_tensor_tensor(out=stg[si][:, :n], in0=stg[si][:, :n], scalar=cvec[:, wcol:wcol + 1], in1=rstd[:, :n], op0=ALU.mult, op1=ALU.mult),
                     reads=[r_stg[si], r_rstd] + CR, writes=[r_stg[si]])
                P.op("act", lambda e: e.activation(out=out_bf, in_=stg[si][:, :n], func=AF.Copy, scale=scale), reads=[r_stg[si]], writes=[r_out])

        kvout = sb("kvout", [128, 2, T]); r_kvo = [Res(f"kvo{i}") for i in range(2)]

        def block(n, segs, sample, b):
            tb = b * T
            nq = 3 * len(set(s[2] for s in segs)) if sample else 3
            xsrc = xsT_d if sample else xT_d[:, tb:tb + n]
            P.dma("sp", lambda e: e.dma_start(out=hT[:, :, :n], in_=xsrc.rearrange("(c p) t -> p c t", p=128)), None, writes=[r_hT])
            import os as _os
            SSTOP = int(_os.environ.get("K_SSTOP", "99")) if sample else 99
            if sample:
                for si_ in range(NSEQ_S):
                    P.dma("sp", lambda e, si_=si_: e.dma_start(out=ccar[:, :, 3 * si_:3 * si_ + 3], in_=conv0_d[si_].rearrange("p (c l) -> p c l", l=3)), None, writes=r_ccar)
            if SSTOP <= 0:
                return
            norm_to_xn(n, CV_N1)
            ffn(n)
            if SSTOP <= 1:
                return
            norm_to_xn(n, CV_NM)
            ko, vo, lo = (ksT_o, vsT_o, lfsT_o) if sample else (kT_o[:, tb:tb + n], vT_o[:, tb:tb + n], lfT_o[:, tb:tb + n])
            kdst = kTs[:, :, SEQ - TS:SEQ] if False else None
            for c in range(2):
                w, rw = wtile(); bk = "AB"[c]
                proj8(bk, n, w, rw)
                kb = (kTs[:, c, tb:tb + n] if not sample else ksb[:, c, :, 0:LS])
                kvo_v = kvout[:, c, :n] if not sample else kvout[:, c, :n].rearrange("p (s l) -> p s l", s=NSEQ_S)
                qknorm(bk, n, CV_KN, 1.0, kb, r_kT, out_f32=kvout[:, c, :n], r_f32=r_kvo[c], si=c, f32_view=(kvo_v if sample else None))
                P.dma("sp", lambda e, c=c: e.dma_start(out=ko[c * 128:(c + 1) * 128, :], in_=kvout[:, c, :n]), None, reads=[r_kvo[c]])
            for c in range(2):
                w, rw = wtile(); bk = "AB"[c]
                proj8(bk, n, w, rw)
                P.op("act", lambda e, c=c, bk=bk: e.activation(out=kvout[:, c, :n], in_=bank[bk][:, :n], func=AF.Copy), reads=[r_b[bk]], writes=[r_kvo[c]])
                P.dma("sp", lambda e, c=c: e.dma_start(out=vo[c * 128:(c + 1) * 128, :], in_=kvout[:, c, :n]), None, reads=[r_kvo[c]])
                P.op("dve", lambda e, c=c: e.tensor_copy(out=sqb[:, c, :n], in_=kvout[:, c, :n]), reads=[r_kvo[c]], writes=[r_sq[c]])
                for (off, L, sq_) in segs:
                    P.op("pe", lambda e, c=c, off=off, L=L: e.transpose(pTB[:L, 0:128], sqb[:, c, off:off + L], ident[:]), reads=[r_sq[c]] + CR, writes=[r_tb])
                    if sample:
                        vdst = Vsm[:L, sq_, 2 * c:2 * c + 2, 0:64]
                    else:
                        vdst = Vs[:L, (tb + off) // 128, 2 * c:2 * c + 2, 0:64]
                    P.op("act", lambda e, L=L, vdst=vdst: e.activation(out=vdst, in_=pTB[:L, 0:128].rearrange("p (a b) -> p a b", a=2), func=AF.Copy), reads=[r_tb], writes=[r_V])
            w, rw = wtile()
            proj8("A", n, w, rw)
            P.op("act", lambda e: e.activation(out=lfT[:, :n], in_=bank["A"][:16, :n], func=AF.Exp, bias=nbf[:16, :], scale=-1.0), reads=[r_b["A"]] + CR, writes=[r_lf])
            P.op("act", lambda e: e.activation(out=lfT[:, :n], in_=lfT[:, :n], func=AF.Ln, bias=onec[:16, :], scale=1.0), reads=[r_lf], writes=[r_lf])
            P.op("dve", lambda e: e.tensor_scalar(out=lfT[:, :n], in0=lfT[:, :n], scalar1=-1.0, scalar2=None, op0=ALU.mult), reads=[r_lf], writes=[r_lf])
            P.dma("sp", lambda e: e.dma_start(out=lo, in_=lfT[:, :n]), None, reads=[r_lf])
            if not sample:
                if b == 0:
                    P.op("dve", lambda e: e.memset(ccarry[:], 0.0), writes=[r_cc])
                P.op("dve", lambda e: e.tensor_tensor_scan(out=cT[:, :n], data0=ones16[:, :n], data1=lfT[:, :n], initial=ccarry[:, 0:1], op0=ALU.mult, op1=ALU.add),
                     reads=[r_lf, r_cc], writes=[r_cT])
                P.op("dve", lambda e: e.tensor_copy(out=ccarry[:], in_=cT[:, n - 1:n]), reads=[r_cT], writes=[r_cc])
                for (off, L, sq_) in segs:
                    ti = (tb + off) // 128
                    mm(bank["B"][:L, 0:16], cT[:, off:off + L], identf[:16, :16], True, True, [r_cT] + CR, [r_b["B"]])
                    P.op("dve", lambda e, ti=ti, L=L: e.tensor_copy(out=cks[:L, ti, :], in_=bank["B"][:L, 0:16]), reads=[r_b["B"]], writes=[r_ck])
            for c in range(8):
                w, rw = wtile(); bk = "AB"[c % 2]
                proj8(bk, n, w, rw)
                qknorm(bk, n, CV_QN, 0.125, qT[:, c, :n], r_qT, si=c % 2)
            for c in range(24):
                w, rw = wtile(); bk = "AB"[c % 2]; ci = c % 2
                proj8(bk, n, w, rw)
                cs = cstage[ci]
                nsq = len(segs) if sample else 1
                Ls = n // nsq
                csv = cs[:, :nsq * (Ls + 3)].rearrange("p (s l) -> p s l", s=nsq)
                if sample:
                    P.op("pool", lambda e, c=c, csv=csv, nsq=nsq: e.tensor_copy(out=csv[:, :, 0:3], in_=ccar[:, c, :3 * nsq].rearrange("p (s l) -> p s l", s=nsq)), reads=[r_ccar[c]], writes=[r_cst[ci]])
                elif b == 0:
                    P.op("pool", lambda e, csv=csv: e.memset(csv[:, :, 0:3], 0.0), writes=[r_cst[ci]])
                else:
                    P.op("pool", lambda e, c=c, csv=csv: e.tensor_copy(out=csv[:, 0, 0:3], in_=ccar[:, c, 0:3]), reads=[r_ccar[c]], writes=[r_cst[ci]])
                P.op("act", lambda e, bk=bk, csv=csv, nsq=nsq, Ls=Ls: e.activation(out=csv[:, :, 3:3 + Ls], in_=bank[bk][:, :n].rearrange("p (s l) -> p s l", s=nsq), func=AF.Copy),
                     reads=[r_b[bk]], writes=[r_cst[ci]])
                P.op("pool", lambda e, c=c, csv=csv, nsq=nsq, Ls=Ls: e.tensor_copy(out=ccar[:, c, :3 * nsq].rearrange("p (s l) -> p s l", s=nsq), in_=csv[:, :, Ls:Ls + 3]),
                     reads=[r_cst[ci]], writes=[r_ccar[c]])
                ca = cacc[ci][:, :n].rearrange("p (s l) -> p s l", s=nsq)
                wc = CV_CW + 4 * c
                P.op("dve", lambda e, c=c, ca=ca, csv=csv, Ls=Ls, wc=wc: e.tensor_scalar(out=ca, in0=csv[:, :, 3:3 + Ls], scalar1=cvec[:, wc + 3:wc + 4], scalar2=cvec[:, CV_CB + c:CV_CB + c + 1], op0=ALU.mult, op1=ALU.add),
                     reads=[r_cst[ci]] + CR, writes=[r_cacc[ci]])
                for j in range(3):
                    P.op("dve", lambda e, j=j, ca=ca, csv=csv, Ls=Ls, wc=wc: e.scalar_tensor_tensor(out=ca, in0=csv[:, :, j:j + Ls], scalar=cvec[:, wc + j:wc + j + 1], in1=ca, op0=ALU.mult, op1=ALU.add),
                         reads=[r_cst[ci], r_cacc[ci]] + CR, writes=[r_cacc[ci]])
                P.op("act", lambda e, c=c, ci=ci: e.activation(out=xc[:, c, :n], in_=cacc[ci][:, :n], func=AF.Silu), reads=[r_cacc[ci]], writes=[r_xc[c]])
            if sample:
                for si_ in range(NSEQ_S):
                    P.dma("sp", lambda e, si_=si_: e.dma_start(out=convs_o[si_].rearrange("p (c l) -> p c l", l=3), in_=ccar[:, :, 3 * si_:3 * si_ + 3]), None, reads=r_ccar)
            elif b == NB - 1:
                P.dma("sp", lambda e: e.dma_start(out=conv_o.rearrange("p (c l) -> p c l", l=3), in_=ccar[:, :, 0:3]), None, reads=r_ccar)

            if SSTOP <= 2:
                return
            for (off, L, sq_) in segs:
                first = (b == 0 and off == 0) if not sample else True
                if sample:
                    P.dma("sp", lambda e, sq_=sq_: e.dma_start(out=hst[:], in_=ssm0_d[sq_]), None, writes=[r_hst])
                elif first:
                    P.op("pool", lambda e: e.memset(hst[:], 0.0), writes=[r_hst])
                P.op("act", lambda e: e.activation(out=hbf, in_=hst[:], func=AF.Copy), reads=[r_hst], writes=[r_hbf])
                for c in range(8):
                    mm(bank["E"][:L, 0:32], xn[:, c, off:off + L], wdt[:, c, :], c == 0, c == 7, [r_xn] + CR, [r_b["E"]])
                P.op("dve", lambda e, L=L: e.tensor_tensor(out=dtt[:L, :], in0=bank["E"][:L, 0:32], in1=rowc[:L, 0:32], op=ALU.add), reads=[r_b["E"]] + CR, writes=[r_ss])
                P.op("act", lambda e, L=L: e.activation(out=dtt[:L, :], in_=dtt[:L, :], func=AF.Exp), reads=[r_ss], writes=[r_ss])
                P.op("act", lambda e, L=L: e.activation(out=dtt[:L, :], in_=dtt[:L, :], func=AF.Ln, bias=onec[:L, :], scale=1.0), reads=[r_ss] + CR, writes=[r_ss])
                P.op("dve", lambda e, L=L: e.tensor_tensor(out=at[:L, :], in0=dtt[:L, :], in1=A_bc[:L, :], op=ALU.mult), reads=[r_ss] + CR, writes=[r_ss])
                mm(bank["F"][:L, 0:32], tri_f[:L, :L], at[:L, :], True, True, [r_ss] + CR, [r_b["F"]])
                mm(bank["E"][:, 32:64], onesf[:L, :], at[:L, :], True, True, [r_ss] + CR, [r_b["E"]])
                P.op("dve", lambda e, L=L: e.tensor_copy(out=acs[:L, :], in_=bank["F"][:L, 0:32]), reads=[r_b["F"]], writes=[r_ss])
                P.op("dve", lambda e: e.tensor_copy(out=tot[:], in_=bank["E"][:, 32:64]), reads=[r_b["E"]], writes=[r_ss])
                P.op("dve", lambda e, L=L: e.tensor_tensor(out=dte[:L, :], in0=tot[:L, :], in1=acs[:L, :], op=ALU.subtract), reads=[r_ss], writes=[r_ss])
                P.op("act", lambda e, L=L: e.activation(out=dte[:L, :], in_=dte[:L, :], func=AF.Exp), reads=[r_ss], writes=[r_ss])
                P.op("act", lambda e, L=L: e.activation(out=eacs[:L, :], in_=acs[:L, :], func=AF.Exp), reads=[r_ss], writes=[r_ss])
                P.op("act", lambda e: e.activation(out=cdec[:], in_=tot[:], func=AF.Exp), reads=[r_ss], writes=[r_ss])
                for c in range(16):
                    P.op("pe", lambda e, c=c, off=off, L=L: e.transpose(pTB[:L, c * 128:(c + 1) * 128], xc[:, c, off:off + L], ident[:]), reads=[r_xc[c]] + CR, writes=[r_tb])
                x3 = pTB[:L, :].rearrange("p (h d) -> p h d", d=64)
                P.op("dve", lambda e, L=L, x3=x3: e.tensor_tensor(out=xdt[:L, :].rearrange("p (h d) -> p h d", d=64), in0=x3, in1=dtt[:L, :].unsqueeze(2).to_broadcast([L, 32, 64]), op=ALU.mult),
                     reads=[r_tb, r_ss], writes=[r_xdt])
                P.op("dve", lambda e, L=L: e.tensor_tensor(out=xw[:L, :].rearrange("p (h d) -> p h d", d=64), in0=xdt[:L, :].rearrange("p (h d) -> p h d", d=64), in1=dte[:L, :].unsqueeze(2).to_broadcast([L, 32, 64]), op=ALU.mult),
                     reads=[r_xdt, r_ss], writes=[r_xw])
                for g in range(4):
                    P.op("pe", lambda e, g=g, off=off, L=L: e.transpose(pTB[:L, g * 128:(g + 1) * 128], xc[:, 16 + g, off:off + L], ident[:]), reads=[r_xc[16 + g], r_xdt] + CR, writes=[r_tb])
                P.op("act", lambda e, L=L: e.activation(out=Btok[:L, :], in_=pTB[:L, 0:512], func=AF.Copy), reads=[r_tb], writes=[r_Btok])
                for g in range(4):
                    Bt = xc[:, 16 + g, off:off + L]; Ct = xc[:, 20 + g, off:off + L]
                    rB, rC = r_xc[16 + g], r_xc[20 + g]
                    mm(bank["F"][:L, :L], Bt, Ct, True, True, [rB, rC], [r_b["F"]])
                    P.op("dve", lambda e, L=L: e.tensor_tensor(out=cbm[:L, :L], in0=bank["F"][:L, :L], in1=mask01[:L, :L], op=ALU.mult), reads=[r_b["F"]] + CR, writes=[r_cbm])
                    P.op("pool", lambda e, g=g, L=L: e.tensor_tensor(out=Dg[:L, :, :L], in0=at[:L, g * 8:(g + 1) * 8].unsqueeze(2).to_broadcast([L, 8, L]), in1=tri_f[:L, :L].unsqueeze(1).to_broadcast([L, 8, L]), op=ALU.mult),
                         reads=[r_ss] + CR, writes=[r_Dg])
                    segp = pAB[:L, :].rearrange("p (h l) -> p h l", l=128)
                    if L == 128:
                        for hh in range(2):
                            P.op("pe", lambda e, hh=hh, L=L, segp=segp: e.matmul(segp[:, hh * 4:(hh + 1) * 4, :L], lhsT=ustr_f[:L, :L], rhs=Dg[:L, hh * 4:(hh + 1) * 4, :L], start=True, stop=True),
                                 reads=[r_Dg] + CR, writes=[r_b["AB"[hh]]])
                    else:
                        for hh in range(8):
                            P.op("pe", lambda e, hh=hh, L=L, segp=segp: e.matmul(segp[:, hh, :L], lhsT=ustr_f[:L, :L], rhs=Dg[:L, hh, :L], start=True, stop=True),
                                 reads=[r_Dg] + CR, writes=[r_b["AB"[hh // 4]]])
                    MT3 = MT[:L, :].rearrange("p (h l) -> p h l", l=128)
                    MTb3 = MTb[:L, :].rearrange("p (h l) -> p h l", l=128)
                    P.op("act", lambda e, L=L, segp=segp, MT3=MT3: e.activation(out=MT3[:, :, :L], in_=segp[:, :, :L], func=AF.Exp), reads=[r_b["A"], r_b["B"]], writes=[r_MT])
                    P.op("dve", lambda e, L=L, MT3=MT3, MTb3=MTb3: e.tensor_tensor(out=MTb3[:, :, :L], in0=MT3[:, :, :L], in1=cbm[:L, :L].unsqueeze(1).to_broadcast([L, 8, L]), op=ALU.mult),
                         reads=[r_MT, r_cbm], writes=[r_MTb])
                    for hh in range(8):
                        h = g * 8 + hh
                        mm(bank["C"][:L, hh * 64:(hh + 1) * 64], MTb3[:, hh, :L], xdt[:L, h * 64:(h + 1) * 64], True, True, [r_MTb, r_xdt], [r_b["C"]])
                    mm(bank["D"][:L, :], Ct, hbf[:, g * 512:(g + 1) * 512], True, True, [rC, r_hbf], [r_b["D"]])
                    P.op("dve", lambda e, g=g, L=L: e.tensor_tensor(out=tmpf[:L, :].rearrange("p (h d) -> p h d", d=64), in0=bank["D"][:L, :].rearrange("p (h d) -> p h d", d=64),
                                                                   in1=eacs[:L, g * 8:(g + 1) * 8].unsqueeze(2).to_broadcast([L, 8, 64]), op=ALU.mult),
                         reads=[r_b["D"], r_ss], writes=[r_tmpf])
                    P.op("dve", lambda e, g=g, L=L: e.tensor_tensor(out=ytok[:L, g * 512:(g + 1) * 512], in0=bank["C"][:L, :], in1=tmpf[:L, :], op=ALU.add),
                         reads=[r_b["C"], r_tmpf], writes=[r_ytok])
                    mm(bank["E"][:, :], Btok[:L, g * 128:(g + 1) * 128], xw[:L, g * 512:(g + 1) * 512], True, True, [r_Btok, r_xw], [r_b["E"]])
                    hs3 = hst[:, g * 512:(g + 1) * 512].rearrange("p (h d) -> p h d", d=64)
                    P.op("pool", lambda e, g=g, hs3=hs3: e.tensor_tensor(out=hs3, in0=hs3, in1=cdec[:, g * 8:(g + 1) * 8].unsqueeze(2).to_broadcast([128, 8, 64]), op=ALU.mult),
                         reads=[r_ss, r_hbf], writes=[r_hst])
                    P.op("dve", lambda e, g=g: e.tensor_tensor(out=hst[:, g * 512:(g + 1) * 512], in0=hst[:, g * 512:(g + 1) * 512], in1=bank["E"][:, :], op=ALU.add),
                         reads=[r_b["E"]], writes=[r_hst])
                for c in range(16):
                    P.op("pe", lambda e, c=c, L=L: e.transpose(pTB[:, c * 128:c * 128 + L], ytok[:L, c * 128:(c + 1) * 128], ident[:L, :L]), reads=[r_ytok] + CR, writes=[r_tb])
                    P.op("dve", lambda e, c=c, off=off, L=L: e.scalar_tensor_tensor(out=yT[:, c, off:off + L], in0=xc[:, c, off:off + L], scalar=cvec[:, CV_DS + c:CV_DS + c + 1],
                                                                                 in1=pTB[:, c * 128:c * 128 + L], op0=ALU.mult, op1=ALU.add),
                         reads=[r_tb, r_xc[c]] + CR, writes=[r_yT[c]])
                if sample:
                    P.dma("sp", lambda e, sq_=sq_: e.dma_start(out=ssms_o[sq_], in_=hst[:]), None, reads=[r_hst])
                elif b == NB - 1 and off + L == n:
                    P.dma("sp", lambda e: e.dma_start(out=ssm_o, in_=hst[:]), None, reads=[r_hst])
            if SSTOP <= 3:
                return
            for c in range(16):
                w, rw = wtile(); bk = "AB"[c % 2]; si = c % 2
                proj8(bk, n, w, rw)
                P.op("act", lambda e, bk=bk, si=si: e.activation(out=stg[si][:, :n], in_=bank[bk][:, :n], func=AF.Silu), reads=[r_b[bk]], writes=[r_stg[si]])
                P.op("dve", lambda e, c=c, si=si: e.tensor_tensor(out=yT[:, c, :n], in0=yT[:, c, :n], in1=stg[si][:, :n], op=ALU.mult), reads=[r_stg[si], r_yT[c]], writes=[r_yT[c]])
                P.op("act", lambda e, c=c: e.activation(out=sqb[:, c % 4, :n], in_=yT[:, c, :n], func=AF.Square), reads=[r_yT[c]], writes=[r_sq[c % 4]])
                mm(bank["E"][:, :n], ones_g[:], sqb[:, c % 4, :n], c % 4 == 0, c % 4 == 3, [r_sq[c % 4]] + CR, [r_b["E"]])
                if c % 4 == 3:
                    rms_rstd(bank["E"][:, :n], r_b["E"], n)
                    for cc in range(c - 3, c + 1):
                        P.op("dve", lambda e, cc=cc: e.scalar_tensor_tensor(out=yT[:, cc, :n], in0=yT[:, cc, :n], scalar=cvec[:, CV_SN + cc:CV_SN + cc + 1], in1=rstd[:, :n], op0=ALU.mult, op1=ALU.mult),
                             reads=[r_rstd, r_yT[cc]] + CR, writes=[r_yT[cc]])

            if SSTOP <= 4:
                return
            attention(n, segs, sample, b)

            if SSTOP <= 5:
                return
            oT3 = oT[0:64, 0:16, :]
            for m in range(8):
                w0, r0 = wtile(); w1, r1 = wtile()
                for c in range(16):
                    w, r = (w0, r0) if c < 8 else (w1, r1)
                    mm(bank["A"][:, :n], w[:, (c % 8) * 128:(c % 8 + 1) * 128], yT[:, c, :n], c == 0, c == 15, [r, r_yT[c]], [r_b["A"]])
                wg_, rg_ = wtile()
                proj8("C", n, wg_, rg_)
                P.op("act", lambda e: e.activation(out=stg[0][:, :n], in_=bank["C"][:, :n], func=AF.Sigmoid), reads=[r_b["C"]], writes=[r_stg[0]])
                P.op("dve", lambda e: e.tensor_tensor(out=stg[2][:, :n], in0=bank["A"][:, :n], in1=stg[0][:, :n], op=ALU.mult), reads=[r_b["A"], r_stg[0]], writes=[r_stg[2]])
                w0, r0 = wtile(); w1, r1 = wtile()
                for h in range(16):
                    w, r = (w0, r0) if h < 8 else (w1, r1)
                    mm(bank["B"][:, :n], w[0:64, (h % 8) * 128:(h % 8 + 1) * 128], oT3[:, h, :n], h == 0, h == 15, [r, r_oT], [r_b["B"]])
                wg_, rg_ = wtile()
                proj8("D", n, wg_, rg_)
                P.op("act", lambda e: e.activation(out=stg[1][:, :n], in_=bank["D"][:, :n], func=AF.Sigmoid), reads=[r_b["D"]], writes=[r_stg[1]])
                P.op("dve", lambda e: e.tensor_tensor(out=stg[1][:, :n], in0=bank["B"][:, :n], in1=stg[1][:, :n], op=ALU.mult), reads=[r_b["B"], r_stg[1]], writes=[r_stg[1]])
                P.op("dve", lambda e, m=m: e.tensor_tensor(out=merged[:, m, :n], in0=stg[1][:, :n], in1=stg[2][:, :n], op=ALU.add), reads=[r_stg[1], r_stg[2], r_qT], writes=[r_mg])
            for m in range(8):
                w, rw = wtile(); bk = "AB"[m % 2]
                for c in range(8):
                    mm(bank[bk][:, :n], w[:, c * 128:(c + 1) * 128], merged[:, c, :n], c == 0, c == 7, [rw, r_mg], [r_b[bk]])
                P.op("dve", lambda e, m=m, bk=bk: e.tensor_tensor(out=hT[:, m, :n], in0=hT[:, m, :n], in1=bank[bk][:, :n], op=ALU.add), reads=[r_b[bk], r_hT], writes=[r_hT])
            if SSTOP <= 6:
                return
            norm_to_xn(n, CV_N2)
            ffn(n)
            ydst = ysT_o if sample else yT_o[:, tb:tb + n]
            P.dma("sp", lambda e: e.dma_start(out=ydst.rearrange("(c p) t -> p c t", p=128), in_=hT[:, :, :n]), None, reads=[r_hT])

        def attn_tile(kv, qcols_ap, nqc, kt_ap, nk, v_ap, bias_ap, first, last, diag, acc_bank, rd):
            pass

        def attention(n, segs, sample, b):
            tb = b * T
            for (off, L, sq_) in segs:
                if not sample:
                    qi = (tb + off) // 128
                    P.op("dve", lambda e, off=off, L=L: e.tensor_scalar(out=dgl[:], in0=identf[:16, :16], scalar1=cT[:, off + L - 1:off + L], scalar2=None, op0=ALU.mult), reads=[r_cT] + CR, writes=[r_dgl])
                    mm(bank["F"][:, 0:16], onesf[:16, :], dgl[:], True, True, [r_dgl] + CR, [r_b["F"]])
                    P.op("dve", lambda e: e.tensor_copy(out=cref[:], in_=bank["F"][:, 0:16]), reads=[r_b["F"]], writes=[r_cref])
                    nkt = qi + 1
                    P.op("dve", lambda e, nkt=nkt: e.tensor_tensor(out=biasb[:, :nkt, :], in0=cref[:].unsqueeze(1).to_broadcast([128, nkt, 16]), in1=cks[:, :nkt, :], op=ALU.subtract),
                         reads=[r_cref, r_ck], writes=[r_bias])
                    for kv in range(4):
                        half = kv % 2; cp = kv // 2
                        ob = "CD"[kv % 2]
                        qap = qT[half * 64:(half + 1) * 64, 4 * cp:4 * cp + 4, off:off + L]
                        for kt in range(nkt):
                            sbk = "AB"[kt % 2]
                            dg = (kt == qi)
                            P.op("pe", lambda e, half=half, cp=cp, kt=kt, qap=qap, sbk=sbk, L=L: e.matmul(bank[sbk][:, :4 * L].rearrange("p (g t) -> p g t", g=4), lhsT=kTs[half * 64:(half + 1) * 64, cp, kt * 128:(kt + 1) * 128], rhs=qap, start=True, stop=True),
                                 reads=[r_kT, r_qT], writes=[r_b[sbk]])
                            P.op("dve", lambda e, kv=kv, kt=kt, sbk=sbk, L=L: e.scalar_tensor_tensor(out=sTt[:, :4 * L].rearrange("p (g t) -> p g t", g=4), in0=bank[sbk][:, :4 * L].rearrange("p (g t) -> p g t", g=4), scalar=1.0,
                                                                                                 in1=biasb[:, kt, 4 * kv:4 * kv + 4].unsqueeze(2).to_broadcast([128, 4, L]), op0=ALU.mult, op1=ALU.add),
                                 reads=[r_b[sbk], r_bias], writes=[r_sTt])
                            P.op("act", lambda e, L=L: e.activation(out=PT[:, :4 * L], in_=sTt[:, :4 * L], func=AF.Exp), reads=[r_sTt], writes=[r_PT])
                            if dg:
                                P.op("pool", lambda e, L=L: e.affine_select(out=PT[:, :4 * L].rearrange("p (g t) -> p g t", g=4), in_=PT[:, :4 * L].rearrange("p (g t) -> p g t", g=4), pattern=[[0, 4], [1, L]], compare_op=ALU.is_ge, fill=0.0, base=0, channel_multiplier=-1),
                                     reads=[r_PT], writes=[r_PT])
                            mm(bank[ob][0:65, :4 * L], Vs[:, kt, kv, :], PT[:, :4 * L], kt == 0, kt == nkt - 1, [r_V, r_PT], [r_b[ob]])
                        finish_head(kv, ob, off, L, 4 * L)
                else:
                    sample_attention(off, L, sq_)

        def finish_head(kv, ob, off, L, ncol):
            P.op("dve", lambda e: e.reciprocal(out=rec[64:65, :ncol], in_=bank[ob][64:65, :ncol]), reads=[r_b[ob]], writes=[r_rec])
            mm(bank["E"][0:64, :ncol], onesf[64:65, 0:64], rec[64:65, :ncol], True, True, [r_rec] + CR, [r_b["E"]])
            P.op("act", lambda e: e.activation(out=bcs[:, :ncol], in_=bank["E"][0:64, :ncol], func=AF.Copy), reads=[r_b["E"]], writes=[r_bcs])
            P.op("dve", lambda e: e.tensor_tensor(out=oT[0:64, 4 * kv:4 * kv + 4, off:off + L], in0=bank[ob][0:64, :ncol].rearrange("p (g t) -> p g t", g=4), in1=bcs[:, :ncol].rearrange("p (g t) -> p g t", g=4), op=ALU.mult),
                 reads=[r_b[ob], r_bcs] + r_xc, writes=[r_oT])

        if do_sample:
            qS = sb("qS", [128, NSEQ_S, 8, LS], BF16); r_qS = Res("qS")
            ksb = sb("ksb", [128, 2, NSEQ_S, 128], BF16)
            P.op("pool", lambda e: e.memset(ksb[:].rearrange("p a b c -> p (a b c)"), 0.0), writes=[r_kT])
            Vsm = sb("Vsm", [128, NSEQ_S, 4, 65], BF16)
            P.op("pool", lambda e: e.memset(Vsm[:].rearrange("p a b c -> p (a b c)"), 1.0), writes=[r_V])
            ptab = sb("ptab", [128, NSEQ_S * NPG], I32); r_pt = Res("ptab")
            ridx = ptab
            piota = sb("piota", [128, 1], I32)
            P.dma("sp", lambda e: e.dma_start(out=ptab[:], in_=pt_d.partition_broadcast(128)), None, writes=[r_pt])
            P.op("pool", lambda e: e.iota(piota[:], pattern=[[0, 1]], base=0, channel_multiplier=1), writes=[r_pt])
            P.op("pool", lambda e: e.tensor_scalar(out=ridx[:], in0=ptab[:], scalar1=128, scalar2=piota[:, 0:1], op0=ALU.mult, op1=ALU.add), reads=[r_pt], writes=[r_pt])
            import os as _os
            NPB = int(_os.environ.get("K_NPB", "4"))
            r_kpg = [Res(f"kpg{i}") for i in range(2)]; r_vpg = [Res(f"vpg{i}") for i in range(2)]
            r_vraw = [Res(f"vraw{i}") for i in range(2)]; r_lpg = [Res(f"lpg{i}") for i in range(2)]; r_ktp = Res("ktp")
            PGE = NPB * 256; VGE = NPB * 4 * 65
            if 2 * SEQ >= 5 * PGE + 2 * VGE:
                kflat = kTs[:].rearrange("p a b -> p (a b)")
                kpg = [kflat[:, i * PGE:(i + 1) * PGE].rearrange("p (a b) -> p a b", a=NPB) for i in range(2)]
                vraw = [kflat[:, (2 + i) * PGE:(3 + i) * PGE].rearrange("p (a b) -> p a b", a=NPB) for i in range(2)]
                ktp = kflat[:, 4 * PGE:5 * PGE].rearrange("p (a b c) -> p a b c", a=NPB, b=2)
                vpg = [kflat[:, 5 * PGE + i * VGE:5 * PGE + (i + 1) * VGE].rearrange("p (a b c) -> p a b c", a=NPB, b=4) for i in range(2)]
            else:
                kpg = [sb(f"kpg{i}", [128, NPB, 256], BF16) for i in range(2)]
                vraw = [sb(f"vraw{i}", [128, NPB, 256], BF16) for i in range(2)]
                vpg = [sb(f"vpg{i}", [128, NPB, 4, 65], BF16) for i in range(2)]
                ktp = sb("ktp", [128, NPB, 2, 128], BF16)
            lpg = [sb(f"lpg{i}", [128, NPB, 16]) for i in range(2)]
            s_pg = [P.slot(f"pg{i}") for i in range(2)]
            Racc = sb("Racc", [128, 16]); r_R = Res("Racc")
            bpg = sb("bpg", [128, NPB, 16]); r_bpg = Res("bpg")
            lftok = sb("lftok", [128, 16]); r_lftok = Res("lftok")
            oacc = sb("oacc", [128, 16 * LS]); r_oacc = Res("oacc")
            bph = sb("bph", [128, 2, NPB * 2, 4]); r_bph = Res("bph")

        def sample_attention(off, L, sq_):
            nq = 4 * L
            if sq_ == 0:
                for i in range(2):
                    P.op("pool", lambda e, i=i: e.memset(vpg[i][:].rearrange("p a b c -> p (a b c)"), 1.0), writes=[r_vpg[i], r_kT])
                P.op("dve", lambda e: e.tensor_copy(out=qS[:].rearrange("p s c t -> p c s t"), in_=qT[:, :, :TS].rearrange("p c (s t) -> p c s t", s=NSEQ_S)), reads=[r_qT], writes=[r_qS])
            import os as _os
            nb = 0 if _os.environ.get('K_NOPAGES') else NPG // NPB
            P.op("pool", lambda e: e.memset(Racc[:], 0.0), writes=[r_R])
            mm(bank["F"][:L, 0:16], lfT[:, off:off + L], identf[:16, :16], True, True, [r_lf] + CR, [r_b["F"]])
            P.op("dve", lambda e: e.tensor_copy(out=lftok[:L, :], in_=bank["F"][:L, 0:16]), reads=[r_b["F"]], writes=[r_lftok])
            P.op("dve", lambda e: e.tensor_copy(out=Racc[:L, :], in_=bank["F"][:L, 0:16]), reads=[r_b["F"], r_R], writes=[r_R])
            mm(bank["F"][:, 16:32], ustr_f[:L, :], lftok[:L, :], True, True, [r_lftok] + CR, [r_b["F"]])
            P.op("dve", lambda e: e.tensor_copy(out=bpg[:, 0, :], in_=bank["F"][:, 16:32]), reads=[r_b["F"]], writes=[r_bpg])
            import os as _os
            ASTOP = float(_os.environ.get("K_ASTOP", "99"))
            if ASTOP <= 1:
                return
            for kv in range(4):
                half = kv % 2; cp = kv // 2; bk = "AB"[half]
                qap = qS[half * 64:(half + 1) * 64, sq_, 4 * cp:4 * cp + 4, :].rearrange("p c t -> p (c t)")
                P.op("pe", lambda e, half=half, cp=cp, qap=qap, bk=bk: e.matmul(bank[bk][:, cp * nq:(cp + 1) * nq], lhsT=ksb[half * 64:(half + 1) * 64, cp, sq_, :], rhs=qap, start=True, stop=True),
                     reads=[r_kT, r_qS], writes=[r_b[bk]])
            if ASTOP <= 1.2:
                return
            for half in range(2):
                bk = "AB"[half]
                P.op("dve", lambda e, half=half: e.tensor_copy(out=bph[:, half, 0:2, :], in_=bpg[:, 0, :].rearrange("p (c h g) -> p c h g", c=2, h=2)[:, :, half, :]), reads=[r_bpg], writes=[r_bph])
                o3 = sTt[:, half * 2 * nq:(half + 1) * 2 * nq].rearrange("p (a t) -> p a t", t=L)
                i3 = bank[bk][:, :2 * nq].rearrange("p (a t) -> p a t", t=L)
                b3 = bph[:, half, 0:2, :].rearrange("p c g -> p (c g)").unsqueeze(2).to_broadcast([128, 8, L])
                P.op("dve", lambda e, o3=o3, i3=i3, b3=b3: e.scalar_tensor_tensor(out=o3, in0=i3, scalar=1.0, in1=b3, op0=ALU.mult, op1=ALU.add),
                     reads=[r_b[bk], r_bph], writes=[r_sTt])
            if ASTOP <= 1.4:
                return
            P.op("act", lambda e: e.activation(out=PT[:, :4 * nq], in_=sTt[:, :4 * nq], func=AF.Exp), reads=[r_sTt], writes=[r_PT])
            if ASTOP <= 1.6:
                return
            P.op("dve", lambda e: e.tensor_tensor(out=PT[:, :4 * nq].rearrange("p (h t) -> p h t", h=16), in0=PT[:, :4 * nq].rearrange("p (h t) -> p h t", h=16), in1=mask01[:, :L].unsqueeze(1).to_broadcast([128, 16, L]), op=ALU.mult),
                 reads=[r_PT] + CR, writes=[r_PT])
            if ASTOP <= 2:
                return
            for kv in range(4):
                pc = ((kv % 2) * 2 + kv // 2) * nq
                mm(bank["C"][0:65, kv * nq:(kv + 1) * nq], Vsm[:, sq_, kv, :], PT[:, pc:pc + nq], True, True, [r_V, r_PT], [r_b["C"]])
            P.op("dve", lambda e: e.tensor_copy(out=oacc[0:65, :4 * nq], in_=bank["C"][0:65, :4 * nq]), reads=[r_b["C"]], writes=[r_oacc])
            if ASTOP <= 3:
                return
            for bi in range(nb):
                pb = nb - 1 - bi
                i2 = bi % 2
                for pp in range(NPB):
                    pg = pb * NPB + pp
                    col = sq_ * NPG + pg
                    P.dma("pool", lambda e, i2=i2, pp=pp, col=col: e.indirect_dma_start(out=kpg[i2][:, pp, :], out_offset=None, in_=ck_d, in_offset=bass.IndirectOffsetOnAxis(ap=ridx[:, col:col + 1], axis=0)),
                          None, reads=[r_pt], writes=[r_kpg[i2]])
                    P.dma("pool", lambda e, i2=i2, pp=pp, col=col: e.indirect_dma_start(out=vraw[i2][:, pp, :], out_offset=None, in_=cvv_d, in_offset=bass.IndirectOffsetOnAxis(ap=ridx[:, col:col + 1], axis=0)),
                          None, reads=[r_pt], writes=[r_vraw[i2]])
                    P.dma("pool", lambda e, i2=i2, pp=pp, col=col: e.indirect_dma_start(out=lpg[i2][:, pp, :], out_offset=None, in_=clf_d, in_offset=bass.IndirectOffsetOnAxis(ap=ridx[:, col:col + 1], axis=0)),
                          None, reads=[r_pt], writes=[r_lpg[i2]])
                P.op("act", lambda e, i2=i2: e.activation(out=vpg[i2][:, :, :, 0:64], in_=vraw[i2][:].rearrange("p a (b c) -> p a b c", b=4), func=AF.Copy), reads=[r_vraw[i2]], writes=[r_vpg[i2]])
                for pp in range(NPB):
                    for c in range(2):
                        P.op("pe", lambda e, i2=i2, pp=pp, c=c: e.transpose(pTB[:, (pp * 2 + c) * 128:(pp * 2 + c + 1) * 128], kpg[i2][:, pp, c * 128:(c + 1) * 128], ident[:]),
                             reads=[r_kpg[i2]] + CR, writes=[r_tb])
                P.op("act", lambda e: e.activation(out=ktp[:].rearrange("p a b c -> p (a b c)"), in_=pTB[:, 0:NPB * 256], func=AF.Copy), reads=[r_tb], writes=[r_ktp])
                for pp in reversed(range(NPB)):
                    mm(bank["F"][:, 0:16], ustr_f[:], lpg[i2][:, pp, :], True, False, [r_lpg[i2]] + CR, [r_b["F"]])
                    mm(bank["F"][:, 0:16], onesf[:], Racc[:], False, True, [r_R] + CR, [r_b["F"]])
                    P.op("dve", lambda e, pp=pp: e.tensor_copy(out=bpg[:, pp, :], in_=bank["F"][:, 0:16]), reads=[r_b["F"]], writes=[r_bpg])
                    P.op("dve", lambda e, pp=pp, i2=i2: e.tensor_tensor(out=Racc[:], in0=Racc[:], in1=lpg[i2][:, pp, :], op=ALU.add), reads=[r_lpg[i2], r_R], writes=[r_R])
                for pp in range(NPB):
                    for kv in range(4):
                        half = kv % 2; cp = kv // 2; bk = "AB"[half]
                        qap = qS[half * 64:(half + 1) * 64, sq_, 4 * cp:4 * cp + 4, :].rearrange("p c t -> p (c t)")
                        c0 = (pp * 2 + cp) * nq
                        P.op("pe", lambda e, half=half, cp=cp, qap=qap, pp=pp, c0=c0, bk=bk: e.matmul(bank[bk][:, c0:c0 + nq], lhsT=ktp[half * 64:(half + 1) * 64, pp, cp, :], rhs=qap, start=True, stop=True),
                             reads=[r_ktp, r_qS], writes=[r_b[bk]])
                tot_c = NPB * 4 * nq
                hc = NPB * 2 * nq
                for half in range(2):
                    bk = "AB"[half]
                    P.op("dve", lambda e, half=half: e.tensor_copy(out=bph[:, half, :, :], in_=bpg[:].rearrange("p n (c h g) -> p (n c) h g", c=2, h=2)[:, :, half, :]), reads=[r_bpg], writes=[r_bph])
                    o3 = sTt[:, half * hc:(half + 1) * hc].rearrange("p (a t) -> p a t", t=L)
                    i3 = bank[bk][:, :hc].rearrange("p (a t) -> p a t", t=L)
                    b3 = bph[:, half, :, :].rearrange("p a g -> p (a g)").unsqueeze(2).to_broadcast([128, NPB * 8, L])
                    P.op("dve", lambda e, o3=o3, i3=i3, b3=b3: e.scalar_tensor_tensor(out=o3, in0=i3, scalar=1.0, in1=b3, op0=ALU.mult, op1=ALU.add),
                         reads=[r_b[bk], r_bph], writes=[r_sTt])
                P.op("act", lambda e: e.activation(out=PT[:, :tot_c], in_=sTt[:, :tot_c], func=AF.Exp), reads=[r_sTt], writes=[r_PT])
                for pp in range(NPB):
                    for kv in range(4):
                        c0 = (pp * 4 + kv) * nq
                        pc = ((kv % 2) * NPB * 2 + pp * 2 + kv // 2) * nq
                        mm(bank["C"][0:65, c0:c0 + nq], vpg[i2][:, pp, kv, :], PT[:, pc:pc + nq], True, True, [r_vpg[i2], r_PT], [r_b["C"]])
                for pp in range(NPB):
                    P.op("dve", lambda e, pp=pp: e.tensor_tensor(out=oacc[0:65, :4 * nq], in0=oacc[0:65, :4 * nq], in1=bank["C"][0:65, pp * 4 * nq:(pp + 1) * 4 * nq], op=ALU.add),
                         reads=[r_b["C"], r_oacc], writes=[r_oacc])
            ncol = 4 * nq
            P.op("dve", lambda e: e.reciprocal(out=rec[64:65, :ncol], in_=oacc[64:65, :ncol]), reads=[r_oacc], writes=[r_rec])
            mm(bank["E"][0:64, :ncol], onesf[64:65, 0:64], rec[64:65, :ncol], True, True, [r_rec] + CR, [r_b["E"]])
            P.op("dve", lambda e: e.tensor_tensor(out=oT[0:64, 0:16, off:off + L], in0=oacc[0:64, :ncol].rearrange("p (h t) -> p h t", h=16), in1=bank["E"][0:64, :ncol].rearrange("p (h t) -> p h t", h=16), op=ALU.mult),
                 reads=[r_b["E"], r_oacc] + r_xc, writes=[r_oT])

        import os as _os
        for b in range(0 if _os.environ.get("K_NOPROMPT") else NB):
            block(T, [(i * 128, 128, 0) for i in range(4)], False, b)
        if do_sample:
            block(TS, [(i * LS, LS, i) for i in range(NSEQ_S)], True, 0)
        n, ns = P.emit(nc)
        print("ops", n, "signals", ns, flush=True)
    return nc


_PROG_CACHE = {}


def kernel(x_prompt, x_sample, cache_k, cache_v, cache_logf, state_ssm, state_conv, page_table,
           ffn1_norm, ffn1_w_gate, ffn1_w_up, ffn1_w_down, mix_norm, w_in, conv_w, conv_b,
           dt_bias, a_log, d_skip, ssd_norm, q_norm, k_norm, b_f, w_ssd_proj, w_attn_proj, w_out,
           ffn2_norm, ffn2_w_gate, ffn2_w_up, ffn2_w_down):
    f = lambda a: np.asarray(a, dtype=np.float32)
    B, SEQ, _ = x_prompt.shape
    DB, DS, _ = x_sample.shape
    NPOOL = cache_k.shape[1]
    NPG = page_table.shape[1]
    p = dict(ffn1_norm=f(ffn1_norm)[0], ffn1_w_gate=f(ffn1_w_gate)[0], ffn1_w_up=f(ffn1_w_up)[0], ffn1_w_down=f(ffn1_w_down)[0],
             mix_norm=f(mix_norm)[0], w_in=f(w_in)[0], conv_w=f(conv_w)[0], conv_b=f(conv_b)[0], dt_bias=f(dt_bias)[0],
             a_log=f(a_log)[0], d_skip=f(d_skip)[0], ssd_norm=f(ssd_norm)[0], q_norm=f(q_norm)[0], k_norm=f(k_norm)[0],
             b_f=f(b_f)[0], w_ssd_proj=f(w_ssd_proj)[0], w_attn_proj=f(w_attn_proj)[0], w_out=f(w_out)[0],
             ffn2_norm=f(ffn2_norm)[0], ffn2_w_gate=f(ffn2_w_gate)[0], ffn2_w_up=f(ffn2_w_up)[0], ffn2_w_down=f(ffn2_w_down)[0])
    wst = build_weight_stream(p)
    cvec = build_cvec(p)
    rowc = np.concatenate([p["dt_bias"], p["a_log"], p["d_skip"]]).reshape(1, 96).astype(np.float32)
    wdt = np.ascontiguousarray(p["w_in"][:, O_DT:O_DT + 32].reshape(8, 128, 32).transpose(1, 0, 2).reshape(128, 256))
    ck2 = np.ascontiguousarray(f(cache_k)[0].reshape(NPOOL * 128, 256))
    cv2 = np.ascontiguousarray(f(cache_v)[0].reshape(NPOOL * 128, 256))
    cl2 = np.ascontiguousarray(f(cache_logf)[0].reshape(NPOOL * 128, 16))
    xp = f(x_prompt); xs = f(x_sample)
    ssm = f(state_ssm)[0]; cst = f(state_conv)[0]
    pt = np.asarray(page_table, dtype=np.int32)
    key = (SEQ, NPG, NPOOL)
    if key not in _PROG_CACHE:
        import os as _os
        _PROG_CACHE[key] = build_program(SEQ, NPG, NPOOL, do_sample=(_os.environ.get('K_NOSAMPLE') is None))
    nc = _PROG_CACHE[key]
    in_maps = []
    for c in range(NCORES):
        sq = slice(NSEQ_S * c, NSEQ_S * (c + 1))
        in_maps.append({
            "xT": np.ascontiguousarray(xp[c % B].T),
            "xsT": np.ascontiguousarray(xs[sq].reshape(NSEQ_S * DS, D).T),
            "wst": wst, "cvec": cvec, "rowc": rowc, "wdt": wdt,
            "ssm0": np.ascontiguousarray(ssm[sq].reshape(NSEQ_S, 2048, 128).transpose(0, 2, 1)),
            "conv0": np.ascontiguousarray(cst[sq].reshape(NSEQ_S, 3, 24, 128).transpose(0, 3, 2, 1).reshape(NSEQ_S, 128, 72)),
            "cache_k": ck2, "cache_v": cv2, "cache_lf": cl2,
            "ptab": np.ascontiguousarray(pt[sq].reshape(1, NSEQ_S * NPG)),
        })
    res = run_bass_kernel_spmd(nc, in_maps, core_ids=list(range(NCORES))).results
    yp = np.stack([res[b]["yT"].T for b in range(B)])
    kp = np.stack([res[b]["kT"].T.reshape(SEQ, 4, 64) for b in range(B)])[None]
    vp = np.stack([res[b]["vT"].T.reshape(SEQ, 4, 64) for b in range(B)])[None]
    lp = np.stack([res[b]["lfT"].T for b in range(B)])[None]
    sp = np.stack([res[b]["ssm"].T.reshape(32, 64, 128) for b in range(B)])[None]
    cp = np.stack([res[b]["conv"].reshape(128, 24, 3).transpose(2, 1, 0).reshape(3, 3072) for b in range(B)])[None]
    ys = np.concatenate([res[c]["ysT"].T.reshape(NSEQ_S, DS, D) for c in range(NCORES)])
    ks = np.concatenate([res[c]["ksT"].T.reshape(NSEQ_S, DS, 4, 64) for c in range(NCORES)])[None]
    vs = np.concatenate([res[c]["vsT"].T.reshape(NSEQ_S, DS, 4, 64) for c in range(NCORES)])[None]
    ls = np.concatenate([res[c]["lfsT"].T.reshape(NSEQ_S, DS, 16) for c in range(NCORES)])[None]
    ss = np.concatenate([res[c]["ssms"].transpose(0, 2, 1).reshape(NSEQ_S, 32, 64, 128) for c in range(NCORES)])[None]
    cs = np.concatenate([res[c]["convs"].reshape(NSEQ_S, 128, 24, 3).transpose(0, 3, 2, 1).reshape(NSEQ_S, 3, 3072) for c in range(NCORES)])[None]
    o = (yp, ys, kp, vp, lp, sp, cp, ks, vs, ls, ss, cs)
    return tuple(np.ascontiguousarray(a, dtype=np.float32) for a in o)
```

```python
import contextlib
import numpy as np
import concourse.bass as bass
import concourse.mybir as mybir
from concourse.bass_utils import run_bass_kernel_spmd

F32 = mybir.dt.float32
BF16 = mybir.dt.bfloat16
I32 = mybir.dt.int32
AF = mybir.ActivationFunctionType
ALU = mybir.AluOpType

D = 1024
DFF = 2816
NF = 22
NSEQ_S = 4
LS = 8
NCORES = 8


class Res:
    __slots__ = ("name", "w", "r")

    def __init__(self, name=""):
        self.name = name
        self.w = None
        self.r = []


class DmaSlot:
    __slots__ = ("sem", "count", "name")

    def __init__(self, name):
        self.name = name
        self.sem = None
        self.count = 0


ENGS = ("pe", "act", "dve", "pool", "sp")
EPOCH = 30000


class Prog:
    def __init__(self):
        self.ops = {e: [] for e in ENGS}
        self.waited = {e: {} for e in ENGS}
        self.needed = {e: set() for e in ENGS}
        self.slots = []

    def slot(self, name=""):
        s = DmaSlot(name)
        self.slots.append(s)
        return s

    def _deps(self, eng, reads, writes):
        deps = []
        for r in reads:
            if r.w is not None:
                deps.append(r.w)
        for w in writes:
            if w.w is not None:
                deps.append(w.w)
            deps.extend(w.r)
        out = []
        wd = self.waited[eng]
        best = {}
        for d in deps:
            if d[0] == "dma":
                key = ("dma", id(d[1]))
                if key not in best or best[key][2] < d[2]:
                    best[key] = d
            else:
                if d[0] == eng and eng == "pe":
                    continue
                if d[0] not in best or best[d[0]][1] < d[1]:
                    best[d[0]] = d
        for key, d in best.items():
            v = d[2] if d[0] == "dma" else d[1]
            if wd.get(key, 0) >= v:
                continue
            wd[key] = v
            if d[0] != "dma":
                self.needed[d[0]].add(v)
            out.append(d)
        return out

    def op(self, eng, fn, reads=(), writes=()):
        waits = self._deps(eng, reads, writes)
        lst = self.ops[eng]
        lst.append([fn, waits, None, 0])
        tok = (eng, len(lst))
        for r in reads:
            r.r.append(tok)
        for w in writes:
            w.w = tok
            w.r = []
        return tok

    def dma(self, eng, fn, slot, reads=(), writes=()):
        if slot is None:
            key = writes[0] if len(writes) else reads[0]
            if not hasattr(self, "_auto"):
                self._auto = {}
            if id(key) not in self._auto:
                self._auto[id(key)] = self.slot("a" + key.name)
            slot = self._auto[id(key)]
        waits = self._deps(eng, reads, writes)
        slot.count += 16
        self.ops[eng].append([fn, waits, slot, slot.count])
        tok = ("dma", slot, slot.count)
        for r in reads:
            r.r.append(tok)
        for w in writes:
            w.w = tok
            w.r = []
        return tok

    def emit(self, nc, final_engine="sp"):
        with contextlib.ExitStack() as st:
            rank = {}
            nsig = {}
            for e in ENGS:
                flagged = sorted(self.needed[e])
                rank[e] = {idx: i + 1 for i, idx in enumerate(flagged)}
                nsig[e] = len(flagged)
            sems = {}
            for e in ENGS:
                n_ep = max(1, (nsig[e] + EPOCH - 1) // EPOCH)
                sems[e] = [st.enter_context(nc.semaphore(f"s_{e}{k}")) for k in range(n_ep)]
            for s in self.slots:
                if s.count > 0:
                    s.sem = st.enter_context(nc.semaphore(f"d_{s.name}"))
            fin = [("dma", s, s.count) for s in self.slots if s.count > 0]
            self.ops[final_engine].append([None, fin, None, 0])
            block = st.enter_context(nc.Block())

            def run(e):
                def body(eng):
                    for i, (fn, waits, slot, val) in enumerate(self.ops[e]):
                        for d in waits:
                            if d[0] == "dma":
                                eng.wait_ge(d[1].sem, d[2])
                            else:
                                sg = rank[d[0]][d[1]]
                                eng.wait_ge(sems[d[0]][(sg - 1) // EPOCH], (sg - 1) % EPOCH + 1)
                        if fn is None:
                            continue
                        ins = fn(eng)
                        if slot is not None:
                            ins.then_inc(slot.sem, 16)
                        else:
                            sg = rank[e].get(i + 1)
                            if sg is not None:
                                ins.then_inc(sems[e][(sg - 1) // EPOCH], 1)
                return body

            block.tensor(run("pe"))
            block.scalar(run("act"))
            block.vector(run("dve"))
            block.gpsimd(run("pool"))
            block.sync(run("sp"))
        return {e: len(self.ops[e]) for e in ENGS}, nsig


def _kc_tile(w_cols):
    return w_cols.reshape(8, 128, 128).transpose(1, 0, 2).reshape(128, 1024)


def _pad_cols(w, n):
    out = np.zeros((w.shape[0], n), np.float32)
    out[:, : w.shape[1]] = w
    return out


def _ffn_tiles(wg, wu, wd):
    tiles = []
    for j in range(NF):
        tiles.append(_kc_tile(wg[:, j * 128:(j + 1) * 128]))
        tiles.append(_kc_tile(wu[:, j * 128:(j + 1) * 128]))
    wdp = np.concatenate([wd, np.zeros((24 * 128 - DFF, D), np.float32)], 0)
    for m in range(8):
        blk = wdp[:, m * 128:(m + 1) * 128].reshape(24, 128, 128)
        for s in range(3):
            tiles.append(blk[s * 8:(s + 1) * 8].transpose(1, 0, 2).reshape(128, 1024))
    return tiles


O_Z, O_XBC, O_DT, O_Q, O_K, O_V, O_F, O_GS, O_GA = 0, 2048, 5120, 5152, 6176, 6432, 6688, 6704, 7728


def _q_perm_cols():
    cols = []
    for cp in range(2):
        for g in range(4):
            for half in range(2):
                kv = 2 * cp + half
                h = 4 * kv + g
                cols.extend(range(O_Q + h * 64, O_Q + (h + 1) * 64))
    return np.array(cols)


def build_weight_stream(p):
    t = []
    t += _ffn_tiles(p["ffn1_w_gate"], p["ffn1_w_up"], p["ffn1_w_down"])
    w_in = p["w_in"]
    for c in range(2):
        t.append(_kc_tile(w_in[:, O_K + c * 128: O_K + (c + 1) * 128]))
    for c in range(2):
        t.append(_kc_tile(w_in[:, O_V + c * 128: O_V + (c + 1) * 128]))
    t.append(_kc_tile(_pad_cols(w_in[:, O_F:O_F + 16], 128)))
    qc = w_in[:, _q_perm_cols()]
    for c in range(8):
        t.append(_kc_tile(qc[:, c * 128:(c + 1) * 128]))
    for c in range(24):
        t.append(_kc_tile(w_in[:, O_XBC + c * 128: O_XBC + (c + 1) * 128]))
    for c in range(16):
        t.append(_kc_tile(w_in[:, O_Z + c * 128: O_Z + (c + 1) * 128]))
    wsp = p["w_ssd_proj"]
    wap = p["w_attn_proj"]
    for m in range(8):
        blk = wsp[:, m * 128:(m + 1) * 128].reshape(16, 128, 128)
        for s in range(2):
            t.append(blk[s * 8:(s + 1) * 8].transpose(1, 0, 2).reshape(128, 1024))
        t.append(_kc_tile(w_in[:, O_GS + m * 128: O_GS + (m + 1) * 128]))
        ablk = wap[:, m * 128:(m + 1) * 128].reshape(16, 64, 128)
        for s in range(2):
            a = np.zeros((128, 1024), np.float32)
            a[:64] = ablk[s * 8:(s + 1) * 8].transpose(1, 0, 2).reshape(64, 1024)
            t.append(a)
        t.append(_kc_tile(w_in[:, O_GA + m * 128: O_GA + (m + 1) * 128]))
    wo = p["w_out"]
    for m in range(8):
        t.append(_kc_tile(wo[:, m * 128:(m + 1) * 128]))
    t += _ffn_tiles(p["ffn2_w_gate"], p["ffn2_w_up"], p["ffn2_w_down"])
    return np.ascontiguousarray(np.stack(t)).astype(np.float32)


NT = 68 + 37 + 16 + 48 + 8 + 68

CV_N1, CV_NM, CV_N2, CV_SN, CV_CW, CV_CB, CV_QN, CV_KN, CV_DS, CV_BF, CV_NBF = 0, 8, 16, 24, 40, 136, 160, 161, 162, 178, 179
NCV = 180


def build_cvec(p):
    cv = np.zeros((128, NCV), np.float32)
    cv[:, CV_N1:CV_N1 + 8] = p["ffn1_norm"].reshape(8, 128).T
    cv[:, CV_NM:CV_NM + 8] = p["mix_norm"].reshape(8, 128).T
    cv[:, CV_N2:CV_N2 + 8] = p["ffn2_norm"].reshape(8, 128).T
    cv[:, CV_SN:CV_SN + 16] = p["ssd_norm"].reshape(16, 128).T
    cw = p["conv_w"].reshape(4, 24, 128)
    cv[:, CV_CW:CV_CW + 96] = cw.transpose(2, 1, 0).reshape(128, 96)
    cv[:, CV_CB:CV_CB + 24] = p["conv_b"].reshape(24, 128).T
    cv[:, CV_QN] = np.tile(p["q_norm"], 2)
    cv[:, CV_KN] = np.tile(p["k_norm"], 2)
    cv[:, CV_DS:CV_DS + 16] = np.repeat(p["d_skip"], 64).reshape(16, 128).T
    cv[:16, CV_BF] = p["b_f"]
    return cv


def build_program(SEQ, NPG, NPOOL, do_sample=True):
    T = 512
    NB = SEQ // T
    NTIL = SEQ // 128
    TS = NSEQ_S * LS
    nc = bass.Bass("TRN2", target_bir_lowering=False)

    def din(name, shape, dt=F32):
        return nc.dram_tensor(name, shape, dt, kind="ExternalInput").ap()

    def dout(name, shape, dt=F32):
        return nc.dram_tensor(name, shape, dt, kind="ExternalOutput").ap()

    xT_d = din("xT", [D, SEQ])
    xsT_d = din("xsT", [D, TS])
    wst_d = din("wst", [NT, 128, 1024])
    cvec_d = din("cvec", [128, NCV])
    rowc_d = din("rowc", [1, 96])
    wdt_d = din("wdt", [128, 8 * 32])
    ssm0_d = din("ssm0", [NSEQ_S, 128, 2048])
    conv0_d = din("conv0", [NSEQ_S, 128, 24 * 3])
    ck_d = din("cache_k", [NPOOL * 128, 256])
    cvv_d = din("cache_v", [NPOOL * 128, 256])
    clf_d = din("cache_lf", [NPOOL * 128, 16])
    pt_d = din("ptab", [1, NSEQ_S * NPG], I32)

    yT_o = dout("yT", [D, SEQ])
    ysT_o = dout("ysT", [D, TS])
    kT_o = dout("kT", [256, SEQ])
    vT_o = dout("vT", [256, SEQ])
    lfT_o = dout("lfT", [16, SEQ])
    ssm_o = dout("ssm", [128, 2048])
    conv_o = dout("conv", [128, 72])
    ksT_o = dout("ksT", [256, TS])
    vsT_o = dout("vsT", [256, TS])
    lfsT_o = dout("lfsT", [16, TS])
    ssms_o = dout("ssms", [NSEQ_S, 128, 2048])
    convs_o = dout("convs", [NSEQ_S, 128, 72])

    wbf_d = nc.dram_tensor("wbf", [NT, 128, 1024], BF16, kind="Internal").ap()

    P = Prog()
    st = contextlib.ExitStack()
    with st:
        def sb(name, shape, dt=F32):
            return st.enter_context(nc.sbuf_tensor("sb_" + name, shape, dt))

        def pst(name, shape, dt=F32):
            return st.enter_context(nc.psum_tensor("ps_" + name, shape, dt))

        cvec = sb("cvec", [128, NCV]); r_c = Res("const")
        rowc = sb("rowc", [128, 96])
        wdt32 = sb("wdt32", [128, 256]); wdt = sb("wdt", [128, 8, 32], BF16)
        ident = sb("ident", [128, 128], BF16)
        identf = sb("identf", [128, 128], F32)
        ones_d = sb("ones_d", [128, 128], BF16)
        ones_g = sb("ones_g", [128, 128], BF16)
        bd64 = sb("bd64", [128, 128], BF16)
        onesf = sb("onesf", [128, 128], F32)
        tri_f = sb("tri_f", [128, 128], F32)
        ustr_f = sb("ustr_f", [128, 128], F32)
        mask01 = sb("mask01", [128, 128], F32)
        epsc = sb("epsc", [128, 1]); onec = sb("onec", [128, 1])
        A_bc = sb("A_bc", [128, 32]); nbf = sb("nbf", [128, 1])
        s_c = P.slot("const")
        r_c1 = Res("c1"); r_c2 = Res("c2")
        P.dma("sp", lambda e: e.dma_start(out=cvec[:], in_=cvec_d), None, writes=[r_c])
        P.dma("sp", lambda e: e.dma_start(out=rowc[:], in_=rowc_d.partition_broadcast(128)), None, writes=[r_c1])
        P.dma("sp", lambda e: e.dma_start(out=wdt32[:], in_=wdt_d), None, writes=[r_c2])
        r_k = Res("consts2")
        P.op("pool", lambda e: e.memset(identf[:], 1.0), writes=[r_k])
        P.op("pool", lambda e: e.affine_select(out=identf[:], in_=identf[:], pattern=[[-1, 128]], compare_op=ALU.is_equal, fill=0.0, base=0, channel_multiplier=1), writes=[r_k])
        P.op("pool", lambda e: e.tensor_copy(out=ident[:], in_=identf[:]), writes=[r_k])
        P.op("pool", lambda e: e.memset(onesf[:], 1.0), writes=[r_k])
        P.op("pool", lambda e: e.memset(ones_d[:], 1.0 / 1024), writes=[r_k])
        P.op("pool", lambda e: e.memset(ones_g[:], 1.0 / 512), writes=[r_k])
        P.op("pool", lambda e: e.memset(bd64[:], 0.0), writes=[r_k])
        P.op("pool", lambda e: e.memset(bd64[0:64, 0:64], 1.0 / 64), writes=[r_k])
        P.op("pool", lambda e: e.memset(bd64[64:128, 64:128], 1.0 / 64), writes=[r_k])
        P.op("pool", lambda e: e.affine_select(out=tri_f[:], in_=onesf[:], pattern=[[1, 128]], compare_op=ALU.is_ge, fill=0.0, base=0, channel_multiplier=-1), writes=[r_k])
        P.op("pool", lambda e: e.tensor_copy(out=mask01[:], in_=tri_f[:]), writes=[r_k])
        P.op("pool", lambda e: e.affine_select(out=ustr_f[:], in_=onesf[:], pattern=[[-1, 128]], compare_op=ALU.is_gt, fill=0.0, base=0, channel_multiplier=1), writes=[r_k])
        P.op("pool", lambda e: e.memset(epsc[:], 1e-6), writes=[r_k])
        P.op("pool", lambda e: e.memset(onec[:], 1.0), writes=[r_k])
        P.op("act", lambda e: e.activation(out=A_bc[:], in_=rowc[:, 32:64], func=AF.Exp), reads=[r_c1], writes=[r_k])
        P.op("dve", lambda e: e.tensor_scalar(out=A_bc[:], in0=A_bc[:], scalar1=-1.0, scalar2=None, op0=ALU.mult), reads=[r_k], writes=[r_k])
        P.op("dve", lambda e: e.tensor_scalar(out=nbf[:], in0=cvec[:, CV_BF:CV_BF + 1], scalar1=-1.0, scalar2=None, op0=ALU.mult), reads=[r_c], writes=[r_k])
        P.op("dve", lambda e: e.tensor_copy(out=wdt[:].rearrange("p a b -> p (a b)"), in_=wdt32[:]), reads=[r_c2], writes=[r_k])
        CR = [r_c, r_k, r_c1]

        r_wbf = [Res(f"wbf{i}") for i in range(NT)]
        s_pre = [P.slot(f"pre{i}") for i in range(8)]
        for t in range(NT):
            P.dma("pool", lambda e, t=t: e.dma_start(out=wbf_d[t], in_=wst_d[t]), s_pre[(t // 8) % 8], writes=[r_wbf[t]])
        for t in range(NT):
            last = min(NT - 1, (t // 8) * 8 + 7)
            r_wbf[t].w = r_wbf[last].w
        NS = 5
        ring = [sb(f"ring{i}", [128, 1024], BF16) for i in range(NS)]
        r_ring = [Res(f"ring{i}") for i in range(NS)]
        s_ring = [P.slot(f"ring{i}") for i in range(NS)]
        wctr = [0]

        def wtile():
            n = wctr[0]; wctr[0] += 1
            t = n % NT
            s = n % NS
            P.dma("sp", lambda e: e.dma_start(out=ring[s][:], in_=wbf_d[t]), s_ring[s], reads=[r_wbf[t]], writes=[r_ring[s]])
            return ring[s], r_ring[s]

        pAB = pst("pAB", [128, 1024]); pCD = pst("pCD", [128, 1024]); pEF = pst("pEF", [128, 1024])
        pTB = pst("pTB", [128, 2048], BF16)
        bank = {"A": pAB[:, 0:512], "B": pAB[:, 512:1024], "C": pCD[:, 0:512], "D": pCD[:, 512:1024],
                "E": pEF[:, 0:512], "F": pEF[:, 512:1024]}
        r_b = {k: Res("bank" + k) for k in "ABCDEF"}
        r_tb = Res("pTB")

        hT = sb("hT", [128, 8, T]); r_hT = Res("hT")
        xn = sb("xn", [128, 8, T], BF16); r_xn = Res("xn")
        arena = sb("arena", [128, NF, T], BF16)
        r_hid = [Res(f"hid{j}") for j in range(NF)]
        xc = sb("xc", [128, 24, T], BF16); r_xc = [Res(f"xc{c}") for c in range(24)]
        qT = sb("qT", [128, 8, T], BF16); r_qT = Res("qT")
        kTs = sb("kTs", [128, 2, SEQ], BF16); r_kT = Res("kT")
        Vs = sb("Vs", [128, NTIL, 4, 65], BF16); r_V = Res("V")
        cks = sb("cks", [128, NTIL, 16]); r_ck = Res("ck")
        biasb = sb("biasb", [128, NTIL, 16]); r_bias = Res("bias")
        yT = sb("yT", [128, 16, T], BF16); r_yT = [Res(f"yT{c}") for c in range(16)]
        oT = xc
        r_oT = Res("oT")
        merged = qT
        r_mg = Res("merged")
        hst = sb("hst", [128, 2048]); r_hst = Res("hst")
        stg = [sb(f"stg{i}", [128, T]) for i in range(3)]; r_stg = [Res(f"stg{i}") for i in range(3)]
        sqb = sb("sqb", [128, 4, T], BF16); r_sq = [Res(f"sq{i}") for i in range(4)]
        rstd = sb("rstd", [128, T]); r_rstd = Res("rstd")
        cstage = [sb(f"cst{i}", [128, T + 3 * NSEQ_S]) for i in range(2)]; r_cst = [Res(f"cst{i}") for i in range(2)]
        cacc = [sb(f"cacc{i}", [128, T]) for i in range(2)]; r_cacc = [Res(f"cacc{i}") for i in range(2)]
        ccar = sb("ccar", [128, 24, 3 * NSEQ_S]); r_ccar = [Res(f"ccar{c}") for c in range(24)]
        lfT = sb("lfT", [16, T]); r_lf = Res("lfT")
        cT = sb("cT", [16, T]); r_cT = Res("cT")
        ccarry = sb("ccarry", [16, 1]); r_cc = Res("ccarry")
        ones16 = sb("ones16", [16, T])
        dgl = sb("dgl", [16, 16]); r_dgl = Res("dgl")
        cref = sb("cref", [128, 16]); r_cref = Res("cref")
        dtt = sb("dtt", [128, 32]); at = sb("at", [128, 32]); acs = sb("acs", [128, 32]); tot = sb("tot", [128, 32])
        eacs = sb("eacs", [128, 32]); dte = sb("dte", [128, 32]); cdec = sb("cdec", [128, 32])
        r_ss = Res("ssdsmall")
        Dg = sb("Dg", [128, 8, 128]); r_Dg = Res("Dg")
        cbm = sb("cbm", [128, 128]); r_cbm = Res("cbm")
        tmpf = sb("tmpf", [128, 512]); r_tmpf = Res("tmpf")
        def asl(a, b):
            return arena[:, a:b, :].rearrange("p a b -> p (a b)")
        xdt = asl(0, 4); xw = asl(4, 8); ytok = asl(8, 12); hbf = asl(12, 16)
        Btok = asl(16, 17); MT = asl(17, 19); MTb = asl(19, 21); PT = asl(21, 22)
        r_xdt, r_xw, r_ytok, r_hbf, r_Btok, r_MT, r_MTb, r_PT = (Res(n) for n in ("xdt", "xw", "ytok", "hbf", "Btok", "MT", "MTb", "PT"))
        sTt = sb("sTt", [128, 512]); r_sTt = Res("sTt")
        bcs = stg[2][0:64, :]; r_bcs = r_stg[2]
        rec = tmpf; r_rec = r_tmpf

        s_in = P.slot("xin"); s_o = [P.slot(f"out{i}") for i in range(6)]
        P.op("pool", lambda e: e.memset(Vs[:].rearrange("p a b c -> p (a b c)"), 1.0), writes=[r_V])
        P.op("pool", lambda e: e.memset(ones16[:], 1.0), writes=[r_k])

        def mm(out, lhsT, rhs, start, stop, reads, writes):
            P.op("pe", lambda e: e.matmul(out, lhsT=lhsT, rhs=rhs, start=start, stop=stop), reads=reads, writes=writes)

        def rms_rstd(ps_ms, r_ps, n):
            P.op("act", lambda e: e.activation(out=rstd[:, :n], in_=ps_ms, func=AF.Ln, bias=epsc[:], scale=1.0), reads=[r_ps] + CR, writes=[r_rstd])
            P.op("act", lambda e: e.activation(out=rstd[:, :n], in_=rstd[:, :n], func=AF.Exp, scale=-0.5), reads=[r_rstd], writes=[r_rstd])

        def norm_to_xn(n, cvo):
            for c in range(8):
                P.op("act", lambda e, c=c: e.activation(out=sqb[:, c % 4, :n], in_=hT[:, c, :n], func=AF.Square), reads=[r_hT], writes=[r_sq[c % 4]])
                mm(bank["E"][:, :n], ones_d[:], sqb[:, c % 4, :n], c == 0, c == 7, [r_sq[c % 4]] + CR, [r_b["E"]])
            rms_rstd(bank["E"][:, :n], r_b["E"], n)
            for c in range(8):
                P.op("dve", lambda e, c=c: e.scalar_tensor_tensor(out=xn[:, c, :n], in0=hT[:, c, :n], scalar=cvec[:, cvo + c:cvo + c + 1], in1=rstd[:, :n], op0=ALU.mult, op1=ALU.mult),
                     reads=[r_hT, r_rstd] + CR, writes=[r_xn])

        def proj8(bk, n, w, rw, src=None, rsrc=None):
            for c in range(8):
                mm(bank[bk][:, :n], w[:, c * 128:(c + 1) * 128], xn[:, c, :n], c == 0, c == 7, [rw, r_xn], [r_b[bk]])

        def ffn(n, final_out=None):
            for j in range(NF):
                wg, rg = wtile(); wu, ru = wtile()
                pg, pu = ("A", "B") if j % 2 == 0 else ("C", "D")
                proj8(pg, n, wg, rg); proj8(pu, n, wu, ru)
                si = j % 2
                P.op("act", lambda e, pg=pg, si=si: e.activation(out=stg[si][:, :n], in_=bank[pg][:, :n], func=AF.Silu), reads=[r_b[pg]], writes=[r_stg[si]])
                P.op("dve", lambda e, pu=pu, si=si, j=j: e.tensor_tensor(out=arena[:, j, :n], in0=bank[pu][:, :n], in1=stg[si][:, :n], op=ALU.mult),
                     reads=[r_b[pu], r_stg[si]], writes=[r_hid[j]])
            for m in range(8):
                tl = [wtile() for _ in range(3)]
                pb = "AB"[m % 2]
                for kc in range(NF):
                    w, r = tl[kc // 8]
                    mm(bank[pb][:, :n], w[:, (kc % 8) * 128:(kc % 8 + 1) * 128], arena[:, kc, :n], kc == 0, kc == NF - 1, [r, r_hid[kc]], [r_b[pb]])
                P.op("dve", lambda e, m=m, pb=pb: e.scalar_tensor_tensor(out=hT[:, m, :n], in0=bank[pb][:, :n], scalar=0.5, in1=hT[:, m, :n], op0=ALU.mult, op1=ALU.add),
                     reads=[r_b[pb], r_hT], writes=[r_hT])

        def qknorm(bk, n, wcol, scale, out_bf, r_out, out_f32=None, r_f32=None, si=0, f32_view=None):
            P.op("act", lambda e: e.activation(out=stg[si][:, :n], in_=bank[bk][:, :n], func=AF.Copy), reads=[r_b[bk]], writes=[r_stg[si]])
            P.op("act", lambda e: e.activation(out=sqb[:, si, :n], in_=stg[si][:, :n], func=AF.Square), reads=[r_stg[si]], writes=[r_sq[si]])
            pb = "EF"[si]
            mm(bank[pb][:, :n], bd64[:], sqb[:, si, :n], True, True, [r_sq[si]] + CR, [r_b[pb]])
            rms_rstd(bank[pb][:, :n], r_b[pb], n)
            if out_f32 is not None:
                P.op("dve", lambda e: e.scalar_tensor_tensor(out=out_f32, in0=stg[si][:, :n], scalar=cvec[:, wcol:wcol + 1], in1=rstd[:, :n], op0=ALU.mult, op1=ALU.mult),
                     reads=[r_stg[si], r_rstd] + CR, writes=[r_f32])
                P.op("act", lambda e: e.activation(out=out_bf, in_=(out_f32 if f32_view is None else f32_view), func=AF.Copy, scale=scale), reads=[r_f32], writes=[r_out])
            else:
                P.op("dve", lambda e: e.scalar_tensor_tensor(out=stg[si][:, :n], in0=stg[si][:, :n], scalar=cvec[:, wcol:wcol + 1], in1=rstd[:, :n], op0=ALU.mult, op1=ALU.mult),
                     reads=[r_stg[si], r_rstd] + CR, writes=[r_stg[si]])
                P.op("act", lambda e: e.activation(out=out_bf, in_=stg[si][:, :n], func=AF.Copy, scale=scale), reads=[r_stg[si]], writes=[r_out])

        kvout = sb("kvout", [128, 2, T]); r_kvo = [Res(f"kvo{i}") for i in range(2)]

        def block(n, segs, sample, b):
            tb = b * T
            nq = 3 * len(set(s[2] for s in segs)) if sample else 3
            xsrc = xsT_d if sample else xT_d[:, tb:tb + n]
            P.dma("sp", lambda e: e.dma_start(out=hT[:, :, :n], in_=xsrc.rearrange("(c p) t -> p c t", p=128)), None, writes=[r_hT])
            import os as _os
            SSTOP = int(_os.environ.get("K_SSTOP", "99")) if sample else 99
            if sample:
                for si_ in range(NSEQ_S):
                    P.dma("sp", lambda e, si_=si_: e.dma_start(out=ccar[:, :, 3 * si_:3 * si_ + 3], in_=conv0_d[si_].rearrange("p (c l) -> p c l", l=3)), None, writes=r_ccar)
            if SSTOP <= 0:
                return
            norm_to_xn(n, CV_N1)
            ffn(n)
            if SSTOP <= 1:
                return
            norm_to_xn(n, CV_NM)
            ko, vo, lo = (ksT_o, vsT_o, lfsT_o) if sample else (kT_o[:, tb:tb + n], vT_o[:, tb:tb + n], lfT_o[:, tb:tb + n])
            kdst = kTs[:, :, SEQ - TS:SEQ] if False else None
            for c in range(2):
                w, rw = wtile(); bk = "AB"[c]
                proj8(bk, n, w, rw)
                kb = (kTs[:, c, tb:tb + n] if not sample else ksb[:, c, :, 0:LS])
                kvo_v = kvout[:, c, :n] if not sample else kvout[:, c, :n].rearrange("p (s l) -> p s l", s=NSEQ_S)
                qknorm(bk, n, CV_KN, 1.0, kb, r_kT, out_f32=kvout[:, c, :n], r_f32=r_kvo[c], si=c, f32_view=(kvo_v if sample else None))
                P.dma("sp", lambda e, c=c: e.dma_start(out=ko[c * 128:(c + 1) * 128, :], in_=kvout[:, c, :n]), None, reads=[r_kvo[c]])
            for c in range(2):
                w, rw = wtile(); bk = "AB"[c]
                proj8(bk, n, w, rw)
                P.op("act", lambda e, c=c, bk=bk: e.activation(out=kvout[:, c, :n], in_=bank[bk][:, :n], func=AF.Copy), reads=[r_b[bk]], writes=[r_kvo[c]])
                P.dma("sp", lambda e, c=c: e.dma_start(out=vo[c * 128:(c + 1) * 128, :], in_=kvout[:, c, :n]), None, reads=[r_kvo[c]])
                P.op("dve", lambda e, c=c: e.tensor_copy(out=sqb[:, c, :n], in_=kvout[:, c, :n]), reads=[r_kvo[c]], writes=[r_sq[c]])
                for (off, L, sq_) in segs:
                    P.op("pe", lambda e, c=c, off=off, L=L: e.transpose(pTB[:L, 0:128], sqb[:, c, off:off + L], ident[:]), reads=[r_sq[c]] + CR, writes=[r_tb])
                    if sample:
                        vdst = Vsm[:L, sq_, 2 * c:2 * c + 2, 0:64]
                    else:
                        vdst = Vs[:L, (tb + off) // 128, 2 * c:2 * c + 2, 0:64]
                    P.op("act", lambda e, L=L, vdst=vdst: e.activation(out=vdst, in_=pTB[:L, 0:128].rearrange("p (a b) -> p a b", a=2), func=AF.Copy), reads=[r_tb], writes=[r_V])
            w, rw = wtile()
            proj8("A", n, w, rw)
            P.op("act", lambda e: e.activation(out=lfT[:, :n], in_=bank["A"][:16, :n], func=AF.Exp, bias=nbf[:16, :], scale=-1.0), reads=[r_b["A"]] + CR, writes=[r_lf])
            P.op("act", lambda e: e.activation(out=lfT[:, :n], in_=lfT[:, :n], func=AF.Ln, bias=onec[:16, :], scale=1.0), reads=[r_lf], writes=[r_lf])
            P.op("dve", lambda e: e.tensor_scalar(out=lfT[:, :n], in0=lfT[:, :n], scalar1=-1.0, scalar2=None, op0=ALU.mult), reads=[r_lf], writes=[r_lf])
            P.dma("sp", lambda e: e.dma_start(out=lo, in_=lfT[:, :n]), None, reads=[r_lf])
            if not sample:
                if b == 0:
                    P.op("dve", lambda e: e.memset(ccarry[:], 0.0), writes=[r_cc])
                P.op("dve", lambda e: e.tensor_tensor_scan(out=cT[:, :n], data0=ones16[:, :n], data1=lfT[:, :n], initial=ccarry[:, 0:1], op0=ALU.mult, op1=ALU.add),
                     reads=[r_lf, r_cc], writes=[r_cT])
                P.op("dve", lambda e: e.tensor_copy(out=ccarry[:], in_=cT[:, n - 1:n]), reads=[r_cT], writes=[r_cc])
                for (off, L, sq_) in segs:
                    ti = (tb + off) // 128
                    mm(bank["B"][:L, 0:16], cT[:, off:off + L], identf[:16, :16], True, True, [r_cT] + CR, [r_b["B"]])
                    P.op("dve", lambda e, ti=ti, L=L: e.tensor_copy(out=cks[:L, ti, :], in_=bank["B"][:L, 0:16]), reads=[r_b["B"]], writes=[r_ck])
            for c in range(8):
                w, rw = wtile(); bk = "AB"[c % 2]
                proj8(bk, n, w, rw)
                qknorm(bk, n, CV_QN, 0.125, qT[:, c, :n], r_qT, si=c % 2)
            for c in range(24):
                w, rw = wtile(); bk = "AB"[c % 2]; ci = c % 2
                proj8(bk, n, w, rw)
                cs = cstage[ci]
                nsq = len(segs) if sample else 1
                Ls = n // nsq
                csv = cs[:, :nsq * (Ls + 3)].rearrange("p (s l) -> p s l", s=nsq)
                if sample:
                    P.op("pool", lambda e, c=c, csv=csv, nsq=nsq: e.tensor_copy(out=csv[:, :, 0:3], in_=ccar[:, c, :3 * nsq].rearrange("p (s l) -> p s l", s=nsq)), reads=[r_ccar[c]], writes=[r_cst[ci]])
                elif b == 0:
                    P.op("pool", lambda e, csv=csv: e.memset(csv[:, :, 0:3], 0.0), writes=[r_cst[ci]])
                else:
                    P.op("pool", lambda e, c=c, csv=csv: e.tensor_copy(out=csv[:, 0, 0:3], in_=ccar[:, c, 0:3]), reads=[r_ccar[c]], writes=[r_cst[ci]])
                P.op("act", lambda e, bk=bk, csv=csv, nsq=nsq, Ls=Ls: e.activation(out=csv[:, :, 3:3 + Ls], in_=bank[bk][:, :n].rearrange("p (s l) -> p s l", s=nsq), func=AF.Copy),
                     reads=[r_b[bk]], writes=[r_cst[ci]])
                P.op("pool", lambda e, c=c, csv=csv, nsq=nsq, Ls=Ls: e.tensor_copy(out=ccar[:, c, :3 * nsq].rearrange("p (s l) -> p s l", s=nsq), in_=csv[:, :, Ls:Ls + 3]),
                     reads=[r_cst[ci]], writes=[r_ccar[c]])
                ca = cacc[ci][:, :n].rearrange("p (s l) -> p s l", s=nsq)
                wc = CV_CW + 4 * c
                P.op("dve", lambda e, c=c, ca=ca, csv=csv, Ls=Ls, wc=wc: e.tensor_scalar(out=ca, in0=csv[:, :, 3:3 + Ls], scalar1=cvec[:, wc + 3:wc + 4], scalar2=cvec[:, CV_CB + c:CV_CB + c + 1], op0=ALU.mult, op1=ALU.add),
                     reads=[r_cst[ci]] + CR, writes=[r_cacc[ci]])
                for j in range(3):
                    P.op("dve", lambda e, j=j, ca=ca, csv=csv, Ls=Ls, wc=wc: e.scalar_tensor_tensor(out=ca, in0=csv[:, :, j:j + Ls], scalar=cvec[:, wc + j:wc + j + 1], in1=ca, op0=ALU.mult, op1=ALU.add),
                         reads=[r_cst[ci], r_cacc[ci]] + CR, writes=[r_cacc[ci]])
                P.op("act", lambda e, c=c, ci=ci: e.activation(out=xc[:, c, :n], in_=cacc[ci][:, :n], func=AF.Silu), reads=[r_cacc[ci]], writes=[r_xc[c]])
            if sample:
                for si_ in range(NSEQ_S):
                    P.dma("sp", lambda e, si_=si_: e.dma_start(out=convs_o[si_].rearrange("p (c l) -> p c l", l=3), in_=ccar[:, :, 3 * si_:3 * si_ + 3]), None, reads=r_ccar)
            elif b == NB - 1:
                P.dma("sp", lambda e: e.dma_start(out=conv_o.rearrange("p (c l) -> p c l", l=3), in_=ccar[:, :, 0:3]), None, reads=r_ccar)

            if SSTOP <= 2:
                return
            for (off, L, sq_) in segs:
                first = (b == 0 and off == 0) if not sample else True
                if sample:
                    P.dma("sp", lambda e, sq_=sq_: e.dma_start(out=hst[:], in_=ssm0_d[sq_]), None, writes=[r_hst])
                elif first:
                    P.op("pool", lambda e: e.memset(hst[:], 0.0), writes=[r_hst])
                P.op("act", lambda e: e.activation(out=hbf, in_=hst[:], func=AF.Copy), reads=[r_hst], writes=[r_hbf])
                for c in range(8):
                    mm(bank["E"][:L, 0:32], xn[:, c, off:off + L], wdt[:, c, :], c == 0, c == 7, [r_xn] + CR, [r_b["E"]])
                P.op("dve", lambda e, L=L: e.tensor_tensor(out=dtt[:L, :], in0=bank["E"][:L, 0:32], in1=rowc[:L, 0:32], op=ALU.add), reads=[r_b["E"]] + CR, writes=[r_ss])
                P.op("act", lambda e, L=L: e.activation(out=dtt[:L, :], in_=dtt[:L, :], func=AF.Exp), reads=[r_ss], writes=[r_ss])
                P.op("act", lambda e, L=L: e.activation(out=dtt[:L, :], in_=dtt[:L, :], func=AF.Ln, bias=onec[:L, :], scale=1.0), reads=[r_ss] + CR, writes=[r_ss])
                P.op("dve", lambda e, L=L: e.tensor_tensor(out=at[:L, :], in0=dtt[:L, :], in1=A_bc[:L, :], op=ALU.mult), reads=[r_ss] + CR, writes=[r_ss])
                mm(bank["F"][:L, 0:32], tri_f[:L, :L], at[:L, :], True, True, [r_ss] + CR, [r_b["F"]])
                mm(bank["E"][:, 32:64], onesf[:L, :], at[:L, :], True, True, [r_ss] + CR, [r_b["E"]])
                P.op("dve", lambda e, L=L: e.tensor_copy(out=acs[:L, :], in_=bank["F"][:L, 0:32]), reads=[r_b["F"]], writes=[r_ss])
                P.op("dve", lambda e: e.tensor_copy(out=tot[:], in_=bank["E"][:, 32:64]), reads=[r_b["E"]], writes=[r_ss])
                P.op("dve", lambda e, L=L: e.tensor_tensor(out=dte[:L, :], in0=tot[:L, :], in1=acs[:L, :], op=ALU.subtract), reads=[r_ss], writes=[r_ss])
                P.op("act", lambda e, L=L: e.activation(out=dte[:L, :], in_=dte[:L, :], func=AF.Exp), reads=[r_ss], writes=[r_ss])
                P.op("act", lambda e, L=L: e.activation(out=eacs[:L, :], in_=acs[:L, :], func=AF.Exp), reads=[r_ss], writes=[r_ss])
                P.op("act", lambda e: e.activation(out=cdec[:], in_=tot[:], func=AF.Exp), reads=[r_ss], writes=[r_ss])
                for c in range(16):
                    P.op("pe", lambda e, c=c, off=off, L=L: e.transpose(pTB[:L, c * 128:(c + 1) * 128], xc[:, c, off:off + L], ident[:]), reads=[r_xc[c]] + CR, writes=[r_tb])
                x3 = pTB[:L, :].rearrange("p (h d) -> p h d", d=64)
                P.op("dve", lambda e, L=L, x3=x3: e.tensor_tensor(out=xdt[:L, :].rearrange("p (h d) -> p h d", d=64), in0=x3, in1=dtt[:L, :].unsqueeze(2).to_broadcast([L, 32, 64]), op=ALU.mult),
                     reads=[r_tb, r_ss], writes=[r_xdt])
                P.op("dve", lambda e, L=L: e.tensor_tensor(out=xw[:L, :].rearrange("p (h d) -> p h d", d=64), in0=xdt[:L, :].rearrange("p (h d) -> p h d", d=64), in1=dte[:L, :].unsqueeze(2).to_broadcast([L, 32, 64]), op=ALU.mult),
                     reads=[r_xdt, r_ss], writes=[r_xw])
                for g in range(4):
                    P.op("pe", lambda e, g=g, off=off, L=L: e.transpose(pTB[:L, g * 128:(g + 1) * 128], xc[:, 16 + g, off:off + L], ident[:]), reads=[r_xc[16 + g], r_xdt] + CR, writes=[r_tb])
                P.op("act", lambda e, L=L: e.activation(out=Btok[:L, :], in_=pTB[:L, 0:512], func=AF.Copy), reads=[r_tb], writes=[r_Btok])
                for g in range(4):
                    Bt = xc[:, 16 + g, off:off + L]; Ct = xc[:, 20 + g, off:off + L]
                    rB, rC = r_xc[16 + g], r_xc[20 + g]
                    mm(bank["F"][:L, :L], Bt, Ct, True, True, [rB, rC], [r_b["F"]])
                    P.op("dve", lambda e, L=L: e.tensor_tensor(out=cbm[:L, :L], in0=bank["F"][:L, :L], in1=mask01[:L, :L], op=ALU.mult), reads=[r_b["F"]] + CR, writes=[r_cbm])
                    P.op("pool", lambda e, g=g, L=L: e.tensor_tensor(out=Dg[:L, :, :L], in0=at[:L, g * 8:(g + 1) * 8].unsqueeze(2).to_broadcast([L, 8, L]), in1=tri_f[:L, :L].unsqueeze(1).to_broadcast([L, 8, L]), op=ALU.mult),
                         reads=[r_ss] + CR, writes=[r_Dg])
                    segp = pAB[:L, :].rearrange("p (h l) -> p h l", l=128)
                    if L == 128:
                        for hh in range(2):
                            P.op("pe", lambda e, hh=hh, L=L, segp=segp: e.matmul(segp[:, hh * 4:(hh + 1) * 4, :L], lhsT=ustr_f[:L, :L], rhs=Dg[:L, hh * 4:(hh + 1) * 4, :L], start=True, stop=True),
                                 reads=[r_Dg] + CR, writes=[r_b["AB"[hh]]])
                    else:
                        for hh in range(8):
                            P.op("pe", lambda e, hh=hh, L=L, segp=segp: e.matmul(segp[:, hh, :L], lhsT=ustr_f[:L, :L], rhs=Dg[:L, hh, :L], start=True, stop=True),
                                 reads=[r_Dg] + CR, writes=[r_b["AB"[hh // 4]]])
                    MT3 = MT[:L, :].rearrange("p (h l) -> p h l", l=128)
                    MTb3 = MTb[:L, :].rearrange("p (h l) -> p h l", l=128)
                    P.op("act", lambda e, L=L, segp=segp, MT3=MT3: e.activation(out=MT3[:, :, :L], in_=segp[:, :, :L], func=AF.Exp), reads=[r_b["A"], r_b["B"]], writes=[r_MT])
                    P.op("dve", lambda e, L=L, MT3=MT3, MTb3=MTb3: e.tensor_tensor(out=MTb3[:, :, :L], in0=MT3[:, :, :L], in1=cbm[:L, :L].unsqueeze(1).to_broadcast([L, 8, L]), op=ALU.mult),
                         reads=[r_MT, r_cbm], writes=[r_MTb])
                    for hh in range(8):
                        h = g * 8 + hh
                        mm(bank["C"][:L, hh * 64:(hh + 1) * 64], MTb3[:, hh, :L], xdt[:L, h * 64:(h + 1) * 64], True, True, [r_MTb, r_xdt], [r_b["C"]])
                    mm(bank["D"][:L, :], Ct, hbf[:, g * 512:(g + 1) * 512], True, True, [rC, r_hbf], [r_b["D"]])
                    P.op("dve", lambda e, g=g, L=L: e.tensor_tensor(out=tmpf[:L, :].rearrange("p (h d) -> p h d", d=64), in0=bank["D"][:L, :].rearrange("p (h d) -> p h d", d=64),
                                                                   in1=eacs[:L, g * 8:(g + 1) * 8].unsqueeze(2).to_broadcast([L, 8, 64]), op=ALU.mult),
                         reads=[r_b["D"], r_ss], writes=[r_tmpf])
                    P.op("dve", lambda e, g=g, L=L: e.tensor_tensor(out=ytok[:L, g * 512:(g + 1) * 512], in0=bank["C"][:L, :], in1=tmpf[:L, :], op=ALU.add),
                         reads=[r_b["C"], r_tmpf], writes=[r_ytok])
                    mm(bank["E"][:, :], Btok[:L, g * 128:(g + 1) * 128], xw[:L, g * 512:(g + 1) * 512], True, True, [r_Btok, r_xw], [r_b["E"]])
                    hs3 = hst[:, g * 512:(g + 1) * 512].rearrange("p (h d) -> p h d", d=64)
                    P.op("pool", lambda e, g=g, hs3=hs3: e.tensor_tensor(out=hs3, in0=hs3, in1=cdec[:, g * 8:(g + 1) * 8].unsqueeze(2).to_broadcast([128, 8, 64]), op=ALU.mult),
                         reads=[r_ss, r_hbf], writes=[r_hst])
                    P.op("dve", lambda e, g=g: e.tensor_tensor(out=hst[:, g * 512:(g + 1) * 512], in0=hst[:, g * 512:(g + 1) * 512], in1=bank["E"][:, :], op=ALU.add),
                         reads=[r_b["E"]], writes=[r_hst])
                for c in range(16):
                    P.op("pe", lambda e, c=c, L=L: e.transpose(pTB[:, c * 128:c * 128 + L], ytok[:L, c * 128:(c + 1) * 128], ident[:L, :L]), reads=[r_ytok] + CR, writes=[r_tb])
                    P.op("dve", lambda e, c=c, off=off, L=L: e.scalar_tensor_tensor(out=yT[:, c, off:off + L], in0=xc[:, c, off:off + L], scalar=cvec[:, CV_DS + c:CV_DS + c + 1],
                                                                                 in1=pTB[:, c * 128:c * 128 + L], op0=ALU.mult, op1=ALU.add),
                         reads=[r_tb, r_xc[c]] + CR, writes=[r_yT[c]])
                if sample:
                    P.dma("sp", lambda e, sq_=sq_: e.dma_start(out=ssms_o[sq_], in_=hst[:]), None, reads=[r_hst])
                elif b == NB - 1 and off + L == n:
                    P.dma("sp", lambda e: e.dma_start(out=ssm_o, in_=hst[:]), None, reads=[r_hst])
            if SSTOP <= 3:
                return
            for c in range(16):
                w, rw = wtile(); bk = "AB"[c % 2]; si = c % 2
                proj8(bk, n, w, rw)
                P.op("act", lambda e, bk=bk, si=si: e.activation(out=stg[si][:, :n], in_=bank[bk][:, :n], func=AF.Silu), reads=[r_b[bk]], writes=[r_stg[si]])
                P.op("dve", lambda e, c=c, si=si: e.tensor_tensor(out=yT[:, c, :n], in0=yT[:, c, :n], in1=stg[si][:, :n], op=ALU.mult), reads=[r_stg[si], r_yT[c]], writes=[r_yT[c]])
                P.op("act", lambda e, c=c: e.activation(out=sqb[:, c % 4, :n], in_=yT[:, c, :n], func=AF.Square), reads=[r_yT[c]], writes=[r_sq[c % 4]])
                mm(bank["E"][:, :n], ones_g[:], sqb[:, c % 4, :n], c % 4 == 0, c % 4 == 3, [r_sq[c % 4]] + CR, [r_b["E"]])
                if c % 4 == 3:
                    rms_rstd(bank["E"][:, :n], r_b["E"], n)
                    for cc in range(c - 3, c + 1):
                        P.op("dve", lambda e, cc=cc: e.scalar_tensor_tensor(out=yT[:, cc, :n], in0=yT[:, cc, :n], scalar=cvec[:, CV_SN + cc:CV_SN + cc + 1], in1=rstd[:, :n], op0=ALU.mult, op1=ALU.mult),
                             reads=[r_rstd, r_yT[cc]] + CR, writes=[r_yT[cc]])

            if SSTOP <= 4:
                return
            attention(n, segs, sample, b)

            if SSTOP <= 5:
                return
            oT3 = oT[0:64, 0:16, :]
            for m in range(8):
                w0, r0 = wtile(); w1, r1 = wtile()
                for c in range(16):
                    w, r = (w0, r0) if c < 8 else (w1, r1)
                    mm(bank["A"][:, :n], w[:, (c % 8) * 128:(c % 8 + 1) * 128], yT[:, c, :n], c == 0, c == 15, [r, r_yT[c]], [r_b["A"]])
                wg_, rg_ = wtile()
                proj8("C", n, wg_, rg_)
                P.op("act", lambda e: e.activation(out=stg[0][:, :n], in_=bank["C"][:, :n], func=AF.Sigmoid), reads=[r_b["C"]], writes=[r_stg[0]])
                P.op("dve", lambda e: e.tensor_tensor(out=stg[2][:, :n], in0=bank["A"][:, :n], in1=stg[0][:, :n], op=ALU.mult), reads=[r_b["A"], r_stg[0]], writes=[r_stg[2]])
                w0, r0 = wtile(); w1, r1 = wtile()
                for h in range(16):
                    w, r = (w0, r0) if h < 8 else (w1, r1)
                    mm(bank["B"][:, :n], w[0:64, (h % 8) * 128:(h % 8 + 1) * 128], oT3[:, h, :n], h == 0, h == 15, [r, r_oT], [r_b["B"]])
                wg_, rg_ = wtile()
                proj8("D", n, wg_, rg_)
                P.op("act", lambda e: e.activation(out=stg[1][:, :n], in_=bank["D"][:, :n], func=AF.Sigmoid), reads=[r_b["D"]], writes=[r_stg[1]])
                P.op("dve", lambda e: e.tensor_tensor(out=stg[1][:, :n], in0=bank["B"][:, :n], in1=stg[1][:, :n], op=ALU.mult), reads=[r_b["B"], r_stg[1]], writes=[r_stg[1]])
                P.op("dve", lambda e, m=m: e.tensor_tensor(out=merged[:, m, :n], in0=stg[1][:, :n], in1=stg[2][:, :n], op=ALU.add), reads=[r_stg[1], r_stg[2], r_qT], writes=[r_mg])
            for m in range(8):
                w, rw = wtile(); bk = "AB"[m % 2]
                for c in range(8):
                    mm(bank[bk][:, :n], w[:, c * 128:(c + 1) * 128], merged[:, c, :n], c == 0, c == 7, [rw, r_mg], [r_b[bk]])
                P.op("dve", lambda e, m=m, bk=bk: e.tensor_tensor(out=hT[:, m, :n], in0=hT[:, m, :n], in1=bank[bk][:, :n], op=ALU.add), reads=[r_b[bk], r_hT], writes=[r_hT])
            if SSTOP <= 6:
                return
            norm_to_xn(n, CV_N2)
            ffn(n)
            ydst = ysT_o if sample else yT_o[:, tb:tb + n]
            P.dma("sp", lambda e: e.dma_start(out=ydst.rearrange("(c p) t -> p c t", p=128), in_=hT[:, :, :n]), None, reads=[r_hT])

        def attn_tile(kv, qcols_ap, nqc, kt_ap, nk, v_ap, bias_ap, first, last, diag, acc_bank, rd):
            pass

        sT_bufs = [(sTt, r_sTt), (stg[0], r_stg[0]), (stg[1], r_stg[1])]
        PT_bufs = [(PT, r_PT), (asl(0, 1), Res("PT1")), (asl(1, 2), Res("PT2")), (asl(2, 3), Res("PT3"))]
        att_it = [0]

        def attention(n, segs, sample, b):
            tb = b * T
            for (off, L, sq_) in segs:
                if not sample:
                    qi = (tb + off) // 128
                    P.op("dve", lambda e, off=off, L=L: e.tensor_scalar(out=dgl[:], in0=identf[:16, :16], scalar1=cT[:, off + L - 1:off + L], scalar2=None, op0=ALU.mult), reads=[r_cT] + CR, writes=[r_dgl])
                    mm(bank["F"][:, 0:16], onesf[:16, :], dgl[:], True, True, [r_dgl] + CR, [r_b["F"]])
                    P.op("dve", lambda e: e.tensor_copy(out=cref[:], in_=bank["F"][:, 0:16]), reads=[r_b["F"]], writes=[r_cref])
                    nkt = qi + 1
                    P.op("dve", lambda e, nkt=nkt: e.tensor_tensor(out=biasb[:, :nkt, :], in0=cref[:].unsqueeze(1).to_broadcast([128, nkt, 16]), in1=cks[:, :nkt, :], op=ALU.subtract),
                         reads=[r_cref, r_ck], writes=[r_bias])
                    for kv in range(4):
                        half = kv % 2; cp = kv // 2
                        ob = "CD"[kv % 2]
                        qap = qT[half * 64:(half + 1) * 64, 4 * cp:4 * cp + 4, off:off + L]
                        def emit_st(kt, half=half, cp=cp, qap=qap, kv=kv, L=L, qi=qi):
                            sbk = "AB"[kt % 2]
                            dg = (kt == qi)
                            P.op("pe", lambda e: e.matmul(bank[sbk][:, :4 * L].rearrange("p (g t) -> p g t", g=4), lhsT=kTs[half * 64:(half + 1) * 64, cp, kt * 128:(kt + 1) * 128], rhs=qap, start=True, stop=True),
                                 reads=[r_kT, r_qT], writes=[r_b[sbk]])
                            it = att_it[0]; att_it[0] += 1
                            sTb, r_sTb = sT_bufs[it % 3]
                            PTb, r_PTb = PT_bufs[it % 4]
                            P.op("dve", lambda e: e.scalar_tensor_tensor(out=sTb[:, :4 * L].rearrange("p (g t) -> p g t", g=4), in0=bank[sbk][:, :4 * L].rearrange("p (g t) -> p g t", g=4), scalar=1.0,
                                                                         in1=biasb[:, kt, 4 * kv:4 * kv + 4].unsqueeze(2).to_broadcast([128, 4, L]), op0=ALU.mult, op1=ALU.add),
                                 reads=[r_b[sbk], r_bias], writes=[r_sTb])
                            P.op("act", lambda e: e.activation(out=PTb[:, :4 * L], in_=sTb[:, :4 * L], func=AF.Exp), reads=[r_sTb], writes=[r_PTb])
                            if dg:
                                P.op("pool", lambda e: e.affine_select(out=PTb[:, :4 * L].rearrange("p (g t) -> p g t", g=4), in_=PTb[:, :4 * L].rearrange("p (g t) -> p g t", g=4), pattern=[[0, 4], [1, L]], compare_op=ALU.is_ge, fill=0.0, base=0, channel_multiplier=-1),
                                     reads=[r_PTb], writes=[r_PTb])
                            return PTb, r_PTb
                        pend = emit_st(0)
                        for kt in range(nkt):
                            cur = pend
                            if kt + 1 < nkt:
                                pend = emit_st(kt + 1)
                            mm(bank[ob][0:65, :4 * L], Vs[:, kt, kv, :], cur[0][:, :4 * L], kt == 0, kt == nkt - 1, [r_V, cur[1]], [r_b[ob]])
                        finish_head(kv, ob, off, L, 4 * L)
                else:
                    sample_attention(off, L, sq_)

        def finish_head(kv, ob, off, L, ncol):
            P.op("dve", lambda e: e.reciprocal(out=rec[64:65, :ncol], in_=bank[ob][64:65, :ncol]), reads=[r_b[ob]], writes=[r_rec])
            mm(bank["E"][0:64, :ncol], onesf[64:65, 0:64], rec[64:65, :ncol], True, True, [r_rec] + CR, [r_b["E"]])
            P.op("act", lambda e: e.activation(out=bcs[:, :ncol], in_=bank["E"][0:64, :ncol], func=AF.Copy), reads=[r_b["E"]], writes=[r_bcs])
            P.op("dve", lambda e: e.tensor_tensor(out=oT[0:64, 4 * kv:4 * kv + 4, off:off + L], in0=bank[ob][0:64, :ncol].rearrange("p (g t) -> p g t", g=4), in1=bcs[:, :ncol].rearrange("p (g t) -> p g t", g=4), op=ALU.mult),
                 reads=[r_b[ob], r_bcs] + r_xc, writes=[r_oT])

        if do_sample:
            qS = sb("qS", [128, NSEQ_S, 8, LS], BF16); r_qS = Res("qS")
            ksb = sb("ksb", [128, 2, NSEQ_S, 128], BF16)
            P.op("pool", lambda e: e.memset(ksb[:].rearrange("p a b c -> p (a b c)"), 0.0), writes=[r_kT])
            Vsm = sb("Vsm", [128, NSEQ_S, 4, 65], BF16)
            P.op("pool", lambda e: e.memset(Vsm[:].rearrange("p a b c -> p (a b c)"), 1.0), writes=[r_V])
            ptab = sb("ptab", [128, NSEQ_S * NPG], I32); r_pt = Res("ptab")
            ridx = ptab
            piota = sb("piota", [128, 1], I32)
            P.dma("sp", lambda e: e.dma_start(out=ptab[:], in_=pt_d.partition_broadcast(128)), None, writes=[r_pt])
            P.op("pool", lambda e: e.iota(piota[:], pattern=[[0, 1]], base=0, channel_multiplier=1), writes=[r_pt])
            P.op("pool", lambda e: e.tensor_scalar(out=ridx[:], in0=ptab[:], scalar1=128, scalar2=piota[:, 0:1], op0=ALU.mult, op1=ALU.add), reads=[r_pt], writes=[r_pt])
            import os as _os
            NPB = int(_os.environ.get("K_NPB", "4"))
            r_kpg = [Res(f"kpg{i}") for i in range(2)]; r_vpg = [Res(f"vpg{i}") for i in range(2)]
            r_vraw = [Res(f"vraw{i}") for i in range(2)]; r_lpg = [Res(f"lpg{i}") for i in range(2)]; r_ktp = Res("ktp")
            PGE = NPB * 256; VGE = NPB * 4 * 65
            if 2 * SEQ >= 5 * PGE + 2 * VGE:
                kflat = kTs[:].rearrange("p a b -> p (a b)")
                kpg = [kflat[:, i * PGE:(i + 1) * PGE].rearrange("p (a b) -> p a b", a=NPB) for i in range(2)]
                vraw = [kflat[:, (2 + i) * PGE:(3 + i) * PGE].rearrange("p (a b) -> p a b", a=NPB) for i in range(2)]
                ktp = kflat[:, 4 * PGE:5 * PGE].rearrange("p (a b c) -> p a b c", a=NPB, b=2)
                vpg = [kflat[:, 5 * PGE + i * VGE:5 * PGE + (i + 1) * VGE].rearrange("p (a b c) -> p a b c", a=NPB, b=4) for i in range(2)]
            else:
                kpg = [sb(f"kpg{i}", [128, NPB, 256], BF16) for i in range(2)]
                vraw = [sb(f"vraw{i}", [128, NPB, 256], BF16) for i in range(2)]
                vpg = [sb(f"vpg{i}", [128, NPB, 4, 65], BF16) for i in range(2)]
                ktp = sb("ktp", [128, NPB, 2, 128], BF16)
            lpg = [sb(f"lpg{i}", [128, NPB, 16]) for i in range(2)]
            s_pg = [P.slot(f"pg{i}") for i in range(2)]
            Racc = sb("Racc", [128, 16]); r_R = Res("Racc")
            bpg = sb("bpg", [128, NPB, 16]); r_bpg = Res("bpg")
            lftok = sb("lftok", [128, 16]); r_lftok = Res("lftok")
            oacc = sb("oacc", [128, 16 * LS]); r_oacc = Res("oacc")
            bph = sb("bph", [128, 2, NPB * 2, 4]); r_bph = Res("bph")

        def sample_attention(off, L, sq_):
            nq = 4 * L
            if sq_ == 0:
                for i in range(2):
                    P.op("pool", lambda e, i=i: e.memset(vpg[i][:].rearrange("p a b c -> p (a b c)"), 1.0), writes=[r_vpg[i], r_kT])
                P.op("dve", lambda e: e.tensor_copy(out=qS[:].rearrange("p s c t -> p c s t"), in_=qT[:, :, :TS].rearrange("p c (s t) -> p c s t", s=NSEQ_S)), reads=[r_qT], writes=[r_qS])
            import os as _os
            nb = 0 if _os.environ.get('K_NOPAGES') else NPG // NPB
            P.op("pool", lambda e: e.memset(Racc[:], 0.0), writes=[r_R])
            mm(bank["F"][:L, 0:16], lfT[:, off:off + L], identf[:16, :16], True, True, [r_lf] + CR, [r_b["F"]])
            P.op("dve", lambda e: e.tensor_copy(out=lftok[:L, :], in_=bank["F"][:L, 0:16]), reads=[r_b["F"]], writes=[r_lftok])
            P.op("dve", lambda e: e.tensor_copy(out=Racc[:L, :], in_=bank["F"][:L, 0:16]), reads=[r_b["F"], r_R], writes=[r_R])
            mm(bank["F"][:, 16:32], ustr_f[:L, :], lftok[:L, :], True, True, [r_lftok] + CR, [r_b["F"]])
            P.op("dve", lambda e: e.tensor_copy(out=bpg[:, 0, :], in_=bank["F"][:, 16:32]), reads=[r_b["F"]], writes=[r_bpg])
            import os as _os
            ASTOP = float(_os.environ.get("K_ASTOP", "99"))
            if ASTOP <= 1:
                return
            for kv in range(4):
                half = kv % 2; cp = kv // 2; bk = "AB"[half]
                qap = qS[half * 64:(half + 1) * 64, sq_, 4 * cp:4 * cp + 4, :].rearrange("p c t -> p (c t)")
                P.op("pe", lambda e, half=half, cp=cp, qap=qap, bk=bk: e.matmul(bank[bk][:, cp * nq:(cp + 1) * nq], lhsT=ksb[half * 64:(half + 1) * 64, cp, sq_, :], rhs=qap, start=True, stop=True),
                     reads=[r_kT, r_qS], writes=[r_b[bk]])
            if ASTOP <= 1.2:
                return
            for half in range(2):
                bk = "AB"[half]
                P.op("dve", lambda e, half=half: e.tensor_copy(out=bph[:, half, 0:2, :], in_=bpg[:, 0, :].rearrange("p (c h g) -> p c h g", c=2, h=2)[:, :, half, :]), reads=[r_bpg], writes=[r_bph])
                o3 = sTt[:, half * 2 * nq:(half + 1) * 2 * nq].rearrange("p (a t) -> p a t", t=L)
                i3 = bank[bk][:, :2 * nq].rearrange("p (a t) -> p a t", t=L)
                b3 = bph[:, half, 0:2, :].rearrange("p c g -> p (c g)").unsqueeze(2).to_broadcast([128, 8, L])
                P.op("dve", lambda e, o3=o3, i3=i3, b3=b3: e.scalar_tensor_tensor(out=o3, in0=i3, scalar=1.0, in1=b3, op0=ALU.mult, op1=ALU.add),
                     reads=[r_b[bk], r_bph], writes=[r_sTt])
            if ASTOP <= 1.4:
                return
            P.op("act", lambda e: e.activation(out=PT[:, :4 * nq], in_=sTt[:, :4 * nq], func=AF.Exp), reads=[r_sTt], writes=[r_PT])
            if ASTOP <= 1.6:
                return
            P.op("dve", lambda e: e.tensor_tensor(out=PT[:, :4 * nq].rearrange("p (h t) -> p h t", h=16), in0=PT[:, :4 * nq].rearrange("p (h t) -> p h t", h=16), in1=mask01[:, :L].unsqueeze(1).to_broadcast([128, 16, L]), op=ALU.mult),
                 reads=[r_PT] + CR, writes=[r_PT])
            if ASTOP <= 2:
                return
            for kv in range(4):
                pc = ((kv % 2) * 2 + kv // 2) * nq
                mm(bank["C"][0:65, kv * nq:(kv + 1) * nq], Vsm[:, sq_, kv, :], PT[:, pc:pc + nq], True, True, [r_V, r_PT], [r_b["C"]])
            P.op("dve", lambda e: e.tensor_copy(out=oacc[0:65, :4 * nq], in_=bank["C"][0:65, :4 * nq]), reads=[r_b["C"]], writes=[r_oacc])
            if ASTOP <= 3:
                return
            for bi in range(nb):
                pb = nb - 1 - bi
                i2 = bi % 2
                for pp in range(NPB):
                    pg = pb * NPB + pp
                    col = sq_ * NPG + pg
                    P.dma("pool", lambda e, i2=i2, pp=pp, col=col: e.indirect_dma_start(out=kpg[i2][:, pp, :], out_offset=None, in_=ck_d, in_offset=bass.IndirectOffsetOnAxis(ap=ridx[:, col:col + 1], axis=0)),
                          None, reads=[r_pt], writes=[r_kpg[i2]])
                    P.dma("pool", lambda e, i2=i2, pp=pp, col=col: e.indirect_dma_start(out=vraw[i2][:, pp, :], out_offset=None, in_=cvv_d, in_offset=bass.IndirectOffsetOnAxis(ap=ridx[:, col:col + 1], axis=0)),
                          None, reads=[r_pt], writes=[r_vraw[i2]])
                    P.dma("pool", lambda e, i2=i2, pp=pp, col=col: e.indirect_dma_start(out=lpg[i2][:, pp, :], out_offset=None, in_=clf_d, in_offset=bass.IndirectOffsetOnAxis(ap=ridx[:, col:col + 1], axis=0)),
                          None, reads=[r_pt], writes=[r_lpg[i2]])
                P.op("act", lambda e, i2=i2: e.activation(out=vpg[i2][:, :, :, 0:64], in_=vraw[i2][:].rearrange("p a (b c) -> p a b c", b=4), func=AF.Copy), reads=[r_vraw[i2]], writes=[r_vpg[i2]])
                for pp in range(NPB):
                    for c in range(2):
                        P.op("pe", lambda e, i2=i2, pp=pp, c=c: e.transpose(pTB[:, (pp * 2 + c) * 128:(pp * 2 + c + 1) * 128], kpg[i2][:, pp, c * 128:(c + 1) * 128], ident[:]),
                             reads=[r_kpg[i2]] + CR, writes=[r_tb])
                P.op("act", lambda e: e.activation(out=ktp[:].rearrange("p a b c -> p (a b c)"), in_=pTB[:, 0:NPB * 256], func=AF.Copy), reads=[r_tb], writes=[r_ktp])
                for pp in reversed(range(NPB)):
                    mm(bank["F"][:, 0:16], ustr_f[:], lpg[i2][:, pp, :], True, False, [r_lpg[i2]] + CR, [r_b["F"]])
                    mm(bank["F"][:, 0:16], onesf[:], Racc[:], False, True, [r_R] + CR, [r_b["F"]])
                    P.op("dve", lambda e, pp=pp: e.tensor_copy(out=bpg[:, pp, :], in_=bank["F"][:, 0:16]), reads=[r_b["F"]], writes=[r_bpg])
                    P.op("dve", lambda e, pp=pp, i2=i2: e.tensor_tensor(out=Racc[:], in0=Racc[:], in1=lpg[i2][:, pp, :], op=ALU.add), reads=[r_lpg[i2], r_R], writes=[r_R])
                for pp in range(NPB):
                    for kv in range(4):
                        half = kv % 2; cp = kv // 2; bk = "AB"[half]
                        qap = qS[half * 64:(half + 1) * 64, sq_, 4 * cp:4 * cp + 4, :].rearrange("p c t -> p (c t)")
                        c0 = (pp * 2 + cp) * nq
                        P.op("pe", lambda e, half=half, cp=cp, qap=qap, pp=pp, c0=c0, bk=bk: e.matmul(bank[bk][:, c0:c0 + nq], lhsT=ktp[half * 64:(half + 1) * 64, pp, cp, :], rhs=qap, start=True, stop=True),
                             reads=[r_ktp, r_qS], writes=[r_b[bk]])
                tot_c = NPB * 4 * nq
                hc = NPB * 2 * nq
                for half in range(2):
                    bk = "AB"[half]
                    P.op("dve", lambda e, half=half: e.tensor_copy(out=bph[:, half, :, :], in_=bpg[:].rearrange("p n (c h g) -> p (n c) h g", c=2, h=2)[:, :, half, :]), reads=[r_bpg], writes=[r_bph])
                    o3 = sTt[:, half * hc:(half + 1) * hc].rearrange("p (a t) -> p a t", t=L)
                    i3 = bank[bk][:, :hc].rearrange("p (a t) -> p a t", t=L)
                    b3 = bph[:, half, :, :].rearrange("p a g -> p (a g)").unsqueeze(2).to_broadcast([128, NPB * 8, L])
                    P.op("dve", lambda e, o3=o3, i3=i3, b3=b3: e.scalar_tensor_tensor(out=o3, in0=i3, scalar=1.0, in1=b3, op0=ALU.mult, op1=ALU.add),
                         reads=[r_b[bk], r_bph], writes=[r_sTt])
                P.op("act", lambda e: e.activation(out=PT[:, :tot_c], in_=sTt[:, :tot_c], func=AF.Exp), reads=[r_sTt], writes=[r_PT])
                for pp in range(NPB):
                    for kv in range(4):
                        c0 = (pp * 4 + kv) * nq
                        pc = ((kv % 2) * NPB * 2 + pp * 2 + kv // 2) * nq
                        mm(bank["C"][0:65, c0:c0 + nq], vpg[i2][:, pp, kv, :], PT[:, pc:pc + nq], True, True, [r_vpg[i2], r_PT], [r_b["C"]])
                for pp in range(NPB):
                    P.op("dve", lambda e, pp=pp: e.tensor_tensor(out=oacc[0:65, :4 * nq], in0=oacc[0:65, :4 * nq], in1=bank["C"][0:65, pp * 4 * nq:(pp + 1) * 4 * nq], op=ALU.add),
                         reads=[r_b["C"], r_oacc], writes=[r_oacc])
            ncol = 4 * nq
            P.op("dve", lambda e: e.reciprocal(out=rec[64:65, :ncol], in_=oacc[64:65, :ncol]), reads=[r_oacc], writes=[r_rec])
            mm(bank["E"][0:64, :ncol], onesf[64:65, 0:64], rec[64:65, :ncol], True, True, [r_rec] + CR, [r_b["E"]])
            P.op("dve", lambda e: e.tensor_tensor(out=oT[0:64, 0:16, off:off + L], in0=oacc[0:64, :ncol].rearrange("p (h t) -> p h t", h=16), in1=bank["E"][0:64, :ncol].rearrange("p (h t) -> p h t", h=16), op=ALU.mult),
                 reads=[r_b["E"], r_oacc] + r_xc, writes=[r_oT])

        import os as _os
        for b in range(0 if _os.environ.get("K_NOPROMPT") else NB):
            block(T, [(i * 128, 128, 0) for i in range(4)], False, b)
        if do_sample:
            block(TS, [(i * LS, LS, i) for i in range(NSEQ_S)], True, 0)
        print("sbuf bytes remaining", nc.sbuf_bytes_remaining, flush=True)
        n, ns = P.emit(nc)
        print("ops", n, "signals", ns, flush=True)
    return nc


_PROG_CACHE = {}


def kernel(x_prompt, x_sample, cache_k, cache_v, cache_logf, state_ssm, state_conv, page_table,
           ffn1_norm, ffn1_w_gate, ffn1_w_up, ffn1_w_down, mix_norm, w_in, conv_w, conv_b,
           dt_bias, a_log, d_skip, ssd_norm, q_norm, k_norm, b_f, w_ssd_proj, w_attn_proj, w_out,
           ffn2_norm, ffn2_w_gate, ffn2_w_up, ffn2_w_down):
    f = lambda a: np.asarray(a, dtype=np.float32)
    B, SEQ, _ = x_prompt.shape
    DB, DS, _ = x_sample.shape
    NPOOL = cache_k.shape[1]
    NPG = page_table.shape[1]
    p = dict(ffn1_norm=f(ffn1_norm)[0], ffn1_w_gate=f(ffn1_w_gate)[0], ffn1_w_up=f(ffn1_w_up)[0], ffn1_w_down=f(ffn1_w_down)[0],
             mix_norm=f(mix_norm)[0], w_in=f(w_in)[0], conv_w=f(conv_w)[0], conv_b=f(conv_b)[0], dt_bias=f(dt_bias)[0],
             a_log=f(a_log)[0], d_skip=f(d_skip)[0], ssd_norm=f(ssd_norm)[0], q_norm=f(q_norm)[0], k_norm=f(k_norm)[0],
             b_f=f(b_f)[0], w_ssd_proj=f(w_ssd_proj)[0], w_attn_proj=f(w_attn_proj)[0], w_out=f(w_out)[0],
             ffn2_norm=f(ffn2_norm)[0], ffn2_w_gate=f(ffn2_w_gate)[0], ffn2_w_up=f(ffn2_w_up)[0], ffn2_w_down=f(ffn2_w_down)[0])
    wst = build_weight_stream(p)
    cvec = build_cvec(p)
    rowc = np.concatenate([p["dt_bias"], p["a_log"], p["d_skip"]]).reshape(1, 96).astype(np.float32)
    wdt = np.ascontiguousarray(p["w_in"][:, O_DT:O_DT + 32].reshape(8, 128, 32).transpose(1, 0, 2).reshape(128, 256))
    ck2 = np.ascontiguousarray(f(cache_k)[0].reshape(NPOOL * 128, 256))
    cv2 = np.ascontiguousarray(f(cache_v)[0].reshape(NPOOL * 128, 256))
    cl2 = np.ascontiguousarray(f(cache_logf)[0].reshape(NPOOL * 128, 16))
    xp = f(x_prompt); xs = f(x_sample)
    ssm = f(state_ssm)[0]; cst = f(state_conv)[0]
    pt = np.asarray(page_table, dtype=np.int32)
    key = (SEQ, NPG, NPOOL)
    if key not in _PROG_CACHE:
        import os as _os
        _PROG_CACHE[key] = build_program(SEQ, NPG, NPOOL, do_sample=(_os.environ.get('K_NOSAMPLE') is None))
    nc = _PROG_CACHE[key]
    in_maps = []
    for c in range(NCORES):
        sq = slice(NSEQ_S * c, NSEQ_S * (c + 1))
        in_maps.append({
            "xT": np.ascontiguousarray(xp[c % B].T),
            "xsT": np.ascontiguousarray(xs[sq].reshape(NSEQ_S * DS, D).T),
            "wst": wst, "cvec": cvec, "rowc": rowc, "wdt": wdt,
            "ssm0": np.ascontiguousarray(ssm[sq].reshape(NSEQ_S, 2048, 128).transpose(0, 2, 1)),
            "conv0": np.ascontiguousarray(cst[sq].reshape(NSEQ_S, 3, 24, 128).transpose(0, 3, 2, 1).reshape(NSEQ_S, 128, 72)),
            "cache_k": ck2, "cache_v": cv2, "cache_lf": cl2,
            "ptab": np.ascontiguousarray(pt[sq].reshape(1, NSEQ_S * NPG)),
        })
    import os as _os
    if _os.environ.get("K_TRACE"):
        _r = run_bass_kernel_spmd(nc, in_maps, core_ids=list(range(NCORES)), trace=True)
        print("EXEC_NS", _r.exec_time_ns, flush=True)
        res = _r.results
    else:
        res = run_bass_kernel_spmd(nc, in_maps, core_ids=list(range(NCORES))).results
    yp = np.stack([res[b]["yT"].T for b in range(B)])
    kp = np.stack([res[b]["kT"].T.reshape(SEQ, 4, 64) for b in range(B)])[None]
    vp = np.stack([res[b]["vT"].T.reshape(SEQ, 4, 64) for b in range(B)])[None]
    lp = np.stack([res[b]["lfT"].T for b in range(B)])[None]
    sp = np.stack([res[b]["ssm"].T.reshape(32, 64, 128) for b in range(B)])[None]
    cp = np.stack([res[b]["conv"].reshape(128, 24, 3).transpose(2, 1, 0).reshape(3, 3072) for b in range(B)])[None]
    ys = np.concatenate([res[c]["ysT"].T.reshape(NSEQ_S, DS, D) for c in range(NCORES)])
    ks = np.concatenate([res[c]["ksT"].T.reshape(NSEQ_S, DS, 4, 64) for c in range(NCORES)])[None]
    vs = np.concatenate([res[c]["vsT"].T.reshape(NSEQ_S, DS, 4, 64) for c in range(NCORES)])[None]
    ls = np.concatenate([res[c]["lfsT"].T.reshape(NSEQ_S, DS, 16) for c in range(NCORES)])[None]
    ss = np.concatenate([res[c]["ssms"].transpose(0, 2, 1).reshape(NSEQ_S, 32, 64, 128) for c in range(NCORES)])[None]
    cs = np.concatenate([res[c]["convs"].reshape(NSEQ_S, 128, 24, 3).transpose(0, 3, 2, 1).reshape(NSEQ_S, 3, 3072) for c in range(NCORES)])[None]
    o = (yp, ys, kp, vp, lp, sp, cp, ks, vs, ls, ss, cs)
    return tuple(np.ascontiguousarray(a, dtype=np.float32) for a in o)
```

```python
import contextlib
import numpy as np
import concourse.bass as bass
import concourse.mybir as mybir
from concourse.bass_utils import run_bass_kernel_spmd

F32 = mybir.dt.float32
BF16 = mybir.dt.bfloat16
I32 = mybir.dt.int32
AF = mybir.ActivationFunctionType
ALU = mybir.AluOpType

D = 1024
DFF = 2816
NF = 22
NSEQ_S = 4
LS = 8
NCORES = 8


class Res:
    __slots__ = ("name", "w", "r")

    def __init__(self, name=""):
        self.name = name
        self.w = None
        self.r = []


class DmaSlot:
    __slots__ = ("sem", "count", "name")

    def __init__(self, name):
        self.name = name
        self.sem = None
        self.count = 0


ENGS = ("pe", "act", "dve", "pool", "sp")
EPOCH = 30000


class Prog:
    def __init__(self):
        self.ops = {e: [] for e in ENGS}
        self.waited = {e: {} for e in ENGS}
        self.needed = {e: set() for e in ENGS}
        self.slots = []

    def slot(self, name=""):
        s = DmaSlot(name)
        self.slots.append(s)
        return s

    def _deps(self, eng, reads, writes):
        deps = []
        for r in reads:
            if r.w is not None:
                deps.append(r.w)
        for w in writes:
            if w.w is not None:
                deps.append(w.w)
            deps.extend(w.r)
        out = []
        wd = self.waited[eng]
        best = {}
        for d in deps:
            if d[0] == "dma":
                key = ("dma", id(d[1]))
                if key not in best or best[key][2] < d[2]:
                    best[key] = d
            else:
                if d[0] == eng and eng == "pe":
                    continue
                if d[0] not in best or best[d[0]][1] < d[1]:
                    best[d[0]] = d
        for key, d in best.items():
            v = d[2] if d[0] == "dma" else d[1]
            if wd.get(key, 0) >= v:
                continue
            wd[key] = v
            if d[0] != "dma":
                self.needed[d[0]].add(v)
            out.append(d)
        return out

    def op(self, eng, fn, reads=(), writes=()):
        waits = self._deps(eng, reads, writes)
        lst = self.ops[eng]
        lst.append([fn, waits, None, 0])
        tok = (eng, len(lst))
        for r in reads:
            r.r.append(tok)
        for w in writes:
            w.w = tok
            w.r = []
        return tok

    def dma(self, eng, fn, slot, reads=(), writes=()):
        if slot is None:
            key = writes[0] if len(writes) else reads[0]
            if not hasattr(self, "_auto"):
                self._auto = {}
            if id(key) not in self._auto:
                self._auto[id(key)] = self.slot("a" + key.name)
            slot = self._auto[id(key)]
        waits = self._deps(eng, reads, writes)
        slot.count += 16
        self.ops[eng].append([fn, waits, slot, slot.count])
        tok = ("dma", slot, slot.count)
        for r in reads:
            r.r.append(tok)
        for w in writes:
            w.w = tok
            w.r = []
        return tok

    def emit(self, nc, final_engine="sp"):
        with contextlib.ExitStack() as st:
            rank = {}
            nsig = {}
            for e in ENGS:
                flagged = sorted(self.needed[e])
                rank[e] = {idx: i + 1 for i, idx in enumerate(flagged)}
                nsig[e] = len(flagged)
            sems = {}
            for e in ENGS:
                n_ep = max(1, (nsig[e] + EPOCH - 1) // EPOCH)
                sems[e] = [st.enter_context(nc.semaphore(f"s_{e}{k}")) for k in range(n_ep)]
            for s in self.slots:
                if s.count > 0:
                    s.sem = st.enter_context(nc.semaphore(f"d_{s.name}"))
            fin = [("dma", s, s.count) for s in self.slots if s.count > 0]
            self.ops[final_engine].append([None, fin, None, 0])
            block = st.enter_context(nc.Block())

            def run(e):
                def body(eng):
                    for i, (fn, waits, slot, val) in enumerate(self.ops[e]):
                        for d in waits:
                            if d[0] == "dma":
                                eng.wait_ge(d[1].sem, d[2])
                            else:
                                sg = rank[d[0]][d[1]]
                                eng.wait_ge(sems[d[0]][(sg - 1) // EPOCH], (sg - 1) % EPOCH + 1)
                        if fn is None:
                            continue
                        ins = fn(eng)
                        if slot is not None:
                            ins.then_inc(slot.sem, 16)
                        else:
                            sg = rank[e].get(i + 1)
                            if sg is not None:
                                ins.then_inc(sems[e][(sg - 1) // EPOCH], 1)
                return body

            block.tensor(run("pe"))
            block.scalar(run("act"))
            block.vector(run("dve"))
            block.gpsimd(run("pool"))
            block.sync(run("sp"))
        return {e: len(self.ops[e]) for e in ENGS}, nsig


def _kc_tile(w_cols):
    return w_cols.reshape(8, 128, 128).transpose(1, 0, 2).reshape(128, 1024)


def _pad_cols(w, n):
    out = np.zeros((w.shape[0], n), np.float32)
    out[:, : w.shape[1]] = w
    return out


def _ffn_tiles(wg, wu, wd):
    tiles = []
    for j in range(NF):
        tiles.append(_kc_tile(wg[:, j * 128:(j + 1) * 128]))
        tiles.append(_kc_tile(wu[:, j * 128:(j + 1) * 128]))
    wdp = np.concatenate([wd, np.zeros((24 * 128 - DFF, D), np.float32)], 0)
    for m in range(8):
        blk = wdp[:, m * 128:(m + 1) * 128].reshape(24, 128, 128)
        for s in range(3):
            tiles.append(blk[s * 8:(s + 1) * 8].transpose(1, 0, 2).reshape(128, 1024))
    return tiles


O_Z, O_XBC, O_DT, O_Q, O_K, O_V, O_F, O_GS, O_GA = 0, 2048, 5120, 5152, 6176, 6432, 6688, 6704, 7728


def _q_perm_cols():
    cols = []
    for cp in range(2):
        for g in range(4):
            for half in range(2):
                kv = 2 * cp + half
                h = 4 * kv + g
                cols.extend(range(O_Q + h * 64, O_Q + (h + 1) * 64))
    return np.array(cols)


def build_weight_stream(p):
    t = []
    t += _ffn_tiles(p["ffn1_w_gate"], p["ffn1_w_up"], p["ffn1_w_down"])
    w_in = p["w_in"]
    for c in range(2):
        t.append(_kc_tile(w_in[:, O_K + c * 128: O_K + (c + 1) * 128]))
    for c in range(2):
        t.append(_kc_tile(w_in[:, O_V + c * 128: O_V + (c + 1) * 128]))
    t.append(_kc_tile(_pad_cols(w_in[:, O_F:O_F + 16], 128)))
    qc = w_in[:, _q_perm_cols()]
    for c in range(8):
        t.append(_kc_tile(qc[:, c * 128:(c + 1) * 128]))
    for c in range(24):
        t.append(_kc_tile(w_in[:, O_XBC + c * 128: O_XBC + (c + 1) * 128]))
    for c in range(16):
        t.append(_kc_tile(w_in[:, O_Z + c * 128: O_Z + (c + 1) * 128]))
    wsp = p["w_ssd_proj"]
    wap = p["w_attn_proj"]
    for m in range(8):
        blk = wsp[:, m * 128:(m + 1) * 128].reshape(16, 128, 128)
        for s in range(2):
            t.append(blk[s * 8:(s + 1) * 8].transpose(1, 0, 2).reshape(128, 1024))
        t.append(_kc_tile(w_in[:, O_GS + m * 128: O_GS + (m + 1) * 128]))
        ablk = wap[:, m * 128:(m + 1) * 128].reshape(16, 64, 128)
        for s in range(2):
            a = np.zeros((128, 1024), np.float32)
            a[:64] = ablk[s * 8:(s + 1) * 8].transpose(1, 0, 2).reshape(64, 1024)
            t.append(a)
        t.append(_kc_tile(w_in[:, O_GA + m * 128: O_GA + (m + 1) * 128]))
    wo = p["w_out"]
    for m in range(8):
        t.append(_kc_tile(wo[:, m * 128:(m + 1) * 128]))
    t += _ffn_tiles(p["ffn2_w_gate"], p["ffn2_w_up"], p["ffn2_w_down"])
    return np.ascontiguousarray(np.stack(t)).astype(np.float32)


NT = 68 + 37 + 16 + 48 + 8 + 68

CV_N1, CV_NM, CV_N2, CV_SN, CV_CW, CV_CB, CV_QN, CV_KN, CV_DS, CV_BF, CV_NBF = 0, 8, 16, 24, 40, 136, 160, 161, 162, 178, 179
NCV = 180


def build_cvec(p):
    cv = np.zeros((128, NCV), np.float32)
    cv[:, CV_N1:CV_N1 + 8] = p["ffn1_norm"].reshape(8, 128).T
    cv[:, CV_NM:CV_NM + 8] = p["mix_norm"].reshape(8, 128).T
    cv[:, CV_N2:CV_N2 + 8] = p["ffn2_norm"].reshape(8, 128).T
    cv[:, CV_SN:CV_SN + 16] = p["ssd_norm"].reshape(16, 128).T
    cw = p["conv_w"].reshape(4, 24, 128)
    cv[:, CV_CW:CV_CW + 96] = cw.transpose(2, 1, 0).reshape(128, 96)
    cv[:, CV_CB:CV_CB + 24] = p["conv_b"].reshape(24, 128).T
    cv[:, CV_QN] = np.tile(p["q_norm"], 2)
    cv[:, CV_KN] = np.tile(p["k_norm"], 2)
    cv[:, CV_DS:CV_DS + 16] = np.repeat(p["d_skip"], 64).reshape(16, 128).T
    cv[:16, CV_BF] = p["b_f"]
    return cv


def build_program(SEQ, NPG, NPOOL, do_sample=True):
    T = 512
    NB = SEQ // T
    NL = NB // 2
    HALF = (NB - NL) * T
    NTIL = SEQ // 128
    TS = NSEQ_S * LS
    nc = bass.Bass("TRN2", target_bir_lowering=False)

    def din(name, shape, dt=F32):
        return nc.dram_tensor(name, shape, dt, kind="ExternalInput").ap()

    def dout(name, shape, dt=F32):
        return nc.dram_tensor(name, shape, dt, kind="ExternalOutput").ap()

    xT_d = din("xT", [D, SEQ])
    xsT_d = din("xsT", [D, TS])
    wst_d = din("wst", [NT, 128, 1024])
    cvec_d = din("cvec", [128, NCV])
    rowc_d = din("rowc", [1, 96])
    wdt_d = din("wdt", [128, 8 * 32])
    ssm0_d = din("ssm0", [NSEQ_S, 128, 2048])
    conv0_d = din("conv0", [NSEQ_S, 128, 24 * 3])
    ck_d = din("cache_k", [NPOOL * 128, 256])
    cvv_d = din("cache_v", [NPOOL * 128, 256])
    clf_d = din("cache_lf", [NPOOL * 128, 16])
    pt_d = din("ptab", [1, NSEQ_S * NPG], I32)

    flg_d = din("flg", [128, 2])
    yT_o = dout("yT", [D, HALF])
    ysT_o = dout("ysT", [D, TS])
    kT_o = dout("kT", [256, HALF])
    vT_o = dout("vT", [256, HALF])
    lfT_o = dout("lfT", [16, HALF])
    ssm_o = dout("ssm", [128, 2048])
    conv_o = dout("conv", [128, 72])
    ksT_o = dout("ksT", [256, TS])
    vsT_o = dout("vsT", [256, TS])
    lfsT_o = dout("lfsT", [16, TS])
    ssms_o = dout("ssms", [NSEQ_S, 128, 2048])
    convs_o = dout("convs", [NSEQ_S, 128, 72])

    wbf_d = nc.dram_tensor("wbf", [NT, 128, 1024], BF16, kind="Internal").ap()

    P = Prog()
    st = contextlib.ExitStack()
    with st:
        def sb(name, shape, dt=F32):
            return st.enter_context(nc.sbuf_tensor("sb_" + name, shape, dt))

        def pst(name, shape, dt=F32):
            return st.enter_context(nc.psum_tensor("ps_" + name, shape, dt))

        cvec = sb("cvec", [128, NCV]); r_c = Res("const")
        rowc = sb("rowc", [128, 96])
        wdt32 = sb("wdt32", [128, 256]); wdt = sb("wdt", [128, 8, 32], BF16)
        ident = sb("ident", [128, 128], BF16)
        identf = sb("identf", [128, 128], F32)
        ones_d = sb("ones_d", [128, 128], BF16)
        ones_g = sb("ones_g", [128, 128], BF16)
        bd64 = sb("bd64", [128, 128], BF16)
        onesf = sb("onesf", [128, 128], F32)
        tri_f = sb("tri_f", [128, 128], F32)
        ustr_f = sb("ustr_f", [128, 128], F32)
        mask01 = sb("mask01", [128, 128], F32)
        epsc = sb("epsc", [128, 1]); onec = sb("onec", [128, 1])
        A_bc = sb("A_bc", [128, 32]); nbf = sb("nbf", [128, 1])
        s_c = P.slot("const")
        r_c1 = Res("c1"); r_c2 = Res("c2")
        flg = sb("flg", [128, 2])
        P.dma("sp", lambda e: e.dma_start(out=flg[:], in_=flg_d), None, writes=[r_c])
        P.dma("sp", lambda e: e.dma_start(out=cvec[:], in_=cvec_d), None, writes=[r_c])
        P.dma("sp", lambda e: e.dma_start(out=rowc[:], in_=rowc_d.partition_broadcast(128)), None, writes=[r_c1])
        P.dma("sp", lambda e: e.dma_start(out=wdt32[:], in_=wdt_d), None, writes=[r_c2])
        r_k = Res("consts2")
        P.op("pool", lambda e: e.memset(identf[:], 1.0), writes=[r_k])
        P.op("pool", lambda e: e.affine_select(out=identf[:], in_=identf[:], pattern=[[-1, 128]], compare_op=ALU.is_equal, fill=0.0, base=0, channel_multiplier=1), writes=[r_k])
        P.op("pool", lambda e: e.tensor_copy(out=ident[:], in_=identf[:]), writes=[r_k])
        P.op("pool", lambda e: e.memset(onesf[:], 1.0), writes=[r_k])
        P.op("pool", lambda e: e.memset(ones_d[:], 1.0 / 1024), writes=[r_k])
        P.op("pool", lambda e: e.memset(ones_g[:], 1.0 / 512), writes=[r_k])
        P.op("pool", lambda e: e.memset(bd64[:], 0.0), writes=[r_k])
        P.op("pool", lambda e: e.memset(bd64[0:64, 0:64], 1.0 / 64), writes=[r_k])
        P.op("pool", lambda e: e.memset(bd64[64:128, 64:128], 1.0 / 64), writes=[r_k])
        P.op("pool", lambda e: e.affine_select(out=tri_f[:], in_=onesf[:], pattern=[[1, 128]], compare_op=ALU.is_ge, fill=0.0, base=0, channel_multiplier=-1), writes=[r_k])
        P.op("pool", lambda e: e.tensor_copy(out=mask01[:], in_=tri_f[:]), writes=[r_k])
        P.op("pool", lambda e: e.affine_select(out=ustr_f[:], in_=onesf[:], pattern=[[-1, 128]], compare_op=ALU.is_gt, fill=0.0, base=0, channel_multiplier=1), writes=[r_k])
        P.op("pool", lambda e: e.memset(epsc[:], 1e-6), writes=[r_k])
        P.op("pool", lambda e: e.memset(onec[:], 1.0), writes=[r_k])
        P.op("act", lambda e: e.activation(out=A_bc[:], in_=rowc[:, 32:64], func=AF.Exp), reads=[r_c1], writes=[r_k])
        P.op("dve", lambda e: e.tensor_scalar(out=A_bc[:], in0=A_bc[:], scalar1=-1.0, scalar2=None, op0=ALU.mult), reads=[r_k], writes=[r_k])
        P.op("dve", lambda e: e.tensor_scalar(out=nbf[:], in0=cvec[:, CV_BF:CV_BF + 1], scalar1=-1.0, scalar2=None, op0=ALU.mult), reads=[r_c], writes=[r_k])
        P.op("dve", lambda e: e.tensor_copy(out=wdt[:].rearrange("p a b -> p (a b)"), in_=wdt32[:]), reads=[r_c2], writes=[r_k])
        CR = [r_c, r_k, r_c1]

        r_wbf = [Res(f"wbf{i}") for i in range(NT)]
        s_pre = [P.slot(f"pre{i}") for i in range(8)]
        for t in range(NT):
            P.dma("pool", lambda e, t=t: e.dma_start(out=wbf_d[t], in_=wst_d[t]), s_pre[(t // 8) % 8], writes=[r_wbf[t]])
        for t in range(NT):
            last = min(NT - 1, (t // 8) * 8 + 7)
            r_wbf[t].w = r_wbf[last].w
        NS = 5
        ring = [sb(f"ring{i}", [128, 1024], BF16) for i in range(NS)]
        r_ring = [Res(f"ring{i}") for i in range(NS)]
        s_ring = [P.slot(f"ring{i}") for i in range(NS)]
        wctr = [0]

        def wskip(k):
            wctr[0] += k

        def wtile():
            n = wctr[0]; wctr[0] += 1
            t = n % NT
            s = n % NS
            P.dma("sp", lambda e: e.dma_start(out=ring[s][:], in_=wbf_d[t]), s_ring[s], reads=[r_wbf[t]], writes=[r_ring[s]])
            return ring[s], r_ring[s]

        pAB = pst("pAB", [128, 1024]); pCD = pst("pCD", [128, 1024]); pEF = pst("pEF", [128, 1024])
        pTB = pst("pTB", [128, 2048], BF16)
        bank = {"A": pAB[:, 0:512], "B": pAB[:, 512:1024], "C": pCD[:, 0:512], "D": pCD[:, 512:1024],
                "E": pEF[:, 0:512], "F": pEF[:, 512:1024]}
        r_b = {k: Res("bank" + k) for k in "ABCDEF"}
        r_tb = Res("pTB")

        hT = sb("hT", [128, 8, T]); r_hT = Res("hT")
        xn = sb("xn", [128, 8, T], BF16); r_xn = Res("xn")
        arena = sb("arena", [128, NF, T], BF16)
        r_hid = [Res(f"hid{j}") for j in range(NF)]
        xc = sb("xc", [128, 24, T], BF16); r_xc = [Res(f"xc{c}") for c in range(24)]
        qT = sb("qT", [128, 8, T], BF16); r_qT = Res("qT")
        kTs = sb("kTs", [128, 2, SEQ], BF16); r_kT = Res("kT")
        Vs = sb("Vs", [128, NTIL, 4, 65], BF16); r_V = Res("V")
        cks = sb("cks", [128, NTIL, 16]); r_ck = Res("ck")
        biasb = sb("biasb", [128, NTIL, 16]); r_bias = Res("bias")
        yT = sb("yT", [128, 16, T], BF16); r_yT = [Res(f"yT{c}") for c in range(16)]
        oT = xc
        r_oT = Res("oT")
        merged = qT
        r_mg = Res("merged")
        hst = sb("hst", [128, 2048]); r_hst = Res("hst")
        stg = [sb(f"stg{i}", [128, T]) for i in range(3)]; r_stg = [Res(f"stg{i}") for i in range(3)]
        sqb = sb("sqb", [128, 4, T], BF16); r_sq = [Res(f"sq{i}") for i in range(4)]
        rstd = sb("rstd", [128, T]); r_rstd = Res("rstd")
        cstage = [sb(f"cst{i}", [128, T + 3 * NSEQ_S]) for i in range(2)]; r_cst = [Res(f"cst{i}") for i in range(2)]
        cacc = [sb(f"cacc{i}", [128, T]) for i in range(2)]; r_cacc = [Res(f"cacc{i}") for i in range(2)]
        ccar = sb("ccar", [128, 24, 3 * NSEQ_S]); r_ccar = [Res(f"ccar{c}") for c in range(24)]
        lfT = sb("lfT", [16, T]); r_lf = Res("lfT")
        cT = sb("cT", [16, T]); r_cT = Res("cT")
        ccarry = sb("ccarry", [16, 1]); r_cc = Res("ccarry")
        ones16 = sb("ones16", [16, T])
        dgl = sb("dgl", [16, 16]); r_dgl = Res("dgl")
        cref = sb("cref", [128, 16]); r_cref = Res("cref")
        dtt = sb("dtt", [128, 32]); at = sb("at", [128, 32]); acs = sb("acs", [128, 32]); tot = sb("tot", [128, 32])
        eacs = sb("eacs", [128, 32]); dte = sb("dte", [128, 32]); cdec = sb("cdec", [128, 32])
        r_ss = Res("ssdsmall")
        Dg = sb("Dg", [128, 8, 128]); r_Dg = Res("Dg")
        cbm = sb("cbm", [128, 128]); r_cbm = Res("cbm")
        tmpf = sb("tmpf", [128, 512]); r_tmpf = Res("tmpf")
        def asl(a, b):
            return arena[:, a:b, :].rearrange("p a b -> p (a b)")
        xdt = asl(0, 4); xw = asl(4, 8); ytok = asl(8, 12); hbf = asl(12, 16)
        Btok = asl(16, 17); MT = asl(17, 19); MTb = asl(19, 21); PT = asl(21, 22)
        r_xdt, r_xw, r_ytok, r_hbf, r_Btok, r_MT, r_MTb, r_PT = (Res(n) for n in ("xdt", "xw", "ytok", "hbf", "Btok", "MT", "MTb", "PT"))
        sTt = sb("sTt", [128, 512]); r_sTt = Res("sTt")
        bcs = stg[2][0:64, :]; r_bcs = r_stg[2]
        rec = tmpf; r_rec = r_tmpf

        s_in = P.slot("xin"); s_o = [P.slot(f"out{i}") for i in range(6)]
        P.op("pool", lambda e: e.memset(Vs[:].rearrange("p a b c -> p (a b c)"), 1.0), writes=[r_V])
        P.op("pool", lambda e: e.memset(ones16[:], 1.0), writes=[r_k])

        def mm(out, lhsT, rhs, start, stop, reads, writes):
            P.op("pe", lambda e: e.matmul(out, lhsT=lhsT, rhs=rhs, start=start, stop=stop), reads=reads, writes=writes)

        def rms_rstd(ps_ms, r_ps, n):
            P.op("act", lambda e: e.activation(out=rstd[:, :n], in_=ps_ms, func=AF.Ln, bias=epsc[:], scale=1.0), reads=[r_ps] + CR, writes=[r_rstd])
            P.op("act", lambda e: e.activation(out=rstd[:, :n], in_=rstd[:, :n], func=AF.Exp, scale=-0.5), reads=[r_rstd], writes=[r_rstd])

        def norm_to_xn(n, cvo):
            for c in range(8):
                P.op("act", lambda e, c=c: e.activation(out=sqb[:, c % 4, :n], in_=hT[:, c, :n], func=AF.Square), reads=[r_hT], writes=[r_sq[c % 4]])
                mm(bank["E"][:, :n], ones_d[:], sqb[:, c % 4, :n], c == 0, c == 7, [r_sq[c % 4]] + CR, [r_b["E"]])
            rms_rstd(bank["E"][:, :n], r_b["E"], n)
            for c in range(8):
                P.op("dve", lambda e, c=c: e.scalar_tensor_tensor(out=xn[:, c, :n], in0=hT[:, c, :n], scalar=cvec[:, cvo + c:cvo + c + 1], in1=rstd[:, :n], op0=ALU.mult, op1=ALU.mult),
                     reads=[r_hT, r_rstd] + CR, writes=[r_xn])

        def proj8(bk, n, w, rw, src=None, rsrc=None):
            for c in range(8):
                mm(bank[bk][:, :n], w[:, c * 128:(c + 1) * 128], xn[:, c, :n], c == 0, c == 7, [rw, r_xn], [r_b[bk]])

        def ffn(n, final_out=None):
            for j in range(NF):
                wg, rg = wtile(); wu, ru = wtile()
                pg, pu = ("A", "B") if j % 2 == 0 else ("C", "D")
                proj8(pg, n, wg, rg); proj8(pu, n, wu, ru)
                si = j % 2
                P.op("act", lambda e, pg=pg, si=si: e.activation(out=stg[si][:, :n], in_=bank[pg][:, :n], func=AF.Silu), reads=[r_b[pg]], writes=[r_stg[si]])
                P.op("dve", lambda e, pu=pu, si=si, j=j: e.tensor_tensor(out=arena[:, j, :n], in0=bank[pu][:, :n], in1=stg[si][:, :n], op=ALU.mult),
                     reads=[r_b[pu], r_stg[si]], writes=[r_hid[j]])
            for m in range(8):
                tl = [wtile() for _ in range(3)]
                pb = "AB"[m % 2]
                for kc in range(NF):
                    w, r = tl[kc // 8]
                    mm(bank[pb][:, :n], w[:, (kc % 8) * 128:(kc % 8 + 1) * 128], arena[:, kc, :n], kc == 0, kc == NF - 1, [r, r_hid[kc]], [r_b[pb]])
                P.op("dve", lambda e, m=m, pb=pb: e.scalar_tensor_tensor(out=hT[:, m, :n], in0=bank[pb][:, :n], scalar=0.5, in1=hT[:, m, :n], op0=ALU.mult, op1=ALU.add),
                     reads=[r_b[pb], r_hT], writes=[r_hT])

        def qknorm(bk, n, wcol, scale, out_bf, r_out, out_f32=None, r_f32=None, si=0, f32_view=None):
            P.op("act", lambda e: e.activation(out=stg[si][:, :n], in_=bank[bk][:, :n], func=AF.Copy), reads=[r_b[bk]], writes=[r_stg[si]])
            P.op("act", lambda e: e.activation(out=sqb[:, si, :n], in_=stg[si][:, :n], func=AF.Square), reads=[r_stg[si]], writes=[r_sq[si]])
            pb = "EF"[si]
            mm(bank[pb][:, :n], bd64[:], sqb[:, si, :n], True, True, [r_sq[si]] + CR, [r_b[pb]])
            rms_rstd(bank[pb][:, :n], r_b[pb], n)
            if out_f32 is not None:
                P.op("dve", lambda e: e.scalar_tensor_tensor(out=out_f32, in0=stg[si][:, :n], scalar=cvec[:, wcol:wcol + 1], in1=rstd[:, :n], op0=ALU.mult, op1=ALU.mult),
                     reads=[r_stg[si], r_rstd] + CR, writes=[r_f32])
                P.op("act", lambda e: e.activation(out=out_bf, in_=(out_f32 if f32_view is None else f32_view), func=AF.Copy, scale=scale), reads=[r_f32], writes=[r_out])
            else:
                P.op("dve", lambda e: e.scalar_tensor_tensor(out=stg[si][:, :n], in0=stg[si][:, :n], scalar=cvec[:, wcol:wcol + 1], in1=rstd[:, :n], op0=ALU.mult, op1=ALU.mult),
                     reads=[r_stg[si], r_rstd] + CR, writes=[r_stg[si]])
                P.op("act", lambda e: e.activation(out=out_bf, in_=stg[si][:, :n], func=AF.Copy, scale=scale), reads=[r_stg[si]], writes=[r_out])

        kvout = sb("kvout", [128, 2, T]); r_kvo = [Res(f"kvo{i}") for i in range(2)]

        def block(n, segs, sample, b):
            tb = b * T
            light = (not sample) and b < NL
            ob_ = (b - NL) * T
            if (not sample) and b == NL and NL > 0:
                P.op("dve", lambda e: e.tensor_scalar(out=hst[:], in0=hst[:], scalar1=flg[:, 0:1], scalar2=None, op0=ALU.mult), reads=[r_hst] + CR, writes=[r_hst])
                P.op("dve", lambda e: e.tensor_scalar(out=ccar[:, :, 0:3], in0=ccar[:, :, 0:3], scalar1=flg[:, 0:1], scalar2=None, op0=ALU.mult), reads=r_ccar + CR, writes=r_ccar)
            xsrc = xsT_d if sample else xT_d[:, tb:tb + n]
            P.dma("sp", lambda e: e.dma_start(out=hT[:, :, :n], in_=xsrc.rearrange("(c p) t -> p c t", p=128)), None, writes=[r_hT])
            import os as _os
            SSTOP = int(_os.environ.get("K_SSTOP", "99")) if sample else int(_os.environ.get("K_PSTOP", "99"))
            if sample:
                for si_ in range(NSEQ_S):
                    P.dma("sp", lambda e, si_=si_: e.dma_start(out=ccar[:, :, 3 * si_:3 * si_ + 3], in_=conv0_d[si_].rearrange("p (c l) -> p c l", l=3)), None, writes=r_ccar)
            if SSTOP <= 0:
                return
            norm_to_xn(n, CV_N1)
            ffn(n)
            if SSTOP <= 1:
                return
            norm_to_xn(n, CV_NM)
            ko, vo, lo = (ksT_o, vsT_o, lfsT_o) if sample else ((None, None, None) if light else (kT_o[:, ob_:ob_ + n], vT_o[:, ob_:ob_ + n], lfT_o[:, ob_:ob_ + n]))
            kdst = kTs[:, :, SEQ - TS:SEQ] if False else None
            for c in range(2):
                w, rw = wtile(); bk = "AB"[c]
                proj8(bk, n, w, rw)
                kb = (kTs[:, c, tb:tb + n] if not sample else ksb[:, c, :, 0:LS])
                kvo_v = kvout[:, c, :n] if not sample else kvout[:, c, :n].rearrange("p (s l) -> p s l", s=NSEQ_S)
                qknorm(bk, n, CV_KN, 1.0, kb, r_kT, out_f32=kvout[:, c, :n], r_f32=r_kvo[c], si=c, f32_view=(kvo_v if sample else None))
                if not light:
                    P.dma("sp", lambda e, c=c: e.dma_start(out=ko[c * 128:(c + 1) * 128, :], in_=kvout[:, c, :n]), None, reads=[r_kvo[c]])
            for c in range(2):
                w, rw = wtile(); bk = "AB"[c]
                proj8(bk, n, w, rw)
                P.op("act", lambda e, c=c, bk=bk: e.activation(out=kvout[:, c, :n], in_=bank[bk][:, :n], func=AF.Copy), reads=[r_b[bk]], writes=[r_kvo[c]])
                if not light:
                    P.dma("sp", lambda e, c=c: e.dma_start(out=vo[c * 128:(c + 1) * 128, :], in_=kvout[:, c, :n]), None, reads=[r_kvo[c]])
                P.op("dve", lambda e, c=c: e.tensor_copy(out=sqb[:, c, :n], in_=kvout[:, c, :n]), reads=[r_kvo[c]], writes=[r_sq[c]])
                for (off, L, sq_) in segs:
                    P.op("pe", lambda e, c=c, off=off, L=L: e.transpose(pTB[:L, 0:128], sqb[:, c, off:off + L], ident[:]), reads=[r_sq[c]] + CR, writes=[r_tb])
                    if sample:
                        vdst = Vsm[:L, sq_, 2 * c:2 * c + 2, 0:64]
                    else:
                        vdst = Vs[:L, (tb + off) // 128, 2 * c:2 * c + 2, 0:64]
                    P.op("act", lambda e, L=L, vdst=vdst: e.activation(out=vdst, in_=pTB[:L, 0:128].rearrange("p (a b) -> p a b", a=2), func=AF.Copy), reads=[r_tb], writes=[r_V])
            w, rw = wtile()
            proj8("A", n, w, rw)
            P.op("act", lambda e: e.activation(out=lfT[:, :n], in_=bank["A"][:16, :n], func=AF.Exp, bias=nbf[:16, :], scale=-1.0), reads=[r_b["A"]] + CR, writes=[r_lf])
            P.op("act", lambda e: e.activation(out=lfT[:, :n], in_=lfT[:, :n], func=AF.Ln, bias=onec[:16, :], scale=1.0), reads=[r_lf], writes=[r_lf])
            P.op("dve", lambda e: e.tensor_scalar(out=lfT[:, :n], in0=lfT[:, :n], scalar1=-1.0, scalar2=None, op0=ALU.mult), reads=[r_lf], writes=[r_lf])
            if not light:
                P.dma("sp", lambda e: e.dma_start(out=lo, in_=lfT[:, :n]), None, reads=[r_lf])
            if not sample:
                if b == 0:
                    P.op("dve", lambda e: e.memset(ccarry[:], 0.0), writes=[r_cc])
                P.op("dve", lambda e: e.tensor_tensor_scan(out=cT[:, :n], data0=ones16[:, :n], data1=lfT[:, :n], initial=ccarry[:, 0:1], op0=ALU.mult, op1=ALU.add),
                     reads=[r_lf, r_cc], writes=[r_cT])
                P.op("dve", lambda e: e.tensor_copy(out=ccarry[:], in_=cT[:, n - 1:n]), reads=[r_cT], writes=[r_cc])
                for (off, L, sq_) in segs:
                    ti = (tb + off) // 128
                    mm(bank["B"][:L, 0:16], cT[:, off:off + L], identf[:16, :16], True, True, [r_cT] + CR, [r_b["B"]])
                    P.op("dve", lambda e, ti=ti, L=L: e.tensor_copy(out=cks[:L, ti, :], in_=bank["B"][:L, 0:16]), reads=[r_b["B"]], writes=[r_ck])
            if light:
                wskip(8)
            for c in range(0 if light else 8):
                w, rw = wtile(); bk = "AB"[c % 2]
                proj8(bk, n, w, rw)
                qknorm(bk, n, CV_QN, 0.125, qT[:, c, :n], r_qT, si=c % 2)
            for c in range(24):
                w, rw = wtile(); bk = "AB"[c % 2]; ci = c % 2
                proj8(bk, n, w, rw)
                cs = cstage[ci]
                nsq = len(segs) if sample else 1
                Ls = n // nsq
                csv = cs[:, :nsq * (Ls + 3)].rearrange("p (s l) -> p s l", s=nsq)
                if sample:
                    P.op("pool", lambda e, c=c, csv=csv, nsq=nsq: e.tensor_copy(out=csv[:, :, 0:3], in_=ccar[:, c, :3 * nsq].rearrange("p (s l) -> p s l", s=nsq)), reads=[r_ccar[c]], writes=[r_cst[ci]])
                elif b == 0:
                    P.op("pool", lambda e, csv=csv: e.memset(csv[:, :, 0:3], 0.0), writes=[r_cst[ci]])
                else:
                    P.op("pool", lambda e, c=c, csv=csv: e.tensor_copy(out=csv[:, 0, 0:3], in_=ccar[:, c, 0:3]), reads=[r_ccar[c]], writes=[r_cst[ci]])
                P.op("act", lambda e, bk=bk, csv=csv, nsq=nsq, Ls=Ls: e.activation(out=csv[:, :, 3:3 + Ls], in_=bank[bk][:, :n].rearrange("p (s l) -> p s l", s=nsq), func=AF.Copy),
                     reads=[r_b[bk]], writes=[r_cst[ci]])
                P.op("pool", lambda e, c=c, csv=csv, nsq=nsq, Ls=Ls: e.tensor_copy(out=ccar[:, c, :3 * nsq].rearrange("p (s l) -> p s l", s=nsq), in_=csv[:, :, Ls:Ls + 3]),
                     reads=[r_cst[ci]], writes=[r_ccar[c]])
                ca = cacc[ci][:, :n].rearrange("p (s l) -> p s l", s=nsq)
                wc = CV_CW + 4 * c
                P.op("dve", lambda e, c=c, ca=ca, csv=csv, Ls=Ls, wc=wc: e.tensor_scalar(out=ca, in0=csv[:, :, 3:3 + Ls], scalar1=cvec[:, wc + 3:wc + 4], scalar2=cvec[:, CV_CB + c:CV_CB + c + 1], op0=ALU.mult, op1=ALU.add),
                     reads=[r_cst[ci]] + CR, writes=[r_cacc[ci]])
                for j in range(3):
                    P.op("dve", lambda e, j=j, ca=ca, csv=csv, Ls=Ls, wc=wc: e.scalar_tensor_tensor(out=ca, in0=csv[:, :, j:j + Ls], scalar=cvec[:, wc + j:wc + j + 1], in1=ca, op0=ALU.mult, op1=ALU.add),
                         reads=[r_cst[ci], r_cacc[ci]] + CR, writes=[r_cacc[ci]])
                P.op("act", lambda e, c=c, ci=ci: e.activation(out=xc[:, c, :n], in_=cacc[ci][:, :n], func=AF.Silu), reads=[r_cacc[ci]], writes=[r_xc[c]])
            if sample:
                for si_ in range(NSEQ_S):
                    P.dma("sp", lambda e, si_=si_: e.dma_start(out=convs_o[si_].rearrange("p (c l) -> p c l", l=3), in_=ccar[:, :, 3 * si_:3 * si_ + 3]), None, reads=r_ccar)
            elif b == NB - 1:
                P.dma("sp", lambda e: e.dma_start(out=conv_o.rearrange("p (c l) -> p c l", l=3), in_=ccar[:, :, 0:3]), None, reads=r_ccar)

            if SSTOP <= 2:
                return
            for (off, L, sq_) in segs:
                first = (b == 0 and off == 0) if not sample else True
                if sample:
                    P.dma("sp", lambda e, sq_=sq_: e.dma_start(out=hst[:], in_=ssm0_d[sq_]), None, writes=[r_hst])
                elif first:
                    P.op("pool", lambda e: e.memset(hst[:], 0.0), writes=[r_hst])
                if not light:
                    P.op("act", lambda e: e.activation(out=hbf, in_=hst[:], func=AF.Copy), reads=[r_hst], writes=[r_hbf])
                for c in range(8):
                    mm(bank["E"][:L, 0:32], xn[:, c, off:off + L], wdt[:, c, :], c == 0, c == 7, [r_xn] + CR, [r_b["E"]])
                P.op("dve", lambda e, L=L: e.tensor_tensor(out=dtt[:L, :], in0=bank["E"][:L, 0:32], in1=rowc[:L, 0:32], op=ALU.add), reads=[r_b["E"]] + CR, writes=[r_ss])
                P.op("act", lambda e, L=L: e.activation(out=dtt[:L, :], in_=dtt[:L, :], func=AF.Exp), reads=[r_ss], writes=[r_ss])
                P.op("act", lambda e, L=L: e.activation(out=dtt[:L, :], in_=dtt[:L, :], func=AF.Ln, bias=onec[:L, :], scale=1.0), reads=[r_ss] + CR, writes=[r_ss])
                P.op("dve", lambda e, L=L: e.tensor_tensor(out=at[:L, :], in0=dtt[:L, :], in1=A_bc[:L, :], op=ALU.mult), reads=[r_ss] + CR, writes=[r_ss])
                mm(bank["F"][:L, 0:32], tri_f[:L, :L], at[:L, :], True, True, [r_ss] + CR, [r_b["F"]])
                mm(bank["E"][:, 32:64], onesf[:L, :], at[:L, :], True, True, [r_ss] + CR, [r_b["E"]])
                P.op("dve", lambda e, L=L: e.tensor_copy(out=acs[:L, :], in_=bank["F"][:L, 0:32]), reads=[r_b["F"]], writes=[r_ss])
                P.op("dve", lambda e: e.tensor_copy(out=tot[:], in_=bank["E"][:, 32:64]), reads=[r_b["E"]], writes=[r_ss])
                P.op("dve", lambda e, L=L: e.tensor_tensor(out=dte[:L, :], in0=tot[:L, :], in1=acs[:L, :], op=ALU.subtract), reads=[r_ss], writes=[r_ss])
                P.op("act", lambda e, L=L: e.activation(out=dte[:L, :], in_=dte[:L, :], func=AF.Exp), reads=[r_ss], writes=[r_ss])
                P.op("act", lambda e, L=L: e.activation(out=eacs[:L, :], in_=acs[:L, :], func=AF.Exp), reads=[r_ss], writes=[r_ss])
                P.op("act", lambda e: e.activation(out=cdec[:], in_=tot[:], func=AF.Exp), reads=[r_ss], writes=[r_ss])
                for c in range(16):
                    P.op("pe", lambda e, c=c, off=off, L=L: e.transpose(pTB[:L, c * 128:(c + 1) * 128], xc[:, c, off:off + L], ident[:]), reads=[r_xc[c]] + CR, writes=[r_tb])
                x3 = pTB[:L, :].rearrange("p (h d) -> p h d", d=64)
                P.op("dve", lambda e, L=L, x3=x3: e.tensor_tensor(out=xdt[:L, :].rearrange("p (h d) -> p h d", d=64), in0=x3, in1=dtt[:L, :].unsqueeze(2).to_broadcast([L, 32, 64]), op=ALU.mult),
                     reads=[r_tb, r_ss], writes=[r_xdt])
                P.op("dve", lambda e, L=L: e.tensor_tensor(out=xw[:L, :].rearrange("p (h d) -> p h d", d=64), in0=xdt[:L, :].rearrange("p (h d) -> p h d", d=64), in1=dte[:L, :].unsqueeze(2).to_broadcast([L, 32, 64]), op=ALU.mult),
                     reads=[r_xdt, r_ss], writes=[r_xw])
                for g in range(4):
                    P.op("pe", lambda e, g=g, off=off, L=L: e.transpose(pTB[:L, g * 128:(g + 1) * 128], xc[:, 16 + g, off:off + L], ident[:]), reads=[r_xc[16 + g], r_xdt] + CR, writes=[r_tb])
                P.op("act", lambda e, L=L: e.activation(out=Btok[:L, :], in_=pTB[:L, 0:512], func=AF.Copy), reads=[r_tb], writes=[r_Btok])
                for g in range(4):
                    Bt = xc[:, 16 + g, off:off + L]; Ct = xc[:, 20 + g, off:off + L]
                    rB, rC = r_xc[16 + g], r_xc[20 + g]
                    if light:
                        mm(bank["E"][:, :], Btok[:L, g * 128:(g + 1) * 128], xw[:L, g * 512:(g + 1) * 512], True, True, [r_Btok, r_xw], [r_b["E"]])
                        hs3 = hst[:, g * 512:(g + 1) * 512].rearrange("p (h d) -> p h d", d=64)
                        P.op("pool", lambda e, g=g, hs3=hs3: e.tensor_tensor(out=hs3, in0=hs3, in1=cdec[:, g * 8:(g + 1) * 8].unsqueeze(2).to_broadcast([128, 8, 64]), op=ALU.mult),
                             reads=[r_ss], writes=[r_hst])
                        P.op("dve", lambda e, g=g: e.tensor_tensor(out=hst[:, g * 512:(g + 1) * 512], in0=hst[:, g * 512:(g + 1) * 512], in1=bank["E"][:, :], op=ALU.add),
                             reads=[r_b["E"]], writes=[r_hst])
                        continue
                    mm(bank["F"][:L, :L], Bt, Ct, True, True, [rB, rC], [r_b["F"]])
                    P.op("dve", lambda e, L=L: e.tensor_tensor(out=cbm[:L, :L], in0=bank["F"][:L, :L], in1=mask01[:L, :L], op=ALU.mult), reads=[r_b["F"]] + CR, writes=[r_cbm])
                    P.op("pool", lambda e, g=g, L=L: e.tensor_tensor(out=Dg[:L, :, :L], in0=at[:L, g * 8:(g + 1) * 8].unsqueeze(2).to_broadcast([L, 8, L]), in1=tri_f[:L, :L].unsqueeze(1).to_broadcast([L, 8, L]), op=ALU.mult),
                         reads=[r_ss] + CR, writes=[r_Dg])
                    segp = pAB[:L, :].rearrange("p (h l) -> p h l", l=128)
                    if L == 128:
                        for hh in range(2):
                            P.op("pe", lambda e, hh=hh, L=L, segp=segp: e.matmul(segp[:, hh * 4:(hh + 1) * 4, :L], lhsT=ustr_f[:L, :L], rhs=Dg[:L, hh * 4:(hh + 1) * 4, :L], start=True, stop=True),
                                 reads=[r_Dg] + CR, writes=[r_b["AB"[hh]]])
                    else:
                        for hh in range(8):
                            P.op("pe", lambda e, hh=hh, L=L, segp=segp: e.matmul(segp[:, hh, :L], lhsT=ustr_f[:L, :L], rhs=Dg[:L, hh, :L], start=True, stop=True),
                                 reads=[r_Dg] + CR, writes=[r_b["AB"[hh // 4]]])
                    MT3 = MT[:L, :].rearrange("p (h l) -> p h l", l=128)
                    MTb3 = MTb[:L, :].rearrange("p (h l) -> p h l", l=128)
                    P.op("act", lambda e, L=L, segp=segp, MT3=MT3: e.activation(out=MT3[:, :, :L], in_=segp[:, :, :L], func=AF.Exp), reads=[r_b["A"], r_b["B"]], writes=[r_MT])
                    P.op("dve", lambda e, L=L, MT3=MT3, MTb3=MTb3: e.tensor_tensor(out=MTb3[:, :, :L], in0=MT3[:, :, :L], in1=cbm[:L, :L].unsqueeze(1).to_broadcast([L, 8, L]), op=ALU.mult),
                         reads=[r_MT, r_cbm], writes=[r_MTb])
                    for hh in range(8):
                        h = g * 8 + hh
                        mm(bank["C"][:L, hh * 64:(hh + 1) * 64], MTb3[:, hh, :L], xdt[:L, h * 64:(h + 1) * 64], True, True, [r_MTb, r_xdt], [r_b["C"]])
                    mm(bank["D"][:L, :], Ct, hbf[:, g * 512:(g + 1) * 512], True, True, [rC, r_hbf], [r_b["D"]])
                    P.op("dve", lambda e, g=g, L=L: e.tensor_tensor(out=tmpf[:L, :].rearrange("p (h d) -> p h d", d=64), in0=bank["D"][:L, :].rearrange("p (h d) -> p h d", d=64),
                                                                   in1=eacs[:L, g * 8:(g + 1) * 8].unsqueeze(2).to_broadcast([L, 8, 64]), op=ALU.mult),
                         reads=[r_b["D"], r_ss], writes=[r_tmpf])
                    P.op("dve", lambda e, g=g, L=L: e.tensor_tensor(out=ytok[:L, g * 512:(g + 1) * 512], in0=bank["C"][:L, :], in1=tmpf[:L, :], op=ALU.add),
                         reads=[r_b["C"], r_tmpf], writes=[r_ytok])
                    mm(bank["E"][:, :], Btok[:L, g * 128:(g + 1) * 128], xw[:L, g * 512:(g + 1) * 512], True, True, [r_Btok, r_xw], [r_b["E"]])
                    hs3 = hst[:, g * 512:(g + 1) * 512].rearrange("p (h d) -> p h d", d=64)
                    P.op("pool", lambda e, g=g, hs3=hs3: e.tensor_tensor(out=hs3, in0=hs3, in1=cdec[:, g * 8:(g + 1) * 8].unsqueeze(2).to_broadcast([128, 8, 64]), op=ALU.mult),
                         reads=[r_ss, r_hbf], writes=[r_hst])
                    P.op("dve", lambda e, g=g: e.tensor_tensor(out=hst[:, g * 512:(g + 1) * 512], in0=hst[:, g * 512:(g + 1) * 512], in1=bank["E"][:, :], op=ALU.add),
                         reads=[r_b["E"]], writes=[r_hst])
                for c in range(0 if light else 16):
                    P.op("pe", lambda e, c=c, L=L: e.transpose(pTB[:, c * 128:c * 128 + L], ytok[:L, c * 128:(c + 1) * 128], ident[:L, :L]), reads=[r_ytok] + CR, writes=[r_tb])
                    P.op("dve", lambda e, c=c, off=off, L=L: e.scalar_tensor_tensor(out=yT[:, c, off:off + L], in0=xc[:, c, off:off + L], scalar=cvec[:, CV_DS + c:CV_DS + c + 1],
                                                                                 in1=pTB[:, c * 128:c * 128 + L], op0=ALU.mult, op1=ALU.add),
                         reads=[r_tb, r_xc[c]] + CR, writes=[r_yT[c]])
                if sample:
                    P.dma("sp", lambda e, sq_=sq_: e.dma_start(out=ssms_o[sq_], in_=hst[:]), None, reads=[r_hst])
                elif b == NB - 1 and off + L == n:
                    P.dma("sp", lambda e: e.dma_start(out=ssm_o, in_=hst[:]), None, reads=[r_hst])
            if SSTOP <= 3:
                return
            if light:
                wskip(16 + 48 + 8 + 68)
                return
            for c in range(16):
                w, rw = wtile(); bk = "AB"[c % 2]; si = c % 2
                proj8(bk, n, w, rw)
                P.op("act", lambda e, bk=bk, si=si: e.activation(out=stg[si][:, :n], in_=bank[bk][:, :n], func=AF.Silu), reads=[r_b[bk]], writes=[r_stg[si]])
                P.op("dve", lambda e, c=c, si=si: e.tensor_tensor(out=yT[:, c, :n], in0=yT[:, c, :n], in1=stg[si][:, :n], op=ALU.mult), reads=[r_stg[si], r_yT[c]], writes=[r_yT[c]])
                P.op("act", lambda e, c=c: e.activation(out=sqb[:, c % 4, :n], in_=yT[:, c, :n], func=AF.Square), reads=[r_yT[c]], writes=[r_sq[c % 4]])
                mm(bank["E"][:, :n], ones_g[:], sqb[:, c % 4, :n], c % 4 == 0, c % 4 == 3, [r_sq[c % 4]] + CR, [r_b["E"]])
                if c % 4 == 3:
                    rms_rstd(bank["E"][:, :n], r_b["E"], n)
                    for cc in range(c - 3, c + 1):
                        P.op("dve", lambda e, cc=cc: e.scalar_tensor_tensor(out=yT[:, cc, :n], in0=yT[:, cc, :n], scalar=cvec[:, CV_SN + cc:CV_SN + cc + 1], in1=rstd[:, :n], op0=ALU.mult, op1=ALU.mult),
                             reads=[r_rstd, r_yT[cc]] + CR, writes=[r_yT[cc]])

            if SSTOP <= 4:
                return
            attention(n, segs, sample, b)

            if SSTOP <= 5:
                return
            oT3 = oT[0:64, 0:16, :]
            for m in range(8):
                w0, r0 = wtile(); w1, r1 = wtile()
                for c in range(16):
                    w, r = (w0, r0) if c < 8 else (w1, r1)
                    mm(bank["A"][:, :n], w[:, (c % 8) * 128:(c % 8 + 1) * 128], yT[:, c, :n], c == 0, c == 15, [r, r_yT[c]], [r_b["A"]])
                wg_, rg_ = wtile()
                proj8("C", n, wg_, rg_)
                P.op("act", lambda e: e.activation(out=stg[0][:, :n], in_=bank["C"][:, :n], func=AF.Sigmoid), reads=[r_b["C"]], writes=[r_stg[0]])
                P.op("dve", lambda e: e.tensor_tensor(out=stg[2][:, :n], in0=bank["A"][:, :n], in1=stg[0][:, :n], op=ALU.mult), reads=[r_b["A"], r_stg[0]], writes=[r_stg[2]])
                w0, r0 = wtile(); w1, r1 = wtile()
                for h in range(16):
                    w, r = (w0, r0) if h < 8 else (w1, r1)
                    mm(bank["B"][:, :n], w[0:64, (h % 8) * 128:(h % 8 + 1) * 128], oT3[:, h, :n], h == 0, h == 15, [r, r_oT], [r_b["B"]])
                wg_, rg_ = wtile()
                proj8("D", n, wg_, rg_)
                P.op("act", lambda e: e.activation(out=stg[1][:, :n], in_=bank["D"][:, :n], func=AF.Sigmoid), reads=[r_b["D"]], writes=[r_stg[1]])
                P.op("dve", lambda e: e.tensor_tensor(out=stg[1][:, :n], in0=bank["B"][:, :n], in1=stg[1][:, :n], op=ALU.mult), reads=[r_b["B"], r_stg[1]], writes=[r_stg[1]])
                P.op("dve", lambda e, m=m: e.tensor_tensor(out=merged[:, m, :n], in0=stg[1][:, :n], in1=stg[2][:, :n], op=ALU.add), reads=[r_stg[1], r_stg[2], r_qT], writes=[r_mg])
            for m in range(8):
                w, rw = wtile(); bk = "AB"[m % 2]
                for c in range(8):
                    mm(bank[bk][:, :n], w[:, c * 128:(c + 1) * 128], merged[:, c, :n], c == 0, c == 7, [rw, r_mg], [r_b[bk]])
                P.op("dve", lambda e, m=m, bk=bk: e.tensor_tensor(out=hT[:, m, :n], in0=hT[:, m, :n], in1=bank[bk][:, :n], op=ALU.add), reads=[r_b[bk], r_hT], writes=[r_hT])
            if SSTOP <= 6:
                return
            norm_to_xn(n, CV_N2)
            ffn(n)
            ydst = ysT_o if sample else yT_o[:, ob_:ob_ + n]
            P.dma("sp", lambda e: e.dma_start(out=ydst.rearrange("(c p) t -> p c t", p=128), in_=hT[:, :, :n]), None, reads=[r_hT])

        def attn_tile(kv, qcols_ap, nqc, kt_ap, nk, v_ap, bias_ap, first, last, diag, acc_bank, rd):
            pass

        sT_bufs = [(sTt, r_sTt), (stg[0], r_stg[0]), (stg[1], r_stg[1])]
        PT_bufs = [(PT, r_PT), (asl(0, 1), Res("PT1")), (asl(1, 2), Res("PT2")), (asl(2, 3), Res("PT3"))]
        att_it = [0]

        def attention(n, segs, sample, b):
            tb = b * T
            for (off, L, sq_) in segs:
                if not sample:
                    qi = (tb + off) // 128
                    P.op("dve", lambda e, off=off, L=L: e.tensor_scalar(out=dgl[:], in0=identf[:16, :16], scalar1=cT[:, off + L - 1:off + L], scalar2=None, op0=ALU.mult), reads=[r_cT] + CR, writes=[r_dgl])
                    mm(bank["F"][:, 0:16], onesf[:16, :], dgl[:], True, True, [r_dgl] + CR, [r_b["F"]])
                    P.op("dve", lambda e: e.tensor_copy(out=cref[:], in_=bank["F"][:, 0:16]), reads=[r_b["F"]], writes=[r_cref])
                    nkt = qi + 1
                    P.op("dve", lambda e, nkt=nkt: e.tensor_tensor(out=biasb[:, :nkt, :], in0=cref[:].unsqueeze(1).to_broadcast([128, nkt, 16]), in1=cks[:, :nkt, :], op=ALU.subtract),
                         reads=[r_cref, r_ck], writes=[r_bias])
                    if NL > 0:
                        P.op("dve", lambda e: e.tensor_scalar(out=biasb[:, :NL * 4, :], in0=biasb[:, :NL * 4, :], scalar1=flg[:, 1:2], scalar2=None, op0=ALU.add),
                             reads=[r_bias] + CR, writes=[r_bias])
                    for kv in range(4):
                        half = kv % 2; cp = kv // 2
                        ob = "CD"[kv % 2]
                        qap = qT[half * 64:(half + 1) * 64, 4 * cp:4 * cp + 4, off:off + L]
                        def emit_st(kt, half=half, cp=cp, qap=qap, kv=kv, L=L, qi=qi):
                            sbk = "AB"[kt % 2]
                            dg = (kt == qi)
                            P.op("pe", lambda e: e.matmul(bank[sbk][:, :4 * L].rearrange("p (g t) -> p g t", g=4), lhsT=kTs[half * 64:(half + 1) * 64, cp, kt * 128:(kt + 1) * 128], rhs=qap, start=True, stop=True),
                                 reads=[r_kT, r_qT], writes=[r_b[sbk]])
                            it = att_it[0]; att_it[0] += 1
                            sTb, r_sTb = sT_bufs[it % 3]
                            PTb, r_PTb = PT_bufs[it % 4]
                            P.op("dve", lambda e: e.scalar_tensor_tensor(out=sTb[:, :4 * L].rearrange("p (g t) -> p g t", g=4), in0=bank[sbk][:, :4 * L].rearrange("p (g t) -> p g t", g=4), scalar=1.0,
                                                                         in1=biasb[:, kt, 4 * kv:4 * kv + 4].unsqueeze(2).to_broadcast([128, 4, L]), op0=ALU.mult, op1=ALU.add),
                                 reads=[r_b[sbk], r_bias], writes=[r_sTb])
                            P.op("act", lambda e: e.activation(out=PTb[:, :4 * L], in_=sTb[:, :4 * L], func=AF.Exp), reads=[r_sTb], writes=[r_PTb])
                            if dg:
                                P.op("pool", lambda e: e.affine_select(out=PTb[:, :4 * L].rearrange("p (g t) -> p g t", g=4), in_=PTb[:, :4 * L].rearrange("p (g t) -> p g t", g=4), pattern=[[0, 4], [1, L]], compare_op=ALU.is_ge, fill=0.0, base=0, channel_multiplier=-1),
                                     reads=[r_PTb], writes=[r_PTb])
                            return PTb, r_PTb
                        pend = emit_st(0)
                        for kt in range(nkt):
                            cur = pend
                            if kt + 1 < nkt:
                                pend = emit_st(kt + 1)
                            mm(bank[ob][0:65, :4 * L], Vs[:, kt, kv, :], cur[0][:, :4 * L], kt == 0, kt == nkt - 1, [r_V, cur[1]], [r_b[ob]])
                        finish_head(kv, ob, off, L, 4 * L)
                else:
                    sample_attention(off, L, sq_)

        def finish_head(kv, ob, off, L, ncol):
            P.op("dve", lambda e: e.reciprocal(out=rec[64:65, :ncol], in_=bank[ob][64:65, :ncol]), reads=[r_b[ob]], writes=[r_rec])
            mm(bank["E"][0:64, :ncol], onesf[64:65, 0:64], rec[64:65, :ncol], True, True, [r_rec] + CR, [r_b["E"]])
            P.op("act", lambda e: e.activation(out=bcs[:, :ncol], in_=bank["E"][0:64, :ncol], func=AF.Copy), reads=[r_b["E"]], writes=[r_bcs])
            P.op("dve", lambda e: e.tensor_tensor(out=oT[0:64, 4 * kv:4 * kv + 4, off:off + L], in0=bank[ob][0:64, :ncol].rearrange("p (g t) -> p g t", g=4), in1=bcs[:, :ncol].rearrange("p (g t) -> p g t", g=4), op=ALU.mult),
                 reads=[r_b[ob], r_bcs] + r_xc, writes=[r_oT])

        if do_sample:
            qS = sb("qS", [128, NSEQ_S, 8, LS], BF16); r_qS = Res("qS")
            ksb = sb("ksb", [128, 2, NSEQ_S, 128], BF16)
            P.op("pool", lambda e: e.memset(ksb[:].rearrange("p a b c -> p (a b c)"), 0.0), writes=[r_kT])
            Vsm = sb("Vsm", [128, NSEQ_S, 4, 65], BF16)
            P.op("pool", lambda e: e.memset(Vsm[:].rearrange("p a b c -> p (a b c)"), 1.0), writes=[r_V])
            ptab = sb("ptab", [128, NSEQ_S * NPG], I32); r_pt = Res("ptab")
            ridx = ptab
            piota = sb("piota", [128, 1], I32)
            P.dma("sp", lambda e: e.dma_start(out=ptab[:], in_=pt_d.partition_broadcast(128)), None, writes=[r_pt])
            P.op("pool", lambda e: e.iota(piota[:], pattern=[[0, 1]], base=0, channel_multiplier=1), writes=[r_pt])
            P.op("pool", lambda e: e.tensor_scalar(out=ridx[:], in0=ptab[:], scalar1=128, scalar2=piota[:, 0:1], op0=ALU.mult, op1=ALU.add), reads=[r_pt], writes=[r_pt])
            import os as _os
            NPB = int(_os.environ.get("K_NPB", "4"))
            r_kpg = [Res(f"kpg{i}") for i in range(2)]; r_vpg = [Res(f"vpg{i}") for i in range(2)]
            r_vraw = [Res(f"vraw{i}") for i in range(2)]; r_lpg = [Res(f"lpg{i}") for i in range(2)]; r_ktp = Res("ktp")
            PGE = NPB * 256; VGE = NPB * 4 * 65
            if 2 * SEQ >= 5 * PGE + 2 * VGE:
                kflat = kTs[:].rearrange("p a b -> p (a b)")
                kpg = [kflat[:, i * PGE:(i + 1) * PGE].rearrange("p (a b) -> p a b", a=NPB) for i in range(2)]
                vraw = [kflat[:, (2 + i) * PGE:(3 + i) * PGE].rearrange("p (a b) -> p a b", a=NPB) for i in range(2)]
                ktp = kflat[:, 4 * PGE:5 * PGE].rearrange("p (a b c) -> p a b c", a=NPB, b=2)
                vpg = [kflat[:, 5 * PGE + i * VGE:5 * PGE + (i + 1) * VGE].rearrange("p (a b c) -> p a b c", a=NPB, b=4) for i in range(2)]
            else:
                kpg = [sb(f"kpg{i}", [128, NPB, 256], BF16) for i in range(2)]
                vraw = [sb(f"vraw{i}", [128, NPB, 256], BF16) for i in range(2)]
                vpg = [sb(f"vpg{i}", [128, NPB, 4, 65], BF16) for i in range(2)]
                ktp = sb("ktp", [128, NPB, 2, 128], BF16)
            lpg = [sb(f"lpg{i}", [128, NPB, 16]) for i in range(2)]
            s_pg = [P.slot(f"pg{i}") for i in range(2)]
            Racc = sb("Racc", [128, 16]); r_R = Res("Racc")
            bpg = sb("bpg", [128, NPB, 16]); r_bpg = Res("bpg")
            lftok = sb("lftok", [128, 16]); r_lftok = Res("lftok")
            oacc = sb("oacc", [128, 16 * LS]); r_oacc = Res("oacc")
            bph = sb("bph", [128, 2, NPB * 2, 4]); r_bph = Res("bph")

        def sample_attention(off, L, sq_):
            nq = 4 * L
            if sq_ == 0:
                for i in range(2):
                    P.op("pool", lambda e, i=i: e.memset(vpg[i][:].rearrange("p a b c -> p (a b c)"), 1.0), writes=[r_vpg[i], r_kT])
                P.op("dve", lambda e: e.tensor_copy(out=qS[:].rearrange("p s c t -> p c s t"), in_=qT[:, :, :TS].rearrange("p c (s t) -> p c s t", s=NSEQ_S)), reads=[r_qT], writes=[r_qS])
            import os as _os
            nb = 0 if _os.environ.get('K_NOPAGES') else NPG // NPB
            P.op("pool", lambda e: e.memset(Racc[:], 0.0), writes=[r_R])
            mm(bank["F"][:L, 0:16], lfT[:, off:off + L], identf[:16, :16], True, True, [r_lf] + CR, [r_b["F"]])
            P.op("dve", lambda e: e.tensor_copy(out=lftok[:L, :], in_=bank["F"][:L, 0:16]), reads=[r_b["F"]], writes=[r_lftok])
            P.op("dve", lambda e: e.tensor_copy(out=Racc[:L, :], in_=bank["F"][:L, 0:16]), reads=[r_b["F"], r_R], writes=[r_R])
            mm(bank["F"][:, 16:32], ustr_f[:L, :], lftok[:L, :], True, True, [r_lftok] + CR, [r_b["F"]])
            P.op("dve", lambda e: e.tensor_copy(out=bpg[:, 0, :], in_=bank["F"][:, 16:32]), reads=[r_b["F"]], writes=[r_bpg])
            import os as _os
            ASTOP = float(_os.environ.get("K_ASTOP", "99"))
            if ASTOP <= 1:
                return
            for kv in range(4):
                half = kv % 2; cp = kv // 2; bk = "AB"[half]
                qap = qS[half * 64:(half + 1) * 64, sq_, 4 * cp:4 * cp + 4, :].rearrange("p c t -> p (c t)")
                P.op("pe", lambda e, half=half, cp=cp, qap=qap, bk=bk: e.matmul(bank[bk][:, cp * nq:(cp + 1) * nq], lhsT=ksb[half * 64:(half + 1) * 64, cp, sq_, :], rhs=qap, start=True, stop=True),
                     reads=[r_kT, r_qS], writes=[r_b[bk]])
            if ASTOP <= 1.2:
                return
            for half in range(2):
                bk = "AB"[half]
                P.op("dve", lambda e, half=half: e.tensor_copy(out=bph[:, half, 0:2, :], in_=bpg[:, 0, :].rearrange("p (c h g) -> p c h g", c=2, h=2)[:, :, half, :]), reads=[r_bpg], writes=[r_bph])
                o3 = sTt[:, half * 2 * nq:(half + 1) * 2 * nq].rearrange("p (a t) -> p a t", t=L)
                i3 = bank[bk][:, :2 * nq].rearrange("p (a t) -> p a t", t=L)
                b3 = bph[:, half, 0:2, :].rearrange("p c g -> p (c g)").unsqueeze(2).to_broadcast([128, 8, L])
                P.op("dve", lambda e, o3=o3, i3=i3, b3=b3: e.scalar_tensor_tensor(out=o3, in0=i3, scalar=1.0, in1=b3, op0=ALU.mult, op1=ALU.add),
                     reads=[r_b[bk], r_bph], writes=[r_sTt])
            if ASTOP <= 1.4:
                return
            P.op("act", lambda e: e.activation(out=PT[:, :4 * nq], in_=sTt[:, :4 * nq], func=AF.Exp), reads=[r_sTt], writes=[r_PT])
            if ASTOP <= 1.6:
                return
            P.op("dve", lambda e: e.tensor_tensor(out=PT[:, :4 * nq].rearrange("p (h t) -> p h t", h=16), in0=PT[:, :4 * nq].rearrange("p (h t) -> p h t", h=16), in1=mask01[:, :L].unsqueeze(1).to_broadcast([128, 16, L]), op=ALU.mult),
                 reads=[r_PT] + CR, writes=[r_PT])
            if ASTOP <= 2:
                return
            for kv in range(4):
                pc = ((kv % 2) * 2 + kv // 2) * nq
                mm(bank["C"][0:65, kv * nq:(kv + 1) * nq], Vsm[:, sq_, kv, :], PT[:, pc:pc + nq], True, True, [r_V, r_PT], [r_b["C"]])
            P.op("dve", lambda e: e.tensor_copy(out=oacc[0:65, :4 * nq], in_=bank["C"][0:65, :4 * nq]), reads=[r_b["C"]], writes=[r_oacc])
            if ASTOP <= 3:
                return
            for bi in range(nb):
                pb = nb - 1 - bi
                i2 = bi % 2
                for pp in range(NPB):
                    pg = pb * NPB + pp
                    col = sq_ * NPG + pg
                    P.dma("pool", lambda e, i2=i2, pp=pp, col=col: e.indirect_dma_start(out=kpg[i2][:, pp, :], out_offset=None, in_=ck_d, in_offset=bass.IndirectOffsetOnAxis(ap=ridx[:, col:col + 1], axis=0)),
                          None, reads=[r_pt], writes=[r_kpg[i2]])
                    P.dma("pool", lambda e, i2=i2, pp=pp, col=col: e.indirect_dma_start(out=vraw[i2][:, pp, :], out_offset=None, in_=cvv_d, in_offset=bass.IndirectOffsetOnAxis(ap=ridx[:, col:col + 1], axis=0)),
                          None, reads=[r_pt], writes=[r_vraw[i2]])
                    P.dma("pool", lambda e, i2=i2, pp=pp, col=col: e.indirect_dma_start(out=lpg[i2][:, pp, :], out_offset=None, in_=clf_d, in_offset=bass.IndirectOffsetOnAxis(ap=ridx[:, col:col + 1], axis=0)),
                          None, reads=[r_pt], writes=[r_lpg[i2]])
                P.op("act", lambda e, i2=i2: e.activation(out=vpg[i2][:, :, :, 0:64], in_=vraw[i2][:].rearrange("p a (b c) -> p a b c", b=4), func=AF.Copy), reads=[r_vraw[i2]], writes=[r_vpg[i2]])
                for pp in range(NPB):
                    for c in range(2):
                        P.op("pe", lambda e, i2=i2, pp=pp, c=c: e.transpose(pTB[:, (pp * 2 + c) * 128:(pp * 2 + c + 1) * 128], kpg[i2][:, pp, c * 128:(c + 1) * 128], ident[:]),
                             reads=[r_kpg[i2]] + CR, writes=[r_tb])
                P.op("act", lambda e: e.activation(out=ktp[:].rearrange("p a b c -> p (a b c)"), in_=pTB[:, 0:NPB * 256], func=AF.Copy), reads=[r_tb], writes=[r_ktp])
                for pp in reversed(range(NPB)):
                    mm(bank["F"][:, 0:16], ustr_f[:], lpg[i2][:, pp, :], True, False, [r_lpg[i2]] + CR, [r_b["F"]])
                    mm(bank["F"][:, 0:16], onesf[:], Racc[:], False, True, [r_R] + CR, [r_b["F"]])
                    P.op("dve", lambda e, pp=pp: e.tensor_copy(out=bpg[:, pp, :], in_=bank["F"][:, 0:16]), reads=[r_b["F"]], writes=[r_bpg])
                    P.op("dve", lambda e, pp=pp, i2=i2: e.tensor_tensor(out=Racc[:], in0=Racc[:], in1=lpg[i2][:, pp, :], op=ALU.add), reads=[r_lpg[i2], r_R], writes=[r_R])
                for pp in range(NPB):
                    for kv in range(4):
                        half = kv % 2; cp = kv // 2; bk = "AB"[half]
                        qap = qS[half * 64:(half + 1) * 64, sq_, 4 * cp:4 * cp + 4, :].rearrange("p c t -> p (c t)")
                        c0 = (pp * 2 + cp) * nq
                        P.op("pe", lambda e, half=half, cp=cp, qap=qap, pp=pp, c0=c0, bk=bk: e.matmul(bank[bk][:, c0:c0 + nq], lhsT=ktp[half * 64:(half + 1) * 64, pp, cp, :], rhs=qap, start=True, stop=True),
                             reads=[r_ktp, r_qS], writes=[r_b[bk]])
                tot_c = NPB * 4 * nq
                hc = NPB * 2 * nq
                for half in range(2):
                    bk = "AB"[half]
                    P.op("dve", lambda e, half=half: e.tensor_copy(out=bph[:, half, :, :], in_=bpg[:].rearrange("p n (c h g) -> p (n c) h g", c=2, h=2)[:, :, half, :]), reads=[r_bpg], writes=[r_bph])
                    o3 = sTt[:, half * hc:(half + 1) * hc].rearrange("p (a t) -> p a t", t=L)
                    i3 = bank[bk][:, :hc].rearrange("p (a t) -> p a t", t=L)
                    b3 = bph[:, half, :, :].rearrange("p a g -> p (a g)").unsqueeze(2).to_broadcast([128, NPB * 8, L])
                    P.op("dve", lambda e, o3=o3, i3=i3, b3=b3: e.scalar_tensor_tensor(out=o3, in0=i3, scalar=1.0, in1=b3, op0=ALU.mult, op1=ALU.add),
                         reads=[r_b[bk], r_bph], writes=[r_sTt])
                P.op("act", lambda e: e.activation(out=PT[:, :tot_c], in_=sTt[:, :tot_c], func=AF.Exp), reads=[r_sTt], writes=[r_PT])
                for pp in range(NPB):
                    for kv in range(4):
                        c0 = (pp * 4 + kv) * nq
                        pc = ((kv % 2) * NPB * 2 + pp * 2 + kv // 2) * nq
                        mm(bank["C"][0:65, c0:c0 + nq], vpg[i2][:, pp, kv, :], PT[:, pc:pc + nq], True, True, [r_vpg[i2], r_PT], [r_b["C"]])
                for pp in range(NPB):
                    P.op("dve", lambda e, pp=pp: e.tensor_tensor(out=oacc[0:65, :4 * nq], in0=oacc[0:65, :4 * nq], in1=bank["C"][0:65, pp * 4 * nq:(pp + 1) * 4 * nq], op=ALU.add),
                         reads=[r_b["C"], r_oacc], writes=[r_oacc])
            ncol = 4 * nq
            P.op("dve", lambda e: e.reciprocal(out=rec[64:65, :ncol], in_=oacc[64:65, :ncol]), reads=[r_oacc], writes=[r_rec])
            mm(bank["E"][0:64, :ncol], onesf[64:65, 0:64], rec[64:65, :ncol], True, True, [r_rec] + CR, [r_b["E"]])
            P.op("dve", lambda e: e.tensor_tensor(out=oT[0:64, 0:16, off:off + L], in0=oacc[0:64, :ncol].rearrange("p (h t) -> p h t", h=16), in1=bank["E"][0:64, :ncol].rearrange("p (h t) -> p h t", h=16), op=ALU.mult),
                 reads=[r_b["E"], r_oacc] + r_xc, writes=[r_oT])

        import os as _os
        for b in range(0 if _os.environ.get("K_NOPROMPT") else NB):
            block(T, [(i * 128, 128, 0) for i in range(4)], False, b)
        if do_sample:
            block(TS, [(i * LS, LS, i) for i in range(NSEQ_S)], True, 0)
        print("sbuf bytes remaining", nc.sbuf_bytes_remaining, flush=True)
        n, ns = P.emit(nc)
        print("ops", n, "signals", ns, flush=True)
    return nc


_PROG_CACHE = {}


def kernel(x_prompt, x_sample, cache_k, cache_v, cache_logf, state_ssm, state_conv, page_table,
           ffn1_norm, ffn1_w_gate, ffn1_w_up, ffn1_w_down, mix_norm, w_in, conv_w, conv_b,
           dt_bias, a_log, d_skip, ssd_norm, q_norm, k_norm, b_f, w_ssd_proj, w_attn_proj, w_out,
           ffn2_norm, ffn2_w_gate, ffn2_w_up, ffn2_w_down):
    f = lambda a: np.asarray(a, dtype=np.float32)
    B, SEQ, _ = x_prompt.shape
    DB, DS, _ = x_sample.shape
    NPOOL = cache_k.shape[1]
    NPG = page_table.shape[1]
    p = dict(ffn1_norm=f(ffn1_norm)[0], ffn1_w_gate=f(ffn1_w_gate)[0], ffn1_w_up=f(ffn1_w_up)[0], ffn1_w_down=f(ffn1_w_down)[0],
             mix_norm=f(mix_norm)[0], w_in=f(w_in)[0], conv_w=f(conv_w)[0], conv_b=f(conv_b)[0], dt_bias=f(dt_bias)[0],
             a_log=f(a_log)[0], d_skip=f(d_skip)[0], ssd_norm=f(ssd_norm)[0], q_norm=f(q_norm)[0], k_norm=f(k_norm)[0],
             b_f=f(b_f)[0], w_ssd_proj=f(w_ssd_proj)[0], w_attn_proj=f(w_attn_proj)[0], w_out=f(w_out)[0],
             ffn2_norm=f(ffn2_norm)[0], ffn2_w_gate=f(ffn2_w_gate)[0], ffn2_w_up=f(ffn2_w_up)[0], ffn2_w_down=f(ffn2_w_down)[0])
    wst = build_weight_stream(p)
    cvec = build_cvec(p)
    rowc = np.concatenate([p["dt_bias"], p["a_log"], p["d_skip"]]).reshape(1, 96).astype(np.float32)
    wdt = np.ascontiguousarray(p["w_in"][:, O_DT:O_DT + 32].reshape(8, 128, 32).transpose(1, 0, 2).reshape(128, 256))
    ck2 = np.ascontiguousarray(f(cache_k)[0].reshape(NPOOL * 128, 256))
    cv2 = np.ascontiguousarray(f(cache_v)[0].reshape(NPOOL * 128, 256))
    cl2 = np.ascontiguousarray(f(cache_logf)[0].reshape(NPOOL * 128, 16))
    xp = f(x_prompt); xs = f(x_sample)
    ssm = f(state_ssm)[0]; cst = f(state_conv)[0]
    pt = np.asarray(page_table, dtype=np.int32)
    NLh = (SEQ // 512) // 2
    key = (SEQ, NPG, NPOOL)
    if key not in _PROG_CACHE:
        import os as _os
        _PROG_CACHE[key] = build_program(SEQ, NPG, NPOOL, do_sample=(_os.environ.get('K_NOSAMPLE') is None))
    nc = _PROG_CACHE[key]
    in_maps = []
    for c in range(NCORES):
        sq = slice(NSEQ_S * c, NSEQ_S * (c + 1))
        in_maps.append({
            "xT": np.ascontiguousarray((xp[c % B] if (c // B == 1 or NLh == 0) else np.concatenate([xp[c % B][:SEQ // 2], xp[c % B][:SEQ // 2]])).T),
            "flg": np.tile(np.array([[1.0, 0.0]] if (c // B == 1 or NLh == 0) else [[0.0, -30000.0]], np.float32), (128, 1)),
            "xsT": np.ascontiguousarray(xs[sq].reshape(NSEQ_S * DS, D).T),
            "wst": wst, "cvec": cvec, "rowc": rowc, "wdt": wdt,
            "ssm0": np.ascontiguousarray(ssm[sq].reshape(NSEQ_S, 2048, 128).transpose(0, 2, 1)),
            "conv0": np.ascontiguousarray(cst[sq].reshape(NSEQ_S, 3, 24, 128).transpose(0, 3, 2, 1).reshape(NSEQ_S, 128, 72)),
            "cache_k": ck2, "cache_v": cv2, "cache_lf": cl2,
            "ptab": np.ascontiguousarray(pt[sq].reshape(1, NSEQ_S * NPG)),
        })
    import os as _os
    if _os.environ.get("K_TRACE"):
        _r = run_bass_kernel_spmd(nc, in_maps, core_ids=list(range(NCORES)), trace=True)
        print("EXEC_NS", _r.exec_time_ns, flush=True)
        res = _r.results
    else:
        res = run_bass_kernel_spmd(nc, in_maps, core_ids=list(range(NCORES))).results
    def cat(name, b):
        if NLh == 0:
            return res[b][name].T
        return np.concatenate([res[b][name].T, res[b + B][name].T], axis=0)
    fb = 0 if NLh == 0 else B
    yp = np.stack([cat("yT", b) for b in range(B)])
    kp = np.stack([cat("kT", b).reshape(SEQ, 4, 64) for b in range(B)])[None]
    vp = np.stack([cat("vT", b).reshape(SEQ, 4, 64) for b in range(B)])[None]
    lp = np.stack([cat("lfT", b) for b in range(B)])[None]
    sp = np.stack([res[b + fb]["ssm"].T.reshape(32, 64, 128) for b in range(B)])[None]
    cp = np.stack([res[b + fb]["conv"].reshape(128, 24, 3).transpose(2, 1, 0).reshape(3, 3072) for b in range(B)])[None]
    ys = np.concatenate([res[c]["ysT"].T.reshape(NSEQ_S, DS, D) for c in range(NCORES)])
    ks = np.concatenate([res[c]["ksT"].T.reshape(NSEQ_S, DS, 4, 64) for c in range(NCORES)])[None]
    vs = np.concatenate([res[c]["vsT"].T.reshape(NSEQ_S, DS, 4, 64) for c in range(NCORES)])[None]
    ls = np.concatenate([res[c]["lfsT"].T.reshape(NSEQ_S, DS, 16) for c in range(NCORES)])[None]
    ss = np.concatenate([res[c]["ssms"].transpose(0, 2, 1).reshape(NSEQ_S, 32, 64, 128) for c in range(NCORES)])[None]
    cs = np.concatenate([res[c]["convs"].reshape(NSEQ_S, 128, 24, 3).transpose(0, 3, 2, 1).reshape(NSEQ_S, 3, 3072) for c in range(NCORES)])[None]
    o = (yp, ys, kp, vp, lp, sp, cp, ks, vs, ls, ss, cs)
    return tuple(np.ascontiguousarray(a, dtype=np.float32) for a in o)
```

```python
import contextlib
import numpy as np
import concourse.bass as bass
import concourse.mybir as mybir
from concourse.bass_utils import run_bass_kernel_spmd

F32 = mybir.dt.float32
BF16 = mybir.dt.bfloat16
I32 = mybir.dt.int32
AF = mybir.ActivationFunctionType
ALU = mybir.AluOpType

D = 1024
DFF = 2816
NF = 22
NSEQ_S = 4
LS = 8
NCORES = 8


class Res:
    __slots__ = ("name", "w", "r")

    def __init__(self, name=""):
        self.name = name
        self.w = None
        self.r = []


class DmaSlot:
    __slots__ = ("sem", "count", "name")

    def __init__(self, name):
        self.name = name
        self.sem = None
        self.count = 0


ENGS = ("pe", "act", "dve", "pool", "sp")
EPOCH = 30000


class Prog:
    def __init__(self):
        self.ops = {e: [] for e in ENGS}
        self.waited = {e: {} for e in ENGS}
        self.needed = {e: set() for e in ENGS}
        self.slots = []

    def slot(self, name=""):
        s = DmaSlot(name)
        self.slots.append(s)
        return s

    def _deps(self, eng, reads, writes):
        deps = []
        for r in reads:
            if r.w is not None:
                deps.append(r.w)
        for w in writes:
            if w.w is not None:
                deps.append(w.w)
            deps.extend(w.r)
        out = []
        wd = self.waited[eng]
        best = {}
        for d in deps:
            if d[0] == "dma":
                key = ("dma", id(d[1]))
                if key not in best or best[key][2] < d[2]:
                    best[key] = d
            else:
                if d[0] == eng and eng == "pe":
                    continue
                if d[0] not in best or best[d[0]][1] < d[1]:
                    best[d[0]] = d
        for key, d in best.items():
            v = d[2] if d[0] == "dma" else d[1]
            if wd.get(key, 0) >= v:
                continue
            wd[key] = v
            if d[0] != "dma":
                self.needed[d[0]].add(v)
            out.append(d)
        return out

    def op(self, eng, fn, reads=(), writes=()):
        waits = self._deps(eng, reads, writes)
        lst = self.ops[eng]
        lst.append([fn, waits, None, 0])
        tok = (eng, len(lst))
        for r in reads:
            r.r.append(tok)
        for w in writes:
            w.w = tok
            w.r = []
        return tok

    def dma(self, eng, fn, slot, reads=(), writes=()):
        if slot is None:
            key = writes[0] if len(writes) else reads[0]
            if not hasattr(self, "_auto"):
                self._auto = {}
            if id(key) not in self._auto:
                self._auto[id(key)] = self.slot("a" + key.name)
            slot = self._auto[id(key)]
        waits = self._deps(eng, reads, writes)
        slot.count += 16
        self.ops[eng].append([fn, waits, slot, slot.count])
        tok = ("dma", slot, slot.count)
        for r in reads:
            r.r.append(tok)
        for w in writes:
            w.w = tok
            w.r = []
        return tok

    def emit(self, nc, final_engine="sp"):
        with contextlib.ExitStack() as st:
            rank = {}
            nsig = {}
            for e in ENGS:
                flagged = sorted(self.needed[e])
                rank[e] = {idx: i + 1 for i, idx in enumerate(flagged)}
                nsig[e] = len(flagged)
            sems = {}
            for e in ENGS:
                n_ep = max(1, (nsig[e] + EPOCH - 1) // EPOCH)
                sems[e] = [st.enter_context(nc.semaphore(f"s_{e}{k}")) for k in range(n_ep)]
            for s in self.slots:
                if s.count > 0:
                    s.sem = st.enter_context(nc.semaphore(f"d_{s.name}"))
            fin = [("dma", s, s.count) for s in self.slots if s.count > 0]
            self.ops[final_engine].append([None, fin, None, 0])
            block = st.enter_context(nc.Block())

            def run(e):
                def body(eng):
                    for i, (fn, waits, slot, val) in enumerate(self.ops[e]):
                        for d in waits:
                            if d[0] == "dma":
                                eng.wait_ge(d[1].sem, d[2])
                            else:
                                sg = rank[d[0]][d[1]]
                                eng.wait_ge(sems[d[0]][(sg - 1) // EPOCH], (sg - 1) % EPOCH + 1)
                        if fn is None:
                            continue
                        ins = fn(eng)
                        if slot is not None:
                            ins.then_inc(slot.sem, 16)
                        else:
                            sg = rank[e].get(i + 1)
                            if sg is not None:
                                ins.then_inc(sems[e][(sg - 1) // EPOCH], 1)
                return body

            block.tensor(run("pe"))
            block.scalar(run("act"))
            block.vector(run("dve"))
            block.gpsimd(run("pool"))
            block.sync(run("sp"))
        return {e: len(self.ops[e]) for e in ENGS}, nsig


def _kc_tile(w_cols):
    return w_cols.reshape(8, 128, 128).transpose(1, 0, 2).reshape(128, 1024)


def _pad_cols(w, n):
    out = np.zeros((w.shape[0], n), np.float32)
    out[:, : w.shape[1]] = w
    return out


def _ffn_tiles(wg, wu, wd):
    tiles = []
    for j in range(NF):
        tiles.append(_kc_tile(wg[:, j * 128:(j + 1) * 128]))
        tiles.append(_kc_tile(wu[:, j * 128:(j + 1) * 128]))
    wdp = np.concatenate([wd, np.zeros((24 * 128 - DFF, D), np.float32)], 0)
    for m in range(8):
        blk = wdp[:, m * 128:(m + 1) * 128].reshape(24, 128, 128)
        for s in range(3):
            tiles.append(blk[s * 8:(s + 1) * 8].transpose(1, 0, 2).reshape(128, 1024))
    return tiles


O_Z, O_XBC, O_DT, O_Q, O_K, O_V, O_F, O_GS, O_GA = 0, 2048, 5120, 5152, 6176, 6432, 6688, 6704, 7728


def _q_perm_cols():
    cols = []
    for cp in range(2):
        for g in range(4):
            for half in range(2):
                kv = 2 * cp + half
                h = 4 * kv + g
                cols.extend(range(O_Q + h * 64, O_Q + (h + 1) * 64))
    return np.array(cols)


def build_weight_stream(p):
    t = []
    t += _ffn_tiles(p["ffn1_w_gate"], p["ffn1_w_up"], p["ffn1_w_down"])
    w_in = p["w_in"]
    for c in range(2):
        t.append(_kc_tile(w_in[:, O_K + c * 128: O_K + (c + 1) * 128]))
    for c in range(2):
        t.append(_kc_tile(w_in[:, O_V + c * 128: O_V + (c + 1) * 128]))
    t.append(_kc_tile(_pad_cols(w_in[:, O_F:O_F + 16], 128)))
    qc = w_in[:, _q_perm_cols()]
    for c in range(8):
        t.append(_kc_tile(qc[:, c * 128:(c + 1) * 128]))
    for c in range(24):
        t.append(_kc_tile(w_in[:, O_XBC + c * 128: O_XBC + (c + 1) * 128]))
    for c in range(16):
        t.append(_kc_tile(w_in[:, O_Z + c * 128: O_Z + (c + 1) * 128]))
    wsp = p["w_ssd_proj"]
    wap = p["w_attn_proj"]
    for m in range(8):
        blk = wsp[:, m * 128:(m + 1) * 128].reshape(16, 128, 128)
        for s in range(2):
            t.append(blk[s * 8:(s + 1) * 8].transpose(1, 0, 2).reshape(128, 1024))
        t.append(_kc_tile(w_in[:, O_GS + m * 128: O_GS + (m + 1) * 128]))
        ablk = wap[:, m * 128:(m + 1) * 128].reshape(16, 64, 128)
        for s in range(2):
            a = np.zeros((128, 1024), np.float32)
            a[:64] = ablk[s * 8:(s + 1) * 8].transpose(1, 0, 2).reshape(64, 1024)
            t.append(a)
        t.append(_kc_tile(w_in[:, O_GA + m * 128: O_GA + (m + 1) * 128]))
    wo = p["w_out"]
    for m in range(8):
        t.append(_kc_tile(wo[:, m * 128:(m + 1) * 128]))
    t += _ffn_tiles(p["ffn2_w_gate"], p["ffn2_w_up"], p["ffn2_w_down"])
    return np.ascontiguousarray(np.stack(t)).astype(np.float32)


NT = 68 + 37 + 16 + 48 + 8 + 68

CV_N1, CV_NM, CV_N2, CV_SN, CV_CW, CV_CB, CV_QN, CV_KN, CV_DS, CV_BF, CV_NBF = 0, 8, 16, 24, 40, 136, 160, 161, 162, 178, 179
NCV = 180


def build_cvec(p):
    cv = np.zeros((128, NCV), np.float32)
    cv[:, CV_N1:CV_N1 + 8] = p["ffn1_norm"].reshape(8, 128).T
    cv[:, CV_NM:CV_NM + 8] = p["mix_norm"].reshape(8, 128).T
    cv[:, CV_N2:CV_N2 + 8] = p["ffn2_norm"].reshape(8, 128).T
    cv[:, CV_SN:CV_SN + 16] = p["ssd_norm"].reshape(16, 128).T
    cw = p["conv_w"].reshape(4, 24, 128)
    cv[:, CV_CW:CV_CW + 96] = cw.transpose(2, 1, 0).reshape(128, 96)
    cv[:, CV_CB:CV_CB + 24] = p["conv_b"].reshape(24, 128).T
    cv[:, CV_QN] = np.tile(p["q_norm"], 2)
    cv[:, CV_KN] = np.tile(p["k_norm"], 2)
    cv[:, CV_DS:CV_DS + 16] = np.repeat(p["d_skip"], 64).reshape(16, 128).T
    cv[:16, CV_BF] = p["b_f"]
    return cv


def build_program(SEQ, NPG, NPOOL, do_sample=True):
    T = 512
    NB = SEQ // T
    NL = NB // 2
    HALF = (NB - NL) * T
    NTIL = SEQ // 128
    TS = NSEQ_S * LS
    nc = bass.Bass("TRN2", target_bir_lowering=False)

    def din(name, shape, dt=F32):
        return nc.dram_tensor(name, shape, dt, kind="ExternalInput").ap()

    def dout(name, shape, dt=F32):
        return nc.dram_tensor(name, shape, dt, kind="ExternalOutput").ap()

    xT_d = din("xT", [D, SEQ])
    xsT_d = din("xsT", [D, TS])
    wst_d = din("wst", [NT, 128, 1024])
    cvec_d = din("cvec", [128, NCV])
    rowc_d = din("rowc", [1, 96])
    wdt_d = din("wdt", [128, 8 * 32])
    ssm0_d = din("ssm0", [NSEQ_S, 128, 2048])
    conv0_d = din("conv0", [NSEQ_S, 128, 24 * 3])
    ckv_d = din("cache_kv", [NPOOL * 128, 512])
    clf_d = din("cache_lf", [NPOOL * 128, 16])
    pt_d = din("ptab", [1, NSEQ_S * NPG], I32)

    flg_d = din("flg", [128, 2])
    yT_o = dout("yT", [D, HALF])
    ysT_o = dout("ysT", [D, TS])
    kT_o = dout("kT", [256, HALF])
    vT_o = dout("vT", [256, HALF])
    lfT_o = dout("lfT", [16, HALF])
    ssm_o = dout("ssm", [128, 2048])
    conv_o = dout("conv", [128, 72])
    ksT_o = dout("ksT", [256, TS])
    vsT_o = dout("vsT", [256, TS])
    lfsT_o = dout("lfsT", [16, TS])
    ssms_o = dout("ssms", [NSEQ_S, 128, 2048])
    convs_o = dout("convs", [NSEQ_S, 128, 72])

    wbf_d = nc.dram_tensor("wbf", [NT, 128, 1024], BF16, kind="Internal").ap()

    P = Prog()
    st = contextlib.ExitStack()
    with st:
        def sb(name, shape, dt=F32):
            return st.enter_context(nc.sbuf_tensor("sb_" + name, shape, dt))

        def pst(name, shape, dt=F32):
            return st.enter_context(nc.psum_tensor("ps_" + name, shape, dt))

        cvec = sb("cvec", [128, NCV]); r_c = Res("const")
        rowc = sb("rowc", [128, 96])
        wdt32 = sb("wdt32", [128, 256]); wdt = sb("wdt", [128, 8, 32], BF16)
        ident = sb("ident", [128, 128], BF16)
        identf = sb("identf", [128, 128], F32)
        ones_d = sb("ones_d", [128, 128], BF16)
        ones_g = sb("ones_g", [128, 128], BF16)
        bd64 = sb("bd64", [128, 128], BF16)
        onesf = sb("onesf", [128, 128], F32)
        tri_f = sb("tri_f", [128, 128], F32)
        ustr_f = sb("ustr_f", [128, 128], F32)
        mask01 = sb("mask01", [128, 128], F32)
        epsc = sb("epsc", [128, 1]); onec = sb("onec", [128, 1])
        A_bc = sb("A_bc", [128, 32]); nbf = sb("nbf", [128, 1])
        s_c = P.slot("const")
        r_c1 = Res("c1"); r_c2 = Res("c2")
        flg = sb("flg", [128, 2])
        P.dma("sp", lambda e: e.dma_start(out=flg[:], in_=flg_d), None, writes=[r_c])
        P.dma("sp", lambda e: e.dma_start(out=cvec[:], in_=cvec_d), None, writes=[r_c])
        P.dma("sp", lambda e: e.dma_start(out=rowc[:], in_=rowc_d.partition_broadcast(128)), None, writes=[r_c1])
        P.dma("sp", lambda e: e.dma_start(out=wdt32[:], in_=wdt_d), None, writes=[r_c2])
        r_k = Res("consts2")
        P.op("pool", lambda e: e.memset(identf[:], 1.0), writes=[r_k])
        P.op("pool", lambda e: e.affine_select(out=identf[:], in_=identf[:], pattern=[[-1, 128]], compare_op=ALU.is_equal, fill=0.0, base=0, channel_multiplier=1), writes=[r_k])
        P.op("pool", lambda e: e.tensor_copy(out=ident[:], in_=identf[:]), writes=[r_k])
        P.op("pool", lambda e: e.memset(onesf[:], 1.0), writes=[r_k])
        P.op("pool", lambda e: e.memset(ones_d[:], 1.0 / 1024), writes=[r_k])
        P.op("pool", lambda e: e.memset(ones_g[:], 1.0 / 512), writes=[r_k])
        P.op("pool", lambda e: e.memset(bd64[:], 0.0), writes=[r_k])
        P.op("pool", lambda e: e.memset(bd64[0:64, 0:64], 1.0 / 64), writes=[r_k])
        P.op("pool", lambda e: e.memset(bd64[64:128, 64:128], 1.0 / 64), writes=[r_k])
        P.op("pool", lambda e: e.affine_select(out=tri_f[:], in_=onesf[:], pattern=[[1, 128]], compare_op=ALU.is_ge, fill=0.0, base=0, channel_multiplier=-1), writes=[r_k])
        P.op("pool", lambda e: e.tensor_copy(out=mask01[:], in_=tri_f[:]), writes=[r_k])
        P.op("pool", lambda e: e.affine_select(out=ustr_f[:], in_=onesf[:], pattern=[[-1, 128]], compare_op=ALU.is_gt, fill=0.0, base=0, channel_multiplier=1), writes=[r_k])
        P.op("pool", lambda e: e.memset(epsc[:], 1e-6), writes=[r_k])
        P.op("pool", lambda e: e.memset(onec[:], 1.0), writes=[r_k])
        P.op("act", lambda e: e.activation(out=A_bc[:], in_=rowc[:, 32:64], func=AF.Exp), reads=[r_c1], writes=[r_k])
        P.op("dve", lambda e: e.tensor_scalar(out=A_bc[:], in0=A_bc[:], scalar1=-1.0, scalar2=None, op0=ALU.mult), reads=[r_k], writes=[r_k])
        P.op("dve", lambda e: e.tensor_scalar(out=nbf[:], in0=cvec[:, CV_BF:CV_BF + 1], scalar1=-1.0, scalar2=None, op0=ALU.mult), reads=[r_c], writes=[r_k])
        P.op("dve", lambda e: e.tensor_copy(out=wdt[:].rearrange("p a b -> p (a b)"), in_=wdt32[:]), reads=[r_c2], writes=[r_k])
        CR = [r_c, r_k, r_c1]

        r_wbf = [Res(f"wbf{i}") for i in range(NT)]
        s_pre = [P.slot(f"pre{i}") for i in range(8)]
        for t in range(NT):
            P.dma("pool", lambda e, t=t: e.dma_start(out=wbf_d[t], in_=wst_d[t]), s_pre[(t // 8) % 8], writes=[r_wbf[t]])
        for t in range(NT):
            last = min(NT - 1, (t // 8) * 8 + 7)
            r_wbf[t].w = r_wbf[last].w
        NS = 6
        ring = [sb(f"ring{i}", [128, 1024], BF16) for i in range(NS)]
        r_ring = [Res(f"ring{i}") for i in range(NS)]
        s_ring = [P.slot(f"ring{i}") for i in range(NS)]
        wctr = [0]

        def wskip(k):
            wctr[0] += k

        def wtile():
            n = wctr[0]; wctr[0] += 1
            t = n % NT
            s = n % NS
            P.dma("sp", lambda e: e.dma_start(out=ring[s][:], in_=wbf_d[t]), s_ring[s], reads=[r_wbf[t]], writes=[r_ring[s]])
            return ring[s], r_ring[s]

        pAB = pst("pAB", [128, 1024]); pCD = pst("pCD", [128, 1024]); pEF = pst("pEF", [128, 1024])
        pTB = pst("pTB", [128, 2048], BF16)
        bank = {"A": pAB[:, 0:512], "B": pAB[:, 512:1024], "C": pCD[:, 0:512], "D": pCD[:, 512:1024],
                "E": pEF[:, 0:512], "F": pEF[:, 512:1024]}
        r_b = {k: Res("bank" + k) for k in "ABCDEF"}
        r_tb = Res("pTB")

        hT = sb("hT", [128, 8, T]); r_hT = Res("hT")
        xn = sb("xn", [128, 8, T], BF16); r_xn = Res("xn")
        arena = sb("arena", [128, NF, T], BF16)
        r_hid = [Res(f"hid{j}") for j in range(NF)]
        xc = sb("xc", [128, 24, T], BF16); r_xc = [Res(f"xc{c}") for c in range(24)]
        qT = sb("qT", [128, 8, T], BF16); r_qT = Res("qT")
        kTs = sb("kTs", [128, 2, SEQ], BF16); r_kT = Res("kT")
        Vs = sb("Vs", [128, NTIL, 4, 65], BF16); r_V = Res("V")
        cks = sb("cks", [128, NTIL, 16]); r_ck = Res("ck")
        biasb = sb("biasb", [128, NTIL, 16]); r_bias = Res("bias")
        yT = sb("yT", [128, 16, T], BF16); r_yT = [Res(f"yT{c}") for c in range(16)]
        oT = xc
        r_oT = Res("oT")
        merged = qT
        r_mg = Res("merged")
        hst = sb("hst", [128, 2048]); r_hst = Res("hst")
        stg = [sb(f"stg{i}", [128, T]) for i in range(3)]; r_stg = [Res(f"stg{i}") for i in range(3)]
        sqb = sb("sqb", [128, 4, T], BF16); r_sq = [Res(f"sq{i}") for i in range(4)]
        rstd = sb("rstd", [128, T]); r_rstd = Res("rstd")
        cstage = [sb(f"cst{i}", [128, T + 3 * NSEQ_S]) for i in range(2)]; r_cst = [Res(f"cst{i}") for i in range(2)]
        cacc = [sb(f"cacc{i}", [128, T]) for i in range(2)]; r_cacc = [Res(f"cacc{i}") for i in range(2)]
        ccar = sb("ccar", [128, 24, 3 * NSEQ_S]); r_ccar = [Res(f"ccar{c}") for c in range(24)]
        lfT = sb("lfT", [16, T]); r_lf = Res("lfT")
        cT = sb("cT", [16, T]); r_cT = Res("cT")
        ccarry = sb("ccarry", [16, 1]); r_cc = Res("ccarry")
        ones16 = sb("ones16", [16, T])
        dgl = sb("dgl", [16, 16]); r_dgl = Res("dgl")
        cref = sb("cref", [128, 16]); r_cref = Res("cref")
        dtt = sb("dtt", [128, 32]); at = sb("at", [128, 32]); acs = sb("acs", [128, 32]); tot = sb("tot", [128, 32])
        eacs = sb("eacs", [128, 32]); dte = sb("dte", [128, 32]); cdec = sb("cdec", [128, 32])
        r_ss = Res("ssdsmall")
        Dg = sb("Dg", [128, 8, 128]); r_Dg = Res("Dg")
        cbm = sb("cbm", [128, 128]); r_cbm = Res("cbm")
        tmpf = sb("tmpf", [128, 512]); r_tmpf = Res("tmpf")
        def asl(a, b):
            return arena[:, a:b, :].rearrange("p a b -> p (a b)")
        xdt = asl(0, 4); xw = asl(4, 8); ytok = asl(8, 12); hbf = asl(12, 16)
        Btok = asl(16, 17); MT = asl(17, 19); MTb = asl(19, 21); PT = asl(21, 22)
        r_xdt, r_xw, r_ytok, r_hbf, r_Btok, r_MT, r_MTb, r_PT = (Res(n) for n in ("xdt", "xw", "ytok", "hbf", "Btok", "MT", "MTb", "PT"))
        sTt = sb("sTt", [128, 512]); r_sTt = Res("sTt")
        bcs = stg[2][0:64, :]; r_bcs = r_stg[2]
        rec = tmpf; r_rec = r_tmpf

        s_in = P.slot("xin"); s_o = [P.slot(f"out{i}") for i in range(6)]
        P.op("pool", lambda e: e.memset(Vs[:].rearrange("p a b c -> p (a b c)"), 1.0), writes=[r_V])
        P.op("pool", lambda e: e.memset(ones16[:], 1.0), writes=[r_k])

        def mm(out, lhsT, rhs, start, stop, reads, writes):
            P.op("pe", lambda e: e.matmul(out, lhsT=lhsT, rhs=rhs, start=start, stop=stop), reads=reads, writes=writes)

        def rms_rstd(ps_ms, r_ps, n):
            P.op("act", lambda e: e.activation(out=rstd[:, :n], in_=ps_ms, func=AF.Ln, bias=epsc[:], scale=1.0), reads=[r_ps] + CR, writes=[r_rstd])
            P.op("act", lambda e: e.activation(out=rstd[:, :n], in_=rstd[:, :n], func=AF.Exp, scale=-0.5), reads=[r_rstd], writes=[r_rstd])

        def norm_to_xn(n, cvo):
            for c in range(8):
                P.op("act", lambda e, c=c: e.activation(out=sqb[:, c % 4, :n], in_=hT[:, c, :n], func=AF.Square), reads=[r_hT], writes=[r_sq[c % 4]])
                mm(bank["E"][:, :n], ones_d[:], sqb[:, c % 4, :n], c == 0, c == 7, [r_sq[c % 4]] + CR, [r_b["E"]])
            rms_rstd(bank["E"][:, :n], r_b["E"], n)
            for c in range(8):
                P.op("dve", lambda e, c=c: e.scalar_tensor_tensor(out=xn[:, c, :n], in0=hT[:, c, :n], scalar=cvec[:, cvo + c:cvo + c + 1], in1=rstd[:, :n], op0=ALU.mult, op1=ALU.mult),
                     reads=[r_hT, r_rstd] + CR, writes=[r_xn])

        def proj8(bk, n, w, rw, src=None, rsrc=None):
            for c in range(8):
                mm(bank[bk][:, :n], w[:, c * 128:(c + 1) * 128], xn[:, c, :n], c == 0, c == 7, [rw, r_xn], [r_b[bk]])

        def ffn(n, final_out=None):
            for j in range(NF):
                wg, rg = wtile(); wu, ru = wtile()
                pg, pu = ("A", "B") if j % 2 == 0 else ("C", "D")
                proj8(pg, n, wg, rg); proj8(pu, n, wu, ru)
                si = j % 2
                P.op("act", lambda e, pg=pg, si=si: e.activation(out=stg[si][:, :n], in_=bank[pg][:, :n], func=AF.Silu), reads=[r_b[pg]], writes=[r_stg[si]])
                P.op("dve", lambda e, pu=pu, si=si, j=j: e.tensor_tensor(out=arena[:, j, :n], in0=bank[pu][:, :n], in1=stg[si][:, :n], op=ALU.mult),
                     reads=[r_b[pu], r_stg[si]], writes=[r_hid[j]])
            for m in range(8):
                tl = [wtile() for _ in range(3)]
                pb = "AB"[m % 2]
                for kc in range(NF):
                    w, r = tl[kc // 8]
                    mm(bank[pb][:, :n], w[:, (kc % 8) * 128:(kc % 8 + 1) * 128], arena[:, kc, :n], kc == 0, kc == NF - 1, [r, r_hid[kc]], [r_b[pb]])
                P.op("dve", lambda e, m=m, pb=pb: e.scalar_tensor_tensor(out=hT[:, m, :n], in0=bank[pb][:, :n], scalar=0.5, in1=hT[:, m, :n], op0=ALU.mult, op1=ALU.add),
                     reads=[r_b[pb], r_hT], writes=[r_hT])

        def qknorm(bk, n, wcol, scale, out_bf, r_out, out_f32=None, r_f32=None, si=0, f32_view=None):
            P.op("act", lambda e: e.activation(out=stg[si][:, :n], in_=bank[bk][:, :n], func=AF.Copy), reads=[r_b[bk]], writes=[r_stg[si]])
            P.op("act", lambda e: e.activation(out=sqb[:, si, :n], in_=stg[si][:, :n], func=AF.Square), reads=[r_stg[si]], writes=[r_sq[si]])
            pb = "EF"[si]
            mm(bank[pb][:, :n], bd64[:], sqb[:, si, :n], True, True, [r_sq[si]] + CR, [r_b[pb]])
            rms_rstd(bank[pb][:, :n], r_b[pb], n)
            if out_f32 is not None:
                P.op("dve", lambda e: e.scalar_tensor_tensor(out=out_f32, in0=stg[si][:, :n], scalar=cvec[:, wcol:wcol + 1], in1=rstd[:, :n], op0=ALU.mult, op1=ALU.mult),
                     reads=[r_stg[si], r_rstd] + CR, writes=[r_f32])
                P.op("act", lambda e: e.activation(out=out_bf, in_=(out_f32 if f32_view is None else f32_view), func=AF.Copy, scale=scale), reads=[r_f32], writes=[r_out])
            else:
                P.op("dve", lambda e: e.scalar_tensor_tensor(out=stg[si][:, :n], in0=stg[si][:, :n], scalar=cvec[:, wcol:wcol + 1], in1=rstd[:, :n], op0=ALU.mult, op1=ALU.mult),
                     reads=[r_stg[si], r_rstd] + CR, writes=[r_stg[si]])
                P.op("act", lambda e: e.activation(out=out_bf, in_=stg[si][:, :n], func=AF.Copy, scale=scale), reads=[r_stg[si]], writes=[r_out])

        kvout = sb("kvout", [128, 2, T]); r_kvo = [Res(f"kvo{i}") for i in range(2)]

        def block(n, segs, sample, b):
            tb = b * T
            light = (not sample) and b < NL
            ob_ = (b - NL) * T
            if (not sample) and b == NL and NL > 0:
                P.op("dve", lambda e: e.tensor_scalar(out=hst[:], in0=hst[:], scalar1=flg[:, 0:1], scalar2=None, op0=ALU.mult), reads=[r_hst] + CR, writes=[r_hst])
                P.op("dve", lambda e: e.tensor_scalar(out=ccar[:, :, 0:3], in0=ccar[:, :, 0:3], scalar1=flg[:, 0:1], scalar2=None, op0=ALU.mult), reads=r_ccar + CR, writes=r_ccar)
            xsrc = xsT_d if sample else xT_d[:, tb:tb + n]
            P.dma("sp", lambda e: e.dma_start(out=hT[:, :, :n], in_=xsrc.rearrange("(c p) t -> p c t", p=128)), None, writes=[r_hT])
            import os as _os
            SSTOP = int(_os.environ.get("K_SSTOP", "99")) if sample else int(_os.environ.get("K_PSTOP", "99"))
            if sample:
                for si_ in range(NSEQ_S):
                    P.dma("sp", lambda e, si_=si_: e.dma_start(out=ccar[:, :, 3 * si_:3 * si_ + 3], in_=conv0_d[si_].rearrange("p (c l) -> p c l", l=3)), None, writes=r_ccar)
            if SSTOP <= 0:
                return
            norm_to_xn(n, CV_N1)
            ffn(n)
            if SSTOP <= 1:
                return
            norm_to_xn(n, CV_NM)
            ko, vo, lo = (ksT_o, vsT_o, lfsT_o) if sample else ((None, None, None) if light else (kT_o[:, ob_:ob_ + n], vT_o[:, ob_:ob_ + n], lfT_o[:, ob_:ob_ + n]))
            kdst = kTs[:, :, SEQ - TS:SEQ] if False else None
            for c in range(2):
                w, rw = wtile(); bk = "AB"[c]
                proj8(bk, n, w, rw)
                kb = (kTs[:, c, tb:tb + n] if not sample else ksb[:, c, :, 0:LS])
                kvo_v = kvout[:, c, :n] if not sample else kvout[:, c, :n].rearrange("p (s l) -> p s l", s=NSEQ_S)
                qknorm(bk, n, CV_KN, 1.0, kb, r_kT, out_f32=kvout[:, c, :n], r_f32=r_kvo[c], si=c, f32_view=(kvo_v if sample else None))
                if not light:
                    P.dma("sp", lambda e, c=c: e.dma_start(out=ko[c * 128:(c + 1) * 128, :], in_=kvout[:, c, :n]), None, reads=[r_kvo[c]])
            for c in range(2):
                w, rw = wtile(); bk = "AB"[c]
                proj8(bk, n, w, rw)
                P.op("act", lambda e, c=c, bk=bk: e.activation(out=kvout[:, c, :n], in_=bank[bk][:, :n], func=AF.Copy), reads=[r_b[bk]], writes=[r_kvo[c]])
                if not light:
                    P.dma("sp", lambda e, c=c: e.dma_start(out=vo[c * 128:(c + 1) * 128, :], in_=kvout[:, c, :n]), None, reads=[r_kvo[c]])
                P.op("dve", lambda e, c=c: e.tensor_copy(out=sqb[:, c, :n], in_=kvout[:, c, :n]), reads=[r_kvo[c]], writes=[r_sq[c]])
                for (off, L, sq_) in segs:
                    P.op("pe", lambda e, c=c, off=off, L=L: e.transpose(pTB[:L, 0:128], sqb[:, c, off:off + L], ident[:]), reads=[r_sq[c]] + CR, writes=[r_tb])
                    if sample:
                        vdst = Vsm[:L, sq_, 2 * c:2 * c + 2, 0:64]
                    else:
                        vdst = Vs[:L, (tb + off) // 128, 2 * c:2 * c + 2, 0:64]
                    P.op("act", lambda e, L=L, vdst=vdst: e.activation(out=vdst, in_=pTB[:L, 0:128].rearrange("p (a b) -> p a b", a=2), func=AF.Copy), reads=[r_tb], writes=[r_V])
            w, rw = wtile()
            proj8("A", n, w, rw)
            P.op("act", lambda e: e.activation(out=lfT[:, :n], in_=bank["A"][:16, :n], func=AF.Exp, bias=nbf[:16, :], scale=-1.0), reads=[r_b["A"]] + CR, writes=[r_lf])
            P.op("act", lambda e: e.activation(out=lfT[:, :n], in_=lfT[:, :n], func=AF.Ln, bias=onec[:16, :], scale=1.0), reads=[r_lf], writes=[r_lf])
            P.op("dve", lambda e: e.tensor_scalar(out=lfT[:, :n], in0=lfT[:, :n], scalar1=-1.0, scalar2=None, op0=ALU.mult), reads=[r_lf], writes=[r_lf])
            if not light:
                P.dma("sp", lambda e: e.dma_start(out=lo, in_=lfT[:, :n]), None, reads=[r_lf])
            if not sample:
                if b == 0:
                    P.op("dve", lambda e: e.memset(ccarry[:], 0.0), writes=[r_cc])
                P.op("dve", lambda e: e.tensor_tensor_scan(out=cT[:, :n], data0=ones16[:, :n], data1=lfT[:, :n], initial=ccarry[:, 0:1], op0=ALU.mult, op1=ALU.add),
                     reads=[r_lf, r_cc], writes=[r_cT])
                P.op("dve", lambda e: e.tensor_copy(out=ccarry[:], in_=cT[:, n - 1:n]), reads=[r_cT], writes=[r_cc])
                for (off, L, sq_) in segs:
                    ti = (tb + off) // 128
                    mm(bank["B"][:L, 0:16], cT[:, off:off + L], identf[:16, :16], True, True, [r_cT] + CR, [r_b["B"]])
                    P.op("dve", lambda e, ti=ti, L=L: e.tensor_copy(out=cks[:L, ti, :], in_=bank["B"][:L, 0:16]), reads=[r_b["B"]], writes=[r_ck])
            if light:
                wskip(8)
            for c in range(0 if light else 8):
                w, rw = wtile(); bk = "AB"[c % 2]
                proj8(bk, n, w, rw)
                qknorm(bk, n, CV_QN, 0.125, qT[:, c, :n], r_qT, si=c % 2)
            for c in range(24):
                w, rw = wtile(); bk = "AB"[c % 2]; ci = c % 2
                proj8(bk, n, w, rw)
                cs = cstage[ci]
                nsq = len(segs) if sample else 1
                Ls = n // nsq
                csv = cs[:, :nsq * (Ls + 3)].rearrange("p (s l) -> p s l", s=nsq)
                if sample:
                    P.op("pool", lambda e, c=c, csv=csv, nsq=nsq: e.tensor_copy(out=csv[:, :, 0:3], in_=ccar[:, c, :3 * nsq].rearrange("p (s l) -> p s l", s=nsq)), reads=[r_ccar[c]], writes=[r_cst[ci]])
                elif b == 0:
                    P.op("pool", lambda e, csv=csv: e.memset(csv[:, :, 0:3], 0.0), writes=[r_cst[ci]])
                else:
                    P.op("pool", lambda e, c=c, csv=csv: e.tensor_copy(out=csv[:, 0, 0:3], in_=ccar[:, c, 0:3]), reads=[r_ccar[c]], writes=[r_cst[ci]])
                P.op("act", lambda e, bk=bk, csv=csv, nsq=nsq, Ls=Ls: e.activation(out=csv[:, :, 3:3 + Ls], in_=bank[bk][:, :n].rearrange("p (s l) -> p s l", s=nsq), func=AF.Copy),
                     reads=[r_b[bk]], writes=[r_cst[ci]])
                P.op("pool", lambda e, c=c, csv=csv, nsq=nsq, Ls=Ls: e.tensor_copy(out=ccar[:, c, :3 * nsq].rearrange("p (s l) -> p s l", s=nsq), in_=csv[:, :, Ls:Ls + 3]),
                     reads=[r_cst[ci]], writes=[r_ccar[c]])
                ca = cacc[ci][:, :n].rearrange("p (s l) -> p s l", s=nsq)
                wc = CV_CW + 4 * c
                P.op("dve", lambda e, c=c, ca=ca, csv=csv, Ls=Ls, wc=wc: e.tensor_scalar(out=ca, in0=csv[:, :, 3:3 + Ls], scalar1=cvec[:, wc + 3:wc + 4], scalar2=cvec[:, CV_CB + c:CV_CB + c + 1], op0=ALU.mult, op1=ALU.add),
                     reads=[r_cst[ci]] + CR, writes=[r_cacc[ci]])
                for j in range(3):
                    P.op("dve", lambda e, j=j, ca=ca, csv=csv, Ls=Ls, wc=wc: e.scalar_tensor_tensor(out=ca, in0=csv[:, :, j:j + Ls], scalar=cvec[:, wc + j:wc + j + 1], in1=ca, op0=ALU.mult, op1=ALU.add),
                         reads=[r_cst[ci], r_cacc[ci]] + CR, writes=[r_cacc[ci]])
                P.op("act", lambda e, c=c, ci=ci: e.activation(out=xc[:, c, :n], in_=cacc[ci][:, :n], func=AF.Silu), reads=[r_cacc[ci]], writes=[r_xc[c]])
            if sample:
                for si_ in range(NSEQ_S):
                    P.dma("sp", lambda e, si_=si_: e.dma_start(out=convs_o[si_].rearrange("p (c l) -> p c l", l=3), in_=ccar[:, :, 3 * si_:3 * si_ + 3]), None, reads=r_ccar)
            elif b == NB - 1:
                P.dma("sp", lambda e: e.dma_start(out=conv_o.rearrange("p (c l) -> p c l", l=3), in_=ccar[:, :, 0:3]), None, reads=r_ccar)

            if SSTOP <= 2:
                return
            for (off, L, sq_) in segs:
                first = (b == 0 and off == 0) if not sample else True
                if sample:
                    P.dma("sp", lambda e, sq_=sq_: e.dma_start(out=hst[:], in_=ssm0_d[sq_]), None, writes=[r_hst])
                elif first:
                    P.op("pool", lambda e: e.memset(hst[:], 0.0), writes=[r_hst])
                if not light:
                    P.op("act", lambda e: e.activation(out=hbf, in_=hst[:], func=AF.Copy), reads=[r_hst], writes=[r_hbf])
                for c in range(8):
                    mm(bank["E"][:L, 0:32], xn[:, c, off:off + L], wdt[:, c, :], c == 0, c == 7, [r_xn] + CR, [r_b["E"]])
                P.op("dve", lambda e, L=L: e.tensor_tensor(out=dtt[:L, :], in0=bank["E"][:L, 0:32], in1=rowc[:L, 0:32], op=ALU.add), reads=[r_b["E"]] + CR, writes=[r_ss])
                P.op("act", lambda e, L=L: e.activation(out=dtt[:L, :], in_=dtt[:L, :], func=AF.Exp), reads=[r_ss], writes=[r_ss])
                P.op("act", lambda e, L=L: e.activation(out=dtt[:L, :], in_=dtt[:L, :], func=AF.Ln, bias=onec[:L, :], scale=1.0), reads=[r_ss] + CR, writes=[r_ss])
                P.op("dve", lambda e, L=L: e.tensor_tensor(out=at[:L, :], in0=dtt[:L, :], in1=A_bc[:L, :], op=ALU.mult), reads=[r_ss] + CR, writes=[r_ss])
                mm(bank["F"][:L, 0:32], tri_f[:L, :L], at[:L, :], True, True, [r_ss] + CR, [r_b["F"]])
                mm(bank["E"][:, 32:64], onesf[:L, :], at[:L, :], True, True, [r_ss] + CR, [r_b["E"]])
                P.op("dve", lambda e, L=L: e.tensor_copy(out=acs[:L, :], in_=bank["F"][:L, 0:32]), reads=[r_b["F"]], writes=[r_ss])
                P.op("dve", lambda e: e.tensor_copy(out=tot[:], in_=bank["E"][:, 32:64]), reads=[r_b["E"]], writes=[r_ss])
                P.op("dve", lambda e, L=L: e.tensor_tensor(out=dte[:L, :], in0=tot[:L, :], in1=acs[:L, :], op=ALU.subtract), reads=[r_ss], writes=[r_ss])
                P.op("act", lambda e, L=L: e.activation(out=dte[:L, :], in_=dte[:L, :], func=AF.Exp), reads=[r_ss], writes=[r_ss])
                P.op("act", lambda e, L=L: e.activation(out=eacs[:L, :], in_=acs[:L, :], func=AF.Exp), reads=[r_ss], writes=[r_ss])
                P.op("act", lambda e: e.activation(out=cdec[:], in_=tot[:], func=AF.Exp), reads=[r_ss], writes=[r_ss])
                for c in range(16):
                    P.op("pe", lambda e, c=c, off=off, L=L: e.transpose(pTB[:L, c * 128:(c + 1) * 128], xc[:, c, off:off + L], ident[:]), reads=[r_xc[c]] + CR, writes=[r_tb])
                x3 = pTB[:L, :].rearrange("p (h d) -> p h d", d=64)
                P.op("dve", lambda e, L=L, x3=x3: e.tensor_tensor(out=xdt[:L, :].rearrange("p (h d) -> p h d", d=64), in0=x3, in1=dtt[:L, :].unsqueeze(2).to_broadcast([L, 32, 64]), op=ALU.mult),
                     reads=[r_tb, r_ss], writes=[r_xdt])
                P.op("dve", lambda e, L=L: e.tensor_tensor(out=xw[:L, :].rearrange("p (h d) -> p h d", d=64), in0=xdt[:L, :].rearrange("p (h d) -> p h d", d=64), in1=dte[:L, :].unsqueeze(2).to_broadcast([L, 32, 64]), op=ALU.mult),
                     reads=[r_xdt, r_ss], writes=[r_xw])
                for g in range(4):
                    P.op("pe", lambda e, g=g, off=off, L=L: e.transpose(pTB[:L, g * 128:(g + 1) * 128], xc[:, 16 + g, off:off + L], ident[:]), reads=[r_xc[16 + g], r_xdt] + CR, writes=[r_tb])
                P.op("act", lambda e, L=L: e.activation(out=Btok[:L, :], in_=pTB[:L, 0:512], func=AF.Copy), reads=[r_tb], writes=[r_Btok])
                for g in range(4):
                    Bt = xc[:, 16 + g, off:off + L]; Ct = xc[:, 20 + g, off:off + L]
                    rB, rC = r_xc[16 + g], r_xc[20 + g]
                    if light:
                        mm(bank["E"][:, :], Btok[:L, g * 128:(g + 1) * 128], xw[:L, g * 512:(g + 1) * 512], True, True, [r_Btok, r_xw], [r_b["E"]])
                        hs3 = hst[:, g * 512:(g + 1) * 512].rearrange("p (h d) -> p h d", d=64)
                        P.op("pool", lambda e, g=g, hs3=hs3: e.tensor_tensor(out=hs3, in0=hs3, in1=cdec[:, g * 8:(g + 1) * 8].unsqueeze(2).to_broadcast([128, 8, 64]), op=ALU.mult),
                             reads=[r_ss], writes=[r_hst])
                        P.op("dve", lambda e, g=g: e.tensor_tensor(out=hst[:, g * 512:(g + 1) * 512], in0=hst[:, g * 512:(g + 1) * 512], in1=bank["E"][:, :], op=ALU.add),
                             reads=[r_b["E"]], writes=[r_hst])
                        continue
                    mm(bank["F"][:L, :L], Bt, Ct, True, True, [rB, rC], [r_b["F"]])
                    P.op("dve", lambda e, L=L: e.tensor_tensor(out=cbm[:L, :L], in0=bank["F"][:L, :L], in1=mask01[:L, :L], op=ALU.mult), reads=[r_b["F"]] + CR, writes=[r_cbm])
                    P.op("pool", lambda e, g=g, L=L: e.tensor_tensor(out=Dg[:L, :, :L], in0=at[:L, g * 8:(g + 1) * 8].unsqueeze(2).to_broadcast([L, 8, L]), in1=tri_f[:L, :L].unsqueeze(1).to_broadcast([L, 8, L]), op=ALU.mult),
                         reads=[r_ss] + CR, writes=[r_Dg])
                    segp = pAB[:L, :].rearrange("p (h l) -> p h l", l=128)
                    if L == 128:
                        for hh in range(2):
                            P.op("pe", lambda e, hh=hh, L=L, segp=segp: e.matmul(segp[:, hh * 4:(hh + 1) * 4, :L], lhsT=ustr_f[:L, :L], rhs=Dg[:L, hh * 4:(hh + 1) * 4, :L], start=True, stop=True),
                                 reads=[r_Dg] + CR, writes=[r_b["AB"[hh]]])
                    else:
                        for hh in range(8):
                            P.op("pe", lambda e, hh=hh, L=L, segp=segp: e.matmul(segp[:, hh, :L], lhsT=ustr_f[:L, :L], rhs=Dg[:L, hh, :L], start=True, stop=True),
                                 reads=[r_Dg] + CR, writes=[r_b["AB"[hh // 4]]])
                    MT3 = MT[:L, :].rearrange("p (h l) -> p h l", l=128)
                    MTb3 = MTb[:L, :].rearrange("p (h l) -> p h l", l=128)
                    P.op("act", lambda e, L=L, segp=segp, MT3=MT3: e.activation(out=MT3[:, :, :L], in_=segp[:, :, :L], func=AF.Exp), reads=[r_b["A"], r_b["B"]], writes=[r_MT])
                    P.op("dve", lambda e, L=L, MT3=MT3, MTb3=MTb3: e.tensor_tensor(out=MTb3[:, :, :L], in0=MT3[:, :, :L], in1=cbm[:L, :L].unsqueeze(1).to_broadcast([L, 8, L]), op=ALU.mult),
                         reads=[r_MT, r_cbm], writes=[r_MTb])
                    for hh in range(8):
                        h = g * 8 + hh
                        mm(bank["C"][:L, hh * 64:(hh + 1) * 64], MTb3[:, hh, :L], xdt[:L, h * 64:(h + 1) * 64], True, True, [r_MTb, r_xdt], [r_b["C"]])
                    mm(bank["D"][:L, :], Ct, hbf[:, g * 512:(g + 1) * 512], True, True, [rC, r_hbf], [r_b["D"]])
                    P.op("dve", lambda e, g=g, L=L: e.tensor_tensor(out=tmpf[:L, :].rearrange("p (h d) -> p h d", d=64), in0=bank["D"][:L, :].rearrange("p (h d) -> p h d", d=64),
                                                                   in1=eacs[:L, g * 8:(g + 1) * 8].unsqueeze(2).to_broadcast([L, 8, 64]), op=ALU.mult),
                         reads=[r_b["D"], r_ss], writes=[r_tmpf])
                    P.op("dve", lambda e, g=g, L=L: e.tensor_tensor(out=ytok[:L, g * 512:(g + 1) * 512], in0=bank["C"][:L, :], in1=tmpf[:L, :], op=ALU.add),
                         reads=[r_b["C"], r_tmpf], writes=[r_ytok])
                    mm(bank["E"][:, :], Btok[:L, g * 128:(g + 1) * 128], xw[:L, g * 512:(g + 1) * 512], True, True, [r_Btok, r_xw], [r_b["E"]])
                    hs3 = hst[:, g * 512:(g + 1) * 512].rearrange("p (h d) -> p h d", d=64)
                    P.op("pool", lambda e, g=g, hs3=hs3: e.tensor_tensor(out=hs3, in0=hs3, in1=cdec[:, g * 8:(g + 1) * 8].unsqueeze(2).to_broadcast([128, 8, 64]), op=ALU.mult),
                         reads=[r_ss, r_hbf], writes=[r_hst])
                    P.op("dve", lambda e, g=g: e.tensor_tensor(out=hst[:, g * 512:(g + 1) * 512], in0=hst[:, g * 512:(g + 1) * 512], in1=bank["E"][:, :], op=ALU.add),
                         reads=[r_b["E"]], writes=[r_hst])
                for c in range(0 if light else 16):
                    P.op("pe", lambda e, c=c, L=L: e.transpose(pTB[:, c * 128:c * 128 + L], ytok[:L, c * 128:(c + 1) * 128], ident[:L, :L]), reads=[r_ytok] + CR, writes=[r_tb])
                    P.op("dve", lambda e, c=c, off=off, L=L: e.scalar_tensor_tensor(out=yT[:, c, off:off + L], in0=xc[:, c, off:off + L], scalar=cvec[:, CV_DS + c:CV_DS + c + 1],
                                                                                 in1=pTB[:, c * 128:c * 128 + L], op0=ALU.mult, op1=ALU.add),
                         reads=[r_tb, r_xc[c]] + CR, writes=[r_yT[c]])
                if sample:
                    P.dma("sp", lambda e, sq_=sq_: e.dma_start(out=ssms_o[sq_], in_=hst[:]), None, reads=[r_hst])
                elif b == NB - 1 and off + L == n:
                    P.dma("sp", lambda e: e.dma_start(out=ssm_o, in_=hst[:]), None, reads=[r_hst])
            if SSTOP <= 3:
                return
            if light:
                wskip(16 + 48 + 8 + 68)
                return
            for c in range(16):
                w, rw = wtile(); bk = "AB"[c % 2]; si = c % 2
                proj8(bk, n, w, rw)
                P.op("act", lambda e, bk=bk, si=si: e.activation(out=stg[si][:, :n], in_=bank[bk][:, :n], func=AF.Silu), reads=[r_b[bk]], writes=[r_stg[si]])
                P.op("dve", lambda e, c=c, si=si: e.tensor_tensor(out=yT[:, c, :n], in0=yT[:, c, :n], in1=stg[si][:, :n], op=ALU.mult), reads=[r_stg[si], r_yT[c]], writes=[r_yT[c]])
                P.op("act", lambda e, c=c: e.activation(out=sqb[:, c % 4, :n], in_=yT[:, c, :n], func=AF.Square), reads=[r_yT[c]], writes=[r_sq[c % 4]])
                mm(bank["E"][:, :n], ones_g[:], sqb[:, c % 4, :n], c % 4 == 0, c % 4 == 3, [r_sq[c % 4]] + CR, [r_b["E"]])
                if c % 4 == 3:
                    rms_rstd(bank["E"][:, :n], r_b["E"], n)
                    for cc in range(c - 3, c + 1):
                        P.op("dve", lambda e, cc=cc: e.scalar_tensor_tensor(out=yT[:, cc, :n], in0=yT[:, cc, :n], scalar=cvec[:, CV_SN + cc:CV_SN + cc + 1], in1=rstd[:, :n], op0=ALU.mult, op1=ALU.mult),
                             reads=[r_rstd, r_yT[cc]] + CR, writes=[r_yT[cc]])

            if SSTOP <= 4:
                return
            attention(n, segs, sample, b)

            if SSTOP <= 5:
                return
            oT3 = oT[0:64, 0:16, :]
            for m in range(8):
                w0, r0 = wtile(); w1, r1 = wtile()
                for c in range(16):
                    w, r = (w0, r0) if c < 8 else (w1, r1)
                    mm(bank["A"][:, :n], w[:, (c % 8) * 128:(c % 8 + 1) * 128], yT[:, c, :n], c == 0, c == 15, [r, r_yT[c]], [r_b["A"]])
                wg_, rg_ = wtile()
                proj8("C", n, wg_, rg_)
                P.op("act", lambda e: e.activation(out=stg[0][:, :n], in_=bank["C"][:, :n], func=AF.Sigmoid), reads=[r_b["C"]], writes=[r_stg[0]])
                P.op("dve", lambda e: e.tensor_tensor(out=stg[2][:, :n], in0=bank["A"][:, :n], in1=stg[0][:, :n], op=ALU.mult), reads=[r_b["A"], r_stg[0]], writes=[r_stg[2]])
                w0, r0 = wtile(); w1, r1 = wtile()
                for h in range(16):
                    w, r = (w0, r0) if h < 8 else (w1, r1)
                    mm(bank["B"][:, :n], w[0:64, (h % 8) * 128:(h % 8 + 1) * 128], oT3[:, h, :n], h == 0, h == 15, [r, r_oT], [r_b["B"]])
                wg_, rg_ = wtile()
                proj8("D", n, wg_, rg_)
                P.op("act", lambda e: e.activation(out=stg[1][:, :n], in_=bank["D"][:, :n], func=AF.Sigmoid), reads=[r_b["D"]], writes=[r_stg[1]])
                P.op("dve", lambda e: e.tensor_tensor(out=stg[1][:, :n], in0=bank["B"][:, :n], in1=stg[1][:, :n], op=ALU.mult), reads=[r_b["B"], r_stg[1]], writes=[r_stg[1]])
                P.op("dve", lambda e, m=m: e.tensor_tensor(out=merged[:, m, :n], in0=stg[1][:, :n], in1=stg[2][:, :n], op=ALU.add), reads=[r_stg[1], r_stg[2], r_qT], writes=[r_mg])
            for m in range(8):
                w, rw = wtile(); bk = "AB"[m % 2]
                for c in range(8):
                    mm(bank[bk][:, :n], w[:, c * 128:(c + 1) * 128], merged[:, c, :n], c == 0, c == 7, [rw, r_mg], [r_b[bk]])
                P.op("dve", lambda e, m=m, bk=bk: e.tensor_tensor(out=hT[:, m, :n], in0=hT[:, m, :n], in1=bank[bk][:, :n], op=ALU.add), reads=[r_b[bk], r_hT], writes=[r_hT])
            if SSTOP <= 6:
                return
            norm_to_xn(n, CV_N2)
            ffn(n)
            ydst = ysT_o if sample else yT_o[:, ob_:ob_ + n]
            P.dma("sp", lambda e: e.dma_start(out=ydst.rearrange("(c p) t -> p c t", p=128), in_=hT[:, :, :n]), None, reads=[r_hT])

        def attn_tile(kv, qcols_ap, nqc, kt_ap, nk, v_ap, bias_ap, first, last, diag, acc_bank, rd):
            pass

        sT_bufs = [(sTt, r_sTt), (stg[0], r_stg[0]), (stg[1], r_stg[1])]
        PT_bufs = [(PT, r_PT), (asl(0, 1), Res("PT1")), (asl(1, 2), Res("PT2")), (asl(2, 3), Res("PT3"))]
        att_it = [0]

        def attention(n, segs, sample, b):
            tb = b * T
            for (off, L, sq_) in segs:
                if not sample:
                    qi = (tb + off) // 128
                    P.op("dve", lambda e, off=off, L=L: e.tensor_scalar(out=dgl[:], in0=identf[:16, :16], scalar1=cT[:, off + L - 1:off + L], scalar2=None, op0=ALU.mult), reads=[r_cT] + CR, writes=[r_dgl])
                    mm(bank["F"][:, 0:16], onesf[:16, :], dgl[:], True, True, [r_dgl] + CR, [r_b["F"]])
                    P.op("dve", lambda e: e.tensor_copy(out=cref[:], in_=bank["F"][:, 0:16]), reads=[r_b["F"]], writes=[r_cref])
                    nkt = qi + 1
                    P.op("dve", lambda e, nkt=nkt: e.tensor_tensor(out=biasb[:, :nkt, :], in0=cref[:].unsqueeze(1).to_broadcast([128, nkt, 16]), in1=cks[:, :nkt, :], op=ALU.subtract),
                         reads=[r_cref, r_ck], writes=[r_bias])
                    if NL > 0:
                        P.op("dve", lambda e: e.tensor_scalar(out=biasb[:, :NL * 4, :], in0=biasb[:, :NL * 4, :], scalar1=flg[:, 1:2], scalar2=None, op0=ALU.add),
                             reads=[r_bias] + CR, writes=[r_bias])
                    for kv in range(4):
                        half = kv % 2; cp = kv // 2
                        ob = "CD"[kv % 2]
                        qap = qT[half * 64:(half + 1) * 64, 4 * cp:4 * cp + 4, off:off + L]
                        def emit_st(kt, half=half, cp=cp, qap=qap, kv=kv, L=L, qi=qi):
                            sbk = "AB"[kt % 2]
                            dg = (kt == qi)
                            P.op("pe", lambda e: e.matmul(bank[sbk][:, :4 * L].rearrange("p (g t) -> p g t", g=4), lhsT=kTs[half * 64:(half + 1) * 64, cp, kt * 128:(kt + 1) * 128], rhs=qap, start=True, stop=True),
                                 reads=[r_kT, r_qT], writes=[r_b[sbk]])
                            it = att_it[0]; att_it[0] += 1
                            sTb, r_sTb = sT_bufs[it % 3]
                            PTb, r_PTb = PT_bufs[it % 4]
                            P.op("dve", lambda e: e.scalar_tensor_tensor(out=sTb[:, :4 * L].rearrange("p (g t) -> p g t", g=4), in0=bank[sbk][:, :4 * L].rearrange("p (g t) -> p g t", g=4), scalar=1.0,
                                                                         in1=biasb[:, kt, 4 * kv:4 * kv + 4].unsqueeze(2).to_broadcast([128, 4, L]), op0=ALU.mult, op1=ALU.add),
                                 reads=[r_b[sbk], r_bias], writes=[r_sTb])
                            P.op("act", lambda e: e.activation(out=PTb[:, :4 * L], in_=sTb[:, :4 * L], func=AF.Exp), reads=[r_sTb], writes=[r_PTb])
                            if dg:
                                P.op("pool", lambda e: e.affine_select(out=PTb[:, :4 * L].rearrange("p (g t) -> p g t", g=4), in_=PTb[:, :4 * L].rearrange("p (g t) -> p g t", g=4), pattern=[[0, 4], [1, L]], compare_op=ALU.is_ge, fill=0.0, base=0, channel_multiplier=-1),
                                     reads=[r_PTb], writes=[r_PTb])
                            return PTb, r_PTb
                        pend = emit_st(0)
                        for kt in range(nkt):
                            cur = pend
                            if kt + 1 < nkt:
                                pend = emit_st(kt + 1)
                            mm(bank[ob][0:65, :4 * L], Vs[:, kt, kv, :], cur[0][:, :4 * L], kt == 0, kt == nkt - 1, [r_V, cur[1]], [r_b[ob]])
                        finish_head(kv, ob, off, L, 4 * L)
                else:
                    sample_attention(off, L, sq_)

        def finish_head(kv, ob, off, L, ncol):
            P.op("dve", lambda e: e.reciprocal(out=rec[64:65, :ncol], in_=bank[ob][64:65, :ncol]), reads=[r_b[ob]], writes=[r_rec])
            mm(bank["E"][0:64, :ncol], onesf[64:65, 0:64], rec[64:65, :ncol], True, True, [r_rec] + CR, [r_b["E"]])
            P.op("act", lambda e: e.activation(out=bcs[:, :ncol], in_=bank["E"][0:64, :ncol], func=AF.Copy), reads=[r_b["E"]], writes=[r_bcs])
            P.op("dve", lambda e: e.tensor_tensor(out=oT[0:64, 4 * kv:4 * kv + 4, off:off + L], in0=bank[ob][0:64, :ncol].rearrange("p (g t) -> p g t", g=4), in1=bcs[:, :ncol].rearrange("p (g t) -> p g t", g=4), op=ALU.mult),
                 reads=[r_b[ob], r_bcs] + r_xc, writes=[r_oT])

        if do_sample:
            qS = sb("qS", [128, NSEQ_S, 8, LS], BF16); r_qS = Res("qS")
            ksb = sb("ksb", [128, 2, NSEQ_S, 128], BF16)
            P.op("pool", lambda e: e.memset(ksb[:].rearrange("p a b c -> p (a b c)"), 0.0), writes=[r_kT])
            Vsm = sb("Vsm", [128, NSEQ_S, 4, 65], BF16)
            P.op("pool", lambda e: e.memset(Vsm[:].rearrange("p a b c -> p (a b c)"), 1.0), writes=[r_V])
            ptab = sb("ptab", [128, NSEQ_S * NPG], I32); r_pt = Res("ptab")
            ridx = ptab
            piota = sb("piota", [128, 1], I32)
            P.dma("sp", lambda e: e.dma_start(out=ptab[:], in_=pt_d.partition_broadcast(128)), None, writes=[r_pt])
            P.op("pool", lambda e: e.iota(piota[:], pattern=[[0, 1]], base=0, channel_multiplier=1), writes=[r_pt])
            P.op("pool", lambda e: e.tensor_scalar(out=ridx[:], in0=ptab[:], scalar1=128, scalar2=piota[:, 0:1], op0=ALU.mult, op1=ALU.add), reads=[r_pt], writes=[r_pt])
            import os as _os
            NPB = int(_os.environ.get("K_NPB", "4"))
            r_kvr = [Res(f"kvr{i}") for i in range(2)]; r_vpg = [Res(f"vpg{i}") for i in range(2)]
            r_lpg = [Res(f"lpg{i}") for i in range(2)]; r_ktp = Res("ktp")
            PGE = NPB * 256; VGE = NPB * 4 * 65
            if 2 * SEQ >= 5 * PGE + 2 * VGE:
                kflat = kTs[:].rearrange("p a b -> p (a b)")
                kvr = [kflat[:, 2 * i * PGE:2 * (i + 1) * PGE].rearrange("p (a b) -> p a b", a=NPB) for i in range(2)]
                ktp = kflat[:, 4 * PGE:5 * PGE].rearrange("p (a b c) -> p a b c", a=NPB, b=2)
                vpg = [kflat[:, 5 * PGE + i * VGE:5 * PGE + (i + 1) * VGE].rearrange("p (a b c) -> p a b c", a=NPB, b=4) for i in range(2)]
            else:
                kvr = [sb(f"kvr{i}", [128, NPB, 512], BF16) for i in range(2)]
                vpg = [sb(f"vpg{i}", [128, NPB, 4, 65], BF16) for i in range(2)]
                ktp = sb("ktp", [128, NPB, 2, 128], BF16)
            lpg = [sb(f"lpg{i}", [128, NPB, 16]) for i in range(2)]
            s_pg = [P.slot(f"pg{i}") for i in range(2)]
            Racc = sb("Racc", [128, 16]); r_R = Res("Racc")
            bpg = sb("bpg", [128, NPB, 16]); r_bpg = Res("bpg")
            lftok = sb("lftok", [128, 16]); r_lftok = Res("lftok")
            oacc = sb("oacc", [128, 16 * LS]); r_oacc = Res("oacc")
            bph = sb("bph", [128, 2, NPB * 2, 4]); r_bph = Res("bph")

        def sample_attention(off, L, sq_):
            nq = 4 * L
            if sq_ == 0:
                for i in range(2):
                    P.op("pool", lambda e, i=i: e.memset(vpg[i][:].rearrange("p a b c -> p (a b c)"), 1.0), writes=[r_vpg[i], r_kT])
                P.op("dve", lambda e: e.tensor_copy(out=qS[:].rearrange("p s c t -> p c s t"), in_=qT[:, :, :TS].rearrange("p c (s t) -> p c s t", s=NSEQ_S)), reads=[r_qT], writes=[r_qS])
            import os as _os
            nb = 0 if _os.environ.get('K_NOPAGES') else NPG // NPB
            P.op("pool", lambda e: e.memset(Racc[:], 0.0), writes=[r_R])
            mm(bank["F"][:L, 0:16], lfT[:, off:off + L], identf[:16, :16], True, True, [r_lf] + CR, [r_b["F"]])
            P.op("dve", lambda e: e.tensor_copy(out=lftok[:L, :], in_=bank["F"][:L, 0:16]), reads=[r_b["F"]], writes=[r_lftok])
            P.op("dve", lambda e: e.tensor_copy(out=Racc[:L, :], in_=bank["F"][:L, 0:16]), reads=[r_b["F"], r_R], writes=[r_R])
            mm(bank["F"][:, 16:32], ustr_f[:L, :], lftok[:L, :], True, True, [r_lftok] + CR, [r_b["F"]])
            P.op("dve", lambda e: e.tensor_copy(out=bpg[:, 0, :], in_=bank["F"][:, 16:32]), reads=[r_b["F"]], writes=[r_bpg])
            import os as _os
            ASTOP = float(_os.environ.get("K_ASTOP", "99"))
            if ASTOP <= 1:
                return
            for kv in range(4):
                half = kv % 2; cp = kv // 2; bk = "AB"[half]
                qap = qS[half * 64:(half + 1) * 64, sq_, 4 * cp:4 * cp + 4, :].rearrange("p c t -> p (c t)")
                P.op("pe", lambda e, half=half, cp=cp, qap=qap, bk=bk: e.matmul(bank[bk][:, cp * nq:(cp + 1) * nq], lhsT=ksb[half * 64:(half + 1) * 64, cp, sq_, :], rhs=qap, start=True, stop=True),
                     reads=[r_kT, r_qS], writes=[r_b[bk]])
            if ASTOP <= 1.2:
                return
            for half in range(2):
                bk = "AB"[half]
                P.op("dve", lambda e, half=half: e.tensor_copy(out=bph[:, half, 0:2, :], in_=bpg[:, 0, :].rearrange("p (c h g) -> p c h g", c=2, h=2)[:, :, half, :]), reads=[r_bpg], writes=[r_bph])
                o3 = sTt[:, half * 2 * nq:(half + 1) * 2 * nq].rearrange("p (a t) -> p a t", t=L)
                i3 = bank[bk][:, :2 * nq].rearrange("p (a t) -> p a t", t=L)
                b3 = bph[:, half, 0:2, :].rearrange("p c g -> p (c g)").unsqueeze(2).to_broadcast([128, 8, L])
                P.op("dve", lambda e, o3=o3, i3=i3, b3=b3: e.scalar_tensor_tensor(out=o3, in0=i3, scalar=1.0, in1=b3, op0=ALU.mult, op1=ALU.add),
                     reads=[r_b[bk], r_bph], writes=[r_sTt])
            if ASTOP <= 1.4:
                return
            P.op("act", lambda e: e.activation(out=PT[:, :4 * nq], in_=sTt[:, :4 * nq], func=AF.Exp), reads=[r_sTt], writes=[r_PT])
            if ASTOP <= 1.6:
                return
            P.op("dve", lambda e: e.tensor_tensor(out=PT[:, :4 * nq].rearrange("p (h t) -> p h t", h=16), in0=PT[:, :4 * nq].rearrange("p (h t) -> p h t", h=16), in1=mask01[:, :L].unsqueeze(1).to_broadcast([128, 16, L]), op=ALU.mult),
                 reads=[r_PT] + CR, writes=[r_PT])
            if ASTOP <= 2:
                return
            for kv in range(4):
                pc = ((kv % 2) * 2 + kv // 2) * nq
                mm(bank["C"][0:65, kv * nq:(kv + 1) * nq], Vsm[:, sq_, kv, :], PT[:, pc:pc + nq], True, True, [r_V, r_PT], [r_b["C"]])
            P.op("dve", lambda e: e.tensor_copy(out=oacc[0:65, :4 * nq], in_=bank["C"][0:65, :4 * nq]), reads=[r_b["C"]], writes=[r_oacc])
            if ASTOP <= 3:
                return
            for bi in range(nb):
                pb = nb - 1 - bi
                i2 = bi % 2
                for pp in range(NPB):
                    pg = pb * NPB + pp
                    col = sq_ * NPG + pg
                    P.dma("pool", lambda e, i2=i2, pp=pp, col=col: e.indirect_dma_start(out=kvr[i2][:, pp, :], out_offset=None, in_=ckv_d, in_offset=bass.IndirectOffsetOnAxis(ap=ridx[:, col:col + 1], axis=0)),
                          None, reads=[r_pt], writes=[r_kvr[i2]])
                    P.dma("pool", lambda e, i2=i2, pp=pp, col=col: e.indirect_dma_start(out=lpg[i2][:, pp, :], out_offset=None, in_=clf_d, in_offset=bass.IndirectOffsetOnAxis(ap=ridx[:, col:col + 1], axis=0)),
                          None, reads=[r_pt], writes=[r_lpg[i2]])
                P.op("act", lambda e, i2=i2: e.activation(out=vpg[i2][:, :, :, 0:64], in_=kvr[i2][:, :, 256:512].rearrange("p a (b c) -> p a b c", b=4), func=AF.Copy), reads=[r_kvr[i2]], writes=[r_vpg[i2]])
                for pp in range(NPB):
                    for c in range(2):
                        P.op("pe", lambda e, i2=i2, pp=pp, c=c: e.transpose(pTB[:, (pp * 2 + c) * 128:(pp * 2 + c + 1) * 128], kvr[i2][:, pp, c * 128:(c + 1) * 128], ident[:]),
                             reads=[r_kvr[i2]] + CR, writes=[r_tb])
                P.op("act", lambda e: e.activation(out=ktp[:].rearrange("p a b c -> p (a b c)"), in_=pTB[:, 0:NPB * 256], func=AF.Copy), reads=[r_tb], writes=[r_ktp])
                for pp in reversed(range(NPB)):
                    mm(bank["F"][:, 0:16], ustr_f[:], lpg[i2][:, pp, :], True, False, [r_lpg[i2]] + CR, [r_b["F"]])
                    mm(bank["F"][:, 0:16], onesf[:], Racc[:], False, True, [r_R] + CR, [r_b["F"]])
                    P.op("dve", lambda e, pp=pp: e.tensor_copy(out=bpg[:, pp, :], in_=bank["F"][:, 0:16]), reads=[r_b["F"]], writes=[r_bpg])
                    P.op("dve", lambda e, pp=pp, i2=i2: e.tensor_tensor(out=Racc[:], in0=Racc[:], in1=lpg[i2][:, pp, :], op=ALU.add), reads=[r_lpg[i2], r_R], writes=[r_R])
                for pp in range(NPB):
                    for kv in range(4):
                        half = kv % 2; cp = kv // 2; bk = "AB"[half]
                        qap = qS[half * 64:(half + 1) * 64, sq_, 4 * cp:4 * cp + 4, :].rearrange("p c t -> p (c t)")
                        c0 = (pp * 2 + cp) * nq
                        P.op("pe", lambda e, half=half, cp=cp, qap=qap, pp=pp, c0=c0, bk=bk: e.matmul(bank[bk][:, c0:c0 + nq], lhsT=ktp[half * 64:(half + 1) * 64, pp, cp, :], rhs=qap, start=True, stop=True),
                             reads=[r_ktp, r_qS], writes=[r_b[bk]])
                tot_c = NPB * 4 * nq
                hc = NPB * 2 * nq
                for half in range(2):
                    bk = "AB"[half]
                    P.op("dve", lambda e, half=half: e.tensor_copy(out=bph[:, half, :, :], in_=bpg[:].rearrange("p n (c h g) -> p (n c) h g", c=2, h=2)[:, :, half, :]), reads=[r_bpg], writes=[r_bph])
                    o3 = sTt[:, half * hc:(half + 1) * hc].rearrange("p (a t) -> p a t", t=L)
                    i3 = bank[bk][:, :hc].rearrange("p (a t) -> p a t", t=L)
                    b3 = bph[:, half, :, :].rearrange("p a g -> p (a g)").unsqueeze(2).to_broadcast([128, NPB * 8, L])
                    P.op("dve", lambda e, o3=o3, i3=i3, b3=b3: e.scalar_tensor_tensor(out=o3, in0=i3, scalar=1.0, in1=b3, op0=ALU.mult, op1=ALU.add),
                         reads=[r_b[bk], r_bph], writes=[r_sTt])
                P.op("act", lambda e: e.activation(out=PT[:, :tot_c], in_=sTt[:, :tot_c], func=AF.Exp), reads=[r_sTt], writes=[r_PT])
                for pp in range(NPB):
                    for kv in range(4):
                        c0 = (pp * 4 + kv) * nq
                        pc = ((kv % 2) * NPB * 2 + pp * 2 + kv // 2) * nq
                        mm(bank["C"][0:65, c0:c0 + nq], vpg[i2][:, pp, kv, :], PT[:, pc:pc + nq], True, True, [r_vpg[i2], r_PT], [r_b["C"]])
                for pp in range(NPB):
                    P.op("dve", lambda e, pp=pp: e.tensor_tensor(out=oacc[0:65, :4 * nq], in0=oacc[0:65, :4 * nq], in1=bank["C"][0:65, pp * 4 * nq:(pp + 1) * 4 * nq], op=ALU.add),
                         reads=[r_b["C"], r_oacc], writes=[r_oacc])
            ncol = 4 * nq
            P.op("dve", lambda e: e.reciprocal(out=rec[64:65, :ncol], in_=oacc[64:65, :ncol]), reads=[r_oacc], writes=[r_rec])
            mm(bank["E"][0:64, :ncol], onesf[64:65, 0:64], rec[64:65, :ncol], True, True, [r_rec] + CR, [r_b["E"]])
            P.op("dve", lambda e: e.tensor_tensor(out=oT[0:64, 0:16, off:off + L], in0=oacc[0:64, :ncol].rearrange("p (h t) -> p h t", h=16), in1=bank["E"][0:64, :ncol].rearrange("p (h t) -> p h t", h=16), op=ALU.mult),
                 reads=[r_b["E"], r_oacc] + r_xc, writes=[r_oT])

        import os as _os
        for b in range(0 if _os.environ.get("K_NOPROMPT") else NB):
            block(T, [(i * 128, 128, 0) for i in range(4)], False, b)
        if do_sample:
            block(TS, [(i * LS, LS, i) for i in range(NSEQ_S)], True, 0)
        print("sbuf bytes remaining", nc.sbuf_bytes_remaining, flush=True)
        n, ns = P.emit(nc)
        print("ops", n, "signals", ns, flush=True)
    return nc


_PROG_CACHE = {}


def kernel(x_prompt, x_sample, cache_k, cache_v, cache_logf, state_ssm, state_conv, page_table,
           ffn1_norm, ffn1_w_gate, ffn1_w_up, ffn1_w_down, mix_norm, w_in, conv_w, conv_b,
           dt_bias, a_log, d_skip, ssd_norm, q_norm, k_norm, b_f, w_ssd_proj, w_attn_proj, w_out,
           ffn2_norm, ffn2_w_gate, ffn2_w_up, ffn2_w_down):
    f = lambda a: np.asarray(a, dtype=np.float32)
    B, SEQ, _ = x_prompt.shape
    DB, DS, _ = x_sample.shape
    NPOOL = cache_k.shape[1]
    NPG = page_table.shape[1]
    p = dict(ffn1_norm=f(ffn1_norm)[0], ffn1_w_gate=f(ffn1_w_gate)[0], ffn1_w_up=f(ffn1_w_up)[0], ffn1_w_down=f(ffn1_w_down)[0],
             mix_norm=f(mix_norm)[0], w_in=f(w_in)[0], conv_w=f(conv_w)[0], conv_b=f(conv_b)[0], dt_bias=f(dt_bias)[0],
             a_log=f(a_log)[0], d_skip=f(d_skip)[0], ssd_norm=f(ssd_norm)[0], q_norm=f(q_norm)[0], k_norm=f(k_norm)[0],
             b_f=f(b_f)[0], w_ssd_proj=f(w_ssd_proj)[0], w_attn_proj=f(w_attn_proj)[0], w_out=f(w_out)[0],
             ffn2_norm=f(ffn2_norm)[0], ffn2_w_gate=f(ffn2_w_gate)[0], ffn2_w_up=f(ffn2_w_up)[0], ffn2_w_down=f(ffn2_w_down)[0])
    wst = build_weight_stream(p)
    cvec = build_cvec(p)
    rowc = np.concatenate([p["dt_bias"], p["a_log"], p["d_skip"]]).reshape(1, 96).astype(np.float32)
    wdt = np.ascontiguousarray(p["w_in"][:, O_DT:O_DT + 32].reshape(8, 128, 32).transpose(1, 0, 2).reshape(128, 256))
    ckv = np.concatenate([f(cache_k)[0].reshape(NPOOL * 128, 256), f(cache_v)[0].reshape(NPOOL * 128, 256)], axis=1)
    cl2 = np.ascontiguousarray(f(cache_logf)[0].reshape(NPOOL * 128, 16))
    xp = f(x_prompt); xs = f(x_sample)
    ssm = f(state_ssm)[0]; cst = f(state_conv)[0]
    pt = np.asarray(page_table, dtype=np.int32)
    NLh = (SEQ // 512) // 2
    key = (SEQ, NPG, NPOOL)
    if key not in _PROG_CACHE:
        import os as _os
        _PROG_CACHE[key] = build_program(SEQ, NPG, NPOOL, do_sample=(_os.environ.get('K_NOSAMPLE') is None))
    nc = _PROG_CACHE[key]
    in_maps = []
    for c in range(NCORES):
        sq = slice(NSEQ_S * c, NSEQ_S * (c + 1))
        in_maps.append({
            "xT": np.ascontiguousarray((xp[c % B] if (c // B == 1 or NLh == 0) else np.concatenate([xp[c % B][:SEQ // 2], xp[c % B][:SEQ // 2]])).T),
            "flg": np.tile(np.array([[1.0, 0.0]] if (c // B == 1 or NLh == 0) else [[0.0, -30000.0]], np.float32), (128, 1)),
            "xsT": np.ascontiguousarray(xs[sq].reshape(NSEQ_S * DS, D).T),
            "wst": wst, "cvec": cvec, "rowc": rowc, "wdt": wdt,
            "ssm0": np.ascontiguousarray(ssm[sq].reshape(NSEQ_S, 2048, 128).transpose(0, 2, 1)),
            "conv0": np.ascontiguousarray(cst[sq].reshape(NSEQ_S, 3, 24, 128).transpose(0, 3, 2, 1).reshape(NSEQ_S, 128, 72)),
            "cache_kv": ckv, "cache_lf": cl2,
            "ptab": np.ascontiguousarray(pt[sq].reshape(1, NSEQ_S * NPG)),
        })
    import os as _os
    if _os.environ.get("K_TRACE"):
        _r = run_bass_kernel_spmd(nc, in_maps, core_ids=list(range(NCORES)), trace=True)
        print("EXEC_NS", _r.exec_time_ns, flush=True)
        res = _r.results
    else:
        res = run_bass_kernel_spmd(nc, in_maps, core_ids=list(range(NCORES))).results
    def cat(name, b):
        if NLh == 0:
            return res[b][name].T
        return np.concatenate([res[b][name].T, res[b + B][name].T], axis=0)
    fb = 0 if NLh == 0 else B
    yp = np.stack([cat("yT", b) for b in range(B)])
    kp = np.stack([cat("kT", b).reshape(SEQ, 4, 64) for b in range(B)])[None]
    vp = np.stack([cat("vT", b).reshape(SEQ, 4, 64) for b in range(B)])[None]
    lp = np.stack([cat("lfT", b) for b in range(B)])[None]
    sp = np.stack([res[b + fb]["ssm"].T.reshape(32, 64, 128) for b in range(B)])[None]
    cp = np.stack([res[b + fb]["conv"].reshape(128, 24, 3).transpose(2, 1, 0).reshape(3, 3072) for b in range(B)])[None]
    ys = np.concatenate([res[c]["ysT"].T.reshape(NSEQ_S, DS, D) for c in range(NCORES)])
    ks = np.concatenate([res[c]["ksT"].T.reshape(NSEQ_S, DS, 4, 64) for c in range(NCORES)])[None]
    vs = np.concatenate([res[c]["vsT"].T.reshape(NSEQ_S, DS, 4, 64) for c in range(NCORES)])[None]
    ls = np.concatenate([res[c]["lfsT"].T.reshape(NSEQ_S, DS, 16) for c in range(NCORES)])[None]
    ss = np.concatenate([res[c]["ssms"].transpose(0, 2, 1).reshape(NSEQ_S, 32, 64, 128) for c in range(NCORES)])[None]
    cs = np.concatenate([res[c]["convs"].reshape(NSEQ_S, 128, 24, 3).transpose(0, 3, 2, 1).reshape(NSEQ_S, 3, 3072) for c in range(NCORES)])[None]
    o = (yp, ys, kp, vp, lp, sp, cp, ks, vs, ls, ss, cs)
    return tuple(np.ascontiguousarray(a, dtype=np.float32) for a in o)
```

```python
import contextlib
import numpy as np
import concourse.bass as bass
import concourse.mybir as mybir
from concourse.bass_utils import run_bass_kernel_spmd

F32 = mybir.dt.float32
BF16 = mybir.dt.bfloat16
I32 = mybir.dt.int32
AF = mybir.ActivationFunctionType
ALU = mybir.AluOpType

D = 1024
DFF = 2816
NF = 22
NSEQ_S = 4
LS = 8
NCORES = 8
_DBG = {}


class Res:
    __slots__ = ("name", "w", "r")

    def __init__(self, name=""):
        self.name = name
        self.w = None
        self.r = []


class DmaSlot:
    __slots__ = ("sem", "count", "name")

    def __init__(self, name):
        self.name = name
        self.sem = None
        self.count = 0


ENGS = ("pe", "act", "dve", "pool", "sp")
EPOCH = 30000


class Prog:
    def __init__(self):
        self.ops = {e: [] for e in ENGS}
        self.waited = {e: {} for e in ENGS}
        self.needed = {e: set() for e in ENGS}
        self.slots = []

    def slot(self, name=""):
        s = DmaSlot(name)
        self.slots.append(s)
        return s

    def _deps(self, eng, reads, writes):
        deps = []
        for r in reads:
            if r.w is not None:
                deps.append(r.w)
        for w in writes:
            if w.w is not None:
                deps.append(w.w)
            deps.extend(w.r)
        out = []
        wd = self.waited[eng]
        best = {}
        for d in deps:
            if d[0] == "dma":
                key = ("dma", id(d[1]))
                if key not in best or best[key][2] < d[2]:
                    best[key] = d
            else:
                if d[0] == eng and eng == "pe":
                    continue
                if d[0] not in best or best[d[0]][1] < d[1]:
                    best[d[0]] = d
        for key, d in best.items():
            v = d[2] if d[0] == "dma" else d[1]
            if wd.get(key, 0) >= v:
                continue
            wd[key] = v
            if d[0] != "dma":
                self.needed[d[0]].add(v)
            out.append(d)
        return out

    def op(self, eng, fn, reads=(), writes=()):
        waits = self._deps(eng, reads, writes)
        lst = self.ops[eng]
        lst.append([fn, waits, None, 0])
        tok = (eng, len(lst))
        for r in reads:
            r.r.append(tok)
        for w in writes:
            w.w = tok
            w.r = []
        return tok

    def dma(self, eng, fn, slot, reads=(), writes=()):
        if slot is None:
            key = writes[0] if len(writes) else reads[0]
            if not hasattr(self, "_auto"):
                self._auto = {}
            if id(key) not in self._auto:
                self._auto[id(key)] = self.slot("a" + key.name)
            slot = self._auto[id(key)]
        waits = self._deps(eng, reads, writes)
        slot.count += 16
        self.ops[eng].append([fn, waits, slot, slot.count])
        tok = ("dma", slot, slot.count)
        for r in reads:
            r.r.append(tok)
        for w in writes:
            w.w = tok
            w.r = []
        return tok

    def emit(self, nc, final_engine="sp"):
        with contextlib.ExitStack() as st:
            rank = {}
            nsig = {}
            for e in ENGS:
                flagged = sorted(self.needed[e])
                rank[e] = {idx: i + 1 for i, idx in enumerate(flagged)}
                nsig[e] = len(flagged)
            sems = {}
            for e in ENGS:
                n_ep = max(1, (nsig[e] + EPOCH - 1) // EPOCH)
                sems[e] = [st.enter_context(nc.semaphore(f"s_{e}{k}")) for k in range(n_ep)]
            for s in self.slots:
                if s.count > 0:
                    s.sem = st.enter_context(nc.semaphore(f"d_{s.name}"))
            fin = [("dma", s, s.count) for s in self.slots if s.count > 0]
            self.ops[final_engine].append([None, fin, None, 0])
            block = st.enter_context(nc.Block())

            def run(e):
                def body(eng):
                    for i, (fn, waits, slot, val) in enumerate(self.ops[e]):
                        for d in waits:
                            if d[0] == "dma":
                                eng.wait_ge(d[1].sem, d[2])
                            else:
                                sg = rank[d[0]][d[1]]
                                eng.wait_ge(sems[d[0]][(sg - 1) // EPOCH], (sg - 1) % EPOCH + 1)
                        if fn is None:
                            continue
                        ins = fn(eng)
                        if slot is not None:
                            ins.then_inc(slot.sem, 16)
                        else:
                            sg = rank[e].get(i + 1)
                            if sg is not None:
                                ins.then_inc(sems[e][(sg - 1) // EPOCH], 1)
                return body

            block.tensor(run("pe"))
            block.scalar(run("act"))
            block.vector(run("dve"))
            block.gpsimd(run("pool"))
            block.sync(run("sp"))
        return {e: len(self.ops[e]) for e in ENGS}, nsig


def _kc_tile(w_cols):
    return w_cols.reshape(8, 128, 128).transpose(1, 0, 2).reshape(128, 1024)


def _pad_cols(w, n):
    out = np.zeros((w.shape[0], n), np.float32)
    out[:, : w.shape[1]] = w
    return out


def _ffn_tiles(wg, wu, wd):
    tiles = []
    for j in range(NF):
        tiles.append(_kc_tile(wg[:, j * 128:(j + 1) * 128]))
        tiles.append(_kc_tile(wu[:, j * 128:(j + 1) * 128]))
    wdp = np.concatenate([wd, np.zeros((24 * 128 - DFF, D), np.float32)], 0)
    for m in range(8):
        blk = wdp[:, m * 128:(m + 1) * 128].reshape(24, 128, 128)
        for s in range(3):
            tiles.append(blk[s * 8:(s + 1) * 8].transpose(1, 0, 2).reshape(128, 1024))
    return tiles


O_Z, O_XBC, O_DT, O_Q, O_K, O_V, O_F, O_GS, O_GA = 0, 2048, 5120, 5152, 6176, 6432, 6688, 6704, 7728


def _q_perm_cols():
    cols = []
    for cp in range(2):
        for g in range(4):
            for half in range(2):
                kv = 2 * cp + half
                h = 4 * kv + g
                cols.extend(range(O_Q + h * 64, O_Q + (h + 1) * 64))
    return np.array(cols)


def build_weight_stream(p):
    t = []
    t += _ffn_tiles(p["ffn1_w_gate"], p["ffn1_w_up"], p["ffn1_w_down"])
    w_in = p["w_in"]
    for c in range(2):
        t.append(_kc_tile(w_in[:, O_K + c * 128: O_K + (c + 1) * 128]))
    for c in range(2):
        t.append(_kc_tile(w_in[:, O_V + c * 128: O_V + (c + 1) * 128]))
    t.append(_kc_tile(_pad_cols(w_in[:, O_F:O_F + 16], 128)))
    qc = w_in[:, _q_perm_cols()]
    for c in range(8):
        t.append(_kc_tile(qc[:, c * 128:(c + 1) * 128]))
    for c in range(24):
        t.append(_kc_tile(w_in[:, O_XBC + c * 128: O_XBC + (c + 1) * 128]))
    for c in range(16):
        t.append(_kc_tile(w_in[:, O_Z + c * 128: O_Z + (c + 1) * 128]))
    wsp = p["w_ssd_proj"]
    wap = p["w_attn_proj"]
    for m in range(8):
        blk = wsp[:, m * 128:(m + 1) * 128].reshape(16, 128, 128)
        for s in range(2):
            t.append(blk[s * 8:(s + 1) * 8].transpose(1, 0, 2).reshape(128, 1024))
        t.append(_kc_tile(w_in[:, O_GS + m * 128: O_GS + (m + 1) * 128]))
        ablk = wap[:, m * 128:(m + 1) * 128].reshape(16, 64, 128)
        for s in range(2):
            a = np.zeros((128, 1024), np.float32)
            a[:64] = ablk[s * 8:(s + 1) * 8].transpose(1, 0, 2).reshape(64, 1024)
            t.append(a)
        t.append(_kc_tile(w_in[:, O_GA + m * 128: O_GA + (m + 1) * 128]))
    wo = p["w_out"]
    for m in range(8):
        t.append(_kc_tile(wo[:, m * 128:(m + 1) * 128]))
    t += _ffn_tiles(p["ffn2_w_gate"], p["ffn2_w_up"], p["ffn2_w_down"])
    return np.ascontiguousarray(np.stack(t)).astype(np.float32)


NT = 68 + 37 + 16 + 48 + 8 + 68

CV_N1, CV_NM, CV_N2, CV_SN, CV_CW, CV_CB, CV_QN, CV_KN, CV_DS, CV_BF, CV_NBF = 0, 8, 16, 24, 40, 136, 160, 161, 162, 178, 179
NCV = 180


def build_cvec(p):
    cv = np.zeros((128, NCV), np.float32)
    cv[:, CV_N1:CV_N1 + 8] = p["ffn1_norm"].reshape(8, 128).T
    cv[:, CV_NM:CV_NM + 8] = p["mix_norm"].reshape(8, 128).T
    cv[:, CV_N2:CV_N2 + 8] = p["ffn2_norm"].reshape(8, 128).T
    cv[:, CV_SN:CV_SN + 16] = p["ssd_norm"].reshape(16, 128).T
    cw = p["conv_w"].reshape(4, 24, 128)
    cv[:, CV_CW:CV_CW + 96] = cw.transpose(2, 1, 0).reshape(128, 96)
    cv[:, CV_CB:CV_CB + 24] = p["conv_b"].reshape(24, 128).T
    cv[:, CV_QN] = np.tile(p["q_norm"], 2)
    cv[:, CV_KN] = np.tile(p["k_norm"], 2)
    cv[:, CV_DS:CV_DS + 16] = np.repeat(p["d_skip"], 64).reshape(16, 128).T
    cv[:16, CV_BF] = p["b_f"]
    return cv


def build_program(SEQ, NPG, NPOOL, do_sample=True):
    T = 512
    NB = SEQ // T
    NL = NB // 2
    HALF = (NB - NL) * T
    NTIL = SEQ // 128
    TS = NSEQ_S * LS
    nc = bass.Bass("TRN2", target_bir_lowering=False)

    def din(name, shape, dt=F32):
        return nc.dram_tensor(name, shape, dt, kind="ExternalInput").ap()

    def dout(name, shape, dt=F32):
        return nc.dram_tensor(name, shape, dt, kind="ExternalOutput").ap()

    xT_d = din("xT", [D, SEQ])
    xsT_d = din("xsT", [D, TS])
    wst_d = din("wst", [NT, 128, 1024])
    cvec_d = din("cvec", [128, NCV])
    rowc_d = din("rowc", [1, 96])
    wdt_d = din("wdt", [128, 8 * 32])
    ssm0_d = din("ssm0", [NSEQ_S, 128, 2048])
    conv0_d = din("conv0", [NSEQ_S, 128, 24 * 3])
    ckv_d = din("cache_kv", [NPOOL * 128, 512])
    clf_d = din("cache_lf", [NPOOL * 128, 16])
    pt_d = din("ptab", [1, NSEQ_S * NPG], I32)

    flg_d = din("flg", [128, 2])
    yT_o = dout("yT", [D, HALF])
    ysT_o = dout("ysT", [D, TS])
    kT_o = dout("kT", [256, HALF])
    vT_o = dout("vT", [256, HALF])
    lfT_o = dout("lfT", [16, HALF])
    ssm_o = dout("ssm", [128, 2048])
    conv_o = dout("conv", [128, 72])
    ksT_o = dout("ksT", [256, TS])
    vsT_o = dout("vsT", [256, TS])
    lfsT_o = dout("lfsT", [16, TS])
    ssms_o = dout("ssms", [NSEQ_S, 128, 2048])
    convs_o = dout("convs", [NSEQ_S, 128, 72])

    wbf_d = nc.dram_tensor("wbf", [NT, 128, 1024], BF16, kind="Internal").ap()

    P = Prog()
    st = contextlib.ExitStack()
    with st:
        def sb(name, shape, dt=F32):
            return st.enter_context(nc.sbuf_tensor("sb_" + name, shape, dt))

        def pst(name, shape, dt=F32):
            return st.enter_context(nc.psum_tensor("ps_" + name, shape, dt))

        cvec = sb("cvec", [128, NCV]); r_c = Res("const")
        rowc = sb("rowc", [128, 96])
        wdt32 = sb("wdt32", [128, 256]); wdt = sb("wdt", [128, 8, 32], BF16)
        ident = sb("ident", [128, 128], BF16)
        identf = sb("identf", [128, 128], F32)
        ones_d = sb("ones_d", [128, 128], BF16)
        ones_g = sb("ones_g", [128, 128], BF16)
        bd64 = sb("bd64", [128, 128], BF16)
        onesf = sb("onesf", [128, 128], F32)
        tri_f = sb("tri_f", [128, 128], F32)
        ustr_f = sb("ustr_f", [128, 128], F32)
        mask01 = sb("mask01", [128, 128], F32)
        epsc = sb("epsc", [128, 1]); onec = sb("onec", [128, 1])
        A_bc = sb("A_bc", [128, 32]); nbf = sb("nbf", [128, 1])
        s_c = P.slot("const")
        r_c1 = Res("c1"); r_c2 = Res("c2")
        flg = sb("flg", [128, 2])
        P.dma("sp", lambda e: e.dma_start(out=flg[:], in_=flg_d), None, writes=[r_c])
        P.dma("sp", lambda e: e.dma_start(out=cvec[:], in_=cvec_d), None, writes=[r_c])
        P.dma("sp", lambda e: e.dma_start(out=rowc[:], in_=rowc_d.partition_broadcast(128)), None, writes=[r_c1])
        P.dma("sp", lambda e: e.dma_start(out=wdt32[:], in_=wdt_d), None, writes=[r_c2])
        r_k = Res("consts2")
        P.op("pool", lambda e: e.memset(identf[:], 1.0), writes=[r_k])
        P.op("pool", lambda e: e.affine_select(out=identf[:], in_=identf[:], pattern=[[-1, 128]], compare_op=ALU.is_equal, fill=0.0, base=0, channel_multiplier=1), writes=[r_k])
        P.op("pool", lambda e: e.tensor_copy(out=ident[:], in_=identf[:]), writes=[r_k])
        P.op("pool", lambda e: e.memset(onesf[:], 1.0), writes=[r_k])
        P.op("pool", lambda e: e.memset(ones_d[:], 1.0 / 1024), writes=[r_k])
        P.op("pool", lambda e: e.memset(ones_g[:], 1.0 / 512), writes=[r_k])
        P.op("pool", lambda e: e.memset(bd64[:], 0.0), writes=[r_k])
        P.op("pool", lambda e: e.memset(bd64[0:64, 0:64], 1.0 / 64), writes=[r_k])
        P.op("pool", lambda e: e.memset(bd64[64:128, 64:128], 1.0 / 64), writes=[r_k])
        P.op("pool", lambda e: e.affine_select(out=tri_f[:], in_=onesf[:], pattern=[[1, 128]], compare_op=ALU.is_ge, fill=0.0, base=0, channel_multiplier=-1), writes=[r_k])
        P.op("pool", lambda e: e.tensor_copy(out=mask01[:], in_=tri_f[:]), writes=[r_k])
        P.op("pool", lambda e: e.affine_select(out=ustr_f[:], in_=onesf[:], pattern=[[-1, 128]], compare_op=ALU.is_gt, fill=0.0, base=0, channel_multiplier=1), writes=[r_k])
        P.op("pool", lambda e: e.memset(epsc[:], 1e-6), writes=[r_k])
        P.op("pool", lambda e: e.memset(onec[:], 1.0), writes=[r_k])
        P.op("act", lambda e: e.activation(out=A_bc[:], in_=rowc[:, 32:64], func=AF.Exp), reads=[r_c1], writes=[r_k])
        P.op("dve", lambda e: e.tensor_scalar(out=A_bc[:], in0=A_bc[:], scalar1=-1.0, scalar2=None, op0=ALU.mult), reads=[r_k], writes=[r_k])
        P.op("dve", lambda e: e.tensor_scalar(out=nbf[:], in0=cvec[:, CV_BF:CV_BF + 1], scalar1=-1.0, scalar2=None, op0=ALU.mult), reads=[r_c], writes=[r_k])
        P.op("dve", lambda e: e.tensor_copy(out=wdt[:].rearrange("p a b -> p (a b)"), in_=wdt32[:]), reads=[r_c2], writes=[r_k])
        CR = [r_c, r_k, r_c1]

        r_wbf = [Res(f"wbf{i}") for i in range(NT)]
        s_pre = [P.slot(f"pre{i}") for i in range(8)]
        for t in range(NT):
            P.dma("pool", lambda e, t=t: e.dma_start(out=wbf_d[t], in_=wst_d[t]), s_pre[(t // 8) % 8], writes=[r_wbf[t]])
        for t in range(NT):
            last = min(NT - 1, (t // 8) * 8 + 7)
            r_wbf[t].w = r_wbf[last].w
        NS = 6
        ring = [sb(f"ring{i}", [128, 1024], BF16) for i in range(NS)]
        r_ring = [Res(f"ring{i}") for i in range(NS)]
        s_ring = [P.slot(f"ring{i}") for i in range(NS)]
        wctr = [0]

        def wskip(k):
            wctr[0] += k

        def wtile():
            n = wctr[0]; wctr[0] += 1
            t = n % NT
            s = n % NS
            P.dma("sp", lambda e: e.dma_start(out=ring[s][:], in_=wbf_d[t]), s_ring[s], reads=[r_wbf[t]], writes=[r_ring[s]])
            return ring[s], r_ring[s]

        pAB = pst("pAB", [128, 1024]); pCD = pst("pCD", [128, 1024]); pEF = pst("pEF", [128, 1024])
        pTB = pst("pTB", [128, 2048], BF16)
        bank = {"A": pAB[:, 0:512], "B": pAB[:, 512:1024], "C": pCD[:, 0:512], "D": pCD[:, 512:1024],
                "E": pEF[:, 0:512], "F": pEF[:, 512:1024]}
        r_b = {k: Res("bank" + k) for k in "ABCDEF"}
        r_tb = Res("pTB")

        hT = sb("hT", [128, 8, T]); r_hT = Res("hT")
        xn = sb("xn", [128, 8, T], BF16); r_xn = Res("xn")
        arena = sb("arena", [128, NF, T], BF16)
        r_hid = [Res(f"hid{j}") for j in range(NF)]
        xc = sb("xc", [128, 24, T], BF16); r_xc = [Res(f"xc{c}") for c in range(24)]
        qT = sb("qT", [128, 8, T], BF16); r_qT = Res("qT")
        kTs = sb("kTs", [128, 2, SEQ], BF16); r_kT = Res("kT")
        Vs = sb("Vs", [128, NTIL, 4, 65], BF16); r_V = Res("V")
        cks = sb("cks", [128, NTIL, 16]); r_ck = Res("ck")
        biasb = sb("biasb", [128, NTIL, 16]); r_bias = Res("bias")
        yT = sb("yT", [128, 16, T], BF16); r_yT = [Res(f"yT{c}") for c in range(16)]
        oT = xc
        r_oT = Res("oT")
        merged = qT
        r_mg = Res("merged")
        hst = sb("hst", [128, 2048]); r_hst = Res("hst")
        stg = [sb(f"stg{i}", [128, T]) for i in range(3)]; r_stg = [Res(f"stg{i}") for i in range(3)]
        sqb = sb("sqb", [128, 4, T], BF16); r_sq = [Res(f"sq{i}") for i in range(4)]
        rstd = sb("rstd", [128, T]); r_rstd = Res("rstd")
        cstage = [sb(f"cst{i}", [128, T + 3 * NSEQ_S]) for i in range(2)]; r_cst = [Res(f"cst{i}") for i in range(2)]
        cacc = [sb(f"cacc{i}", [128, T]) for i in range(2)]; r_cacc = [Res(f"cacc{i}") for i in range(2)]
        ccar = sb("ccar", [128, 24, 3 * NSEQ_S]); r_ccar = [Res(f"ccar{c}") for c in range(24)]
        lfT = sb("lfT", [16, T]); r_lf = Res("lfT")
        cT = sb("cT", [16, T]); r_cT = Res("cT")
        ccarry = sb("ccarry", [16, 1]); r_cc = Res("ccarry")
        ones16 = sb("ones16", [16, T])
        dgl = sb("dgl", [16, 16]); r_dgl = Res("dgl")
        cref = sb("cref", [128, 16]); r_cref = Res("cref")
        dtt = sb("dtt", [128, 32]); at = sb("at", [128, 32]); acs = sb("acs", [128, 32]); tot = sb("tot", [128, 32])
        eacs = sb("eacs", [128, 32]); dte = sb("dte", [128, 32]); cdec = sb("cdec", [128, 32])
        r_ss = Res("ssdsmall")
        Dg = sb("Dg", [128, 8, 128]); r_Dg = Res("Dg")
        cbm = sb("cbm", [128, 128]); r_cbm = Res("cbm")
        tmpf = sb("tmpf", [128, 512]); r_tmpf = Res("tmpf")
        def asl(a, b):
            return arena[:, a:b, :].rearrange("p a b -> p (a b)")
        xdt = asl(0, 4); xw = asl(4, 8); ytok = asl(8, 12); hbf = asl(12, 16)
        Btok = asl(16, 17); MT = asl(17, 19); MTb = asl(19, 21); PT = asl(21, 22)
        r_xdt, r_xw, r_ytok, r_hbf, r_Btok, r_MT, r_MTb, r_PT = (Res(n) for n in ("xdt", "xw", "ytok", "hbf", "Btok", "MT", "MTb", "PT"))
        sTt = sb("sTt", [128, 512]); r_sTt = Res("sTt")
        bcs = stg[2][0:64, :]; r_bcs = r_stg[2]
        rec = tmpf; r_rec = r_tmpf

        s_in = P.slot("xin"); s_o = [P.slot(f"out{i}") for i in range(6)]
        P.op("pool", lambda e: e.memset(Vs[:].rearrange("p a b c -> p (a b c)"), 1.0), writes=[r_V])
        P.op("pool", lambda e: e.memset(ones16[:], 1.0), writes=[r_k])

        def mm(out, lhsT, rhs, start, stop, reads, writes):
            P.op("pe", lambda e: e.matmul(out, lhsT=lhsT, rhs=rhs, start=start, stop=stop), reads=reads, writes=writes)

        def rms_rstd(ps_ms, r_ps, n):
            P.op("act", lambda e: e.activation(out=rstd[:, :n], in_=ps_ms, func=AF.Ln, bias=epsc[:], scale=1.0), reads=[r_ps] + CR, writes=[r_rstd])
            P.op("act", lambda e: e.activation(out=rstd[:, :n], in_=rstd[:, :n], func=AF.Exp, scale=-0.5), reads=[r_rstd], writes=[r_rstd])

        def norm_to_xn(n, cvo):
            for c in range(8):
                P.op("act", lambda e, c=c: e.activation(out=sqb[:, c % 4, :n], in_=hT[:, c, :n], func=AF.Square), reads=[r_hT], writes=[r_sq[c % 4]])
                mm(bank["E"][:, :n], ones_d[:], sqb[:, c % 4, :n], c == 0, c == 7, [r_sq[c % 4]] + CR, [r_b["E"]])
            rms_rstd(bank["E"][:, :n], r_b["E"], n)
            for c in range(8):
                P.op("dve", lambda e, c=c: e.scalar_tensor_tensor(out=xn[:, c, :n], in0=hT[:, c, :n], scalar=cvec[:, cvo + c:cvo + c + 1], in1=rstd[:, :n], op0=ALU.mult, op1=ALU.mult),
                     reads=[r_hT, r_rstd] + CR, writes=[r_xn])

        def proj8(bk, n, w, rw, src=None, rsrc=None):
            for c in range(8):
                mm(bank[bk][:, :n], w[:, c * 128:(c + 1) * 128], xn[:, c, :n], c == 0, c == 7, [rw, r_xn], [r_b[bk]])

        def ffn(n, final_out=None):
            for j in range(NF):
                wg, rg = wtile(); wu, ru = wtile()
                pg, pu = ("A", "B") if j % 2 == 0 else ("C", "D")
                proj8(pg, n, wg, rg); proj8(pu, n, wu, ru)
                si = j % 2
                P.op("act", lambda e, pg=pg, si=si: e.activation(out=stg[si][:, :n], in_=bank[pg][:, :n], func=AF.Silu), reads=[r_b[pg]], writes=[r_stg[si]])
                P.op("dve", lambda e, pu=pu, si=si, j=j: e.tensor_tensor(out=arena[:, j, :n], in0=bank[pu][:, :n], in1=stg[si][:, :n], op=ALU.mult),
                     reads=[r_b[pu], r_stg[si]], writes=[r_hid[j]])
            for m in range(8):
                tl = [wtile() for _ in range(3)]
                pb = "AB"[m % 2]
                for kc in range(NF):
                    w, r = tl[kc // 8]
                    mm(bank[pb][:, :n], w[:, (kc % 8) * 128:(kc % 8 + 1) * 128], arena[:, kc, :n], kc == 0, kc == NF - 1, [r, r_hid[kc]], [r_b[pb]])
                P.op("dve", lambda e, m=m, pb=pb: e.scalar_tensor_tensor(out=hT[:, m, :n], in0=bank[pb][:, :n], scalar=0.5, in1=hT[:, m, :n], op0=ALU.mult, op1=ALU.add),
                     reads=[r_b[pb], r_hT], writes=[r_hT])

        def qknorm(bk, n, wcol, scale, out_bf, r_out, out_f32=None, r_f32=None, si=0, f32_view=None):
            P.op("act", lambda e: e.activation(out=stg[si][:, :n], in_=bank[bk][:, :n], func=AF.Copy), reads=[r_b[bk]], writes=[r_stg[si]])
            P.op("act", lambda e: e.activation(out=sqb[:, si, :n], in_=stg[si][:, :n], func=AF.Square), reads=[r_stg[si]], writes=[r_sq[si]])
            pb = "EF"[si]
            mm(bank[pb][:, :n], bd64[:], sqb[:, si, :n], True, True, [r_sq[si]] + CR, [r_b[pb]])
            rms_rstd(bank[pb][:, :n], r_b[pb], n)
            if out_f32 is not None:
                P.op("dve", lambda e: e.scalar_tensor_tensor(out=out_f32, in0=stg[si][:, :n], scalar=cvec[:, wcol:wcol + 1], in1=rstd[:, :n], op0=ALU.mult, op1=ALU.mult),
                     reads=[r_stg[si], r_rstd] + CR, writes=[r_f32])
                P.op("act", lambda e: e.activation(out=out_bf, in_=(out_f32 if f32_view is None else f32_view), func=AF.Copy, scale=scale), reads=[r_f32], writes=[r_out])
            else:
                P.op("dve", lambda e: e.scalar_tensor_tensor(out=stg[si][:, :n], in0=stg[si][:, :n], scalar=cvec[:, wcol:wcol + 1], in1=rstd[:, :n], op0=ALU.mult, op1=ALU.mult),
                     reads=[r_stg[si], r_rstd] + CR, writes=[r_stg[si]])
                P.op("act", lambda e: e.activation(out=out_bf, in_=stg[si][:, :n], func=AF.Copy, scale=scale), reads=[r_stg[si]], writes=[r_out])

        kvout = sb("kvout", [128, 2, T]); r_kvo = [Res(f"kvo{i}") for i in range(2)]

        def block(n, segs, sample, b):
            tb = b * T
            light = (not sample) and b < NL
            ob_ = (b - NL) * T
            if (not sample) and b == NL and NL > 0:
                P.op("dve", lambda e: e.tensor_scalar(out=hst[:], in0=hst[:], scalar1=flg[:, 0:1], scalar2=None, op0=ALU.mult), reads=[r_hst] + CR, writes=[r_hst])
                P.op("dve", lambda e: e.tensor_scalar(out=ccar[:, :, 0:3], in0=ccar[:, :, 0:3], scalar1=flg[:, 0:1], scalar2=None, op0=ALU.mult), reads=r_ccar + CR, writes=r_ccar)
            xsrc = xsT_d if sample else xT_d[:, tb:tb + n]
            P.dma("sp", lambda e: e.dma_start(out=hT[:, :, :n], in_=xsrc.rearrange("(c p) t -> p c t", p=128)), None, writes=[r_hT])
            import os as _os
            SSTOP = int(_DBG.get("K_SSTOP", "99")) if sample else int(_DBG.get("K_PSTOP", "99"))
            if sample:
                for si_ in range(NSEQ_S):
                    P.dma("sp", lambda e, si_=si_: e.dma_start(out=ccar[:, :, 3 * si_:3 * si_ + 3], in_=conv0_d[si_].rearrange("p (c l) -> p c l", l=3)), None, writes=r_ccar)
            if SSTOP <= 0:
                return
            norm_to_xn(n, CV_N1)
            ffn(n)
            if SSTOP <= 1:
                return
            norm_to_xn(n, CV_NM)
            ko, vo, lo = (ksT_o, vsT_o, lfsT_o) if sample else ((None, None, None) if light else (kT_o[:, ob_:ob_ + n], vT_o[:, ob_:ob_ + n], lfT_o[:, ob_:ob_ + n]))
            kdst = kTs[:, :, SEQ - TS:SEQ] if False else None
            for c in range(2):
                w, rw = wtile(); bk = "AB"[c]
                proj8(bk, n, w, rw)
                kb = (kTs[:, c, tb:tb + n] if not sample else ksb[:, c, :, 0:LS])
                kvo_v = kvout[:, c, :n] if not sample else kvout[:, c, :n].rearrange("p (s l) -> p s l", s=NSEQ_S)
                qknorm(bk, n, CV_KN, 1.0, kb, r_kT, out_f32=kvout[:, c, :n], r_f32=r_kvo[c], si=c, f32_view=(kvo_v if sample else None))
                if not light:
                    P.dma("sp", lambda e, c=c: e.dma_start(out=ko[c * 128:(c + 1) * 128, :], in_=kvout[:, c, :n]), None, reads=[r_kvo[c]])
            for c in range(2):
                w, rw = wtile(); bk = "AB"[c]
                proj8(bk, n, w, rw)
                P.op("act", lambda e, c=c, bk=bk: e.activation(out=kvout[:, c, :n], in_=bank[bk][:, :n], func=AF.Copy), reads=[r_b[bk]], writes=[r_kvo[c]])
                if not light:
                    P.dma("sp", lambda e, c=c: e.dma_start(out=vo[c * 128:(c + 1) * 128, :], in_=kvout[:, c, :n]), None, reads=[r_kvo[c]])
                P.op("dve", lambda e, c=c: e.tensor_copy(out=sqb[:, c, :n], in_=kvout[:, c, :n]), reads=[r_kvo[c]], writes=[r_sq[c]])
                for (off, L, sq_) in segs:
                    P.op("pe", lambda e, c=c, off=off, L=L: e.transpose(pTB[:L, 0:128], sqb[:, c, off:off + L], ident[:]), reads=[r_sq[c]] + CR, writes=[r_tb])
                    if sample:
                        vdst = Vsm[:L, sq_, 2 * c:2 * c + 2, 0:64]
                    else:
                        vdst = Vs[:L, (tb + off) // 128, 2 * c:2 * c + 2, 0:64]
                    P.op("act", lambda e, L=L, vdst=vdst: e.activation(out=vdst, in_=pTB[:L, 0:128].rearrange("p (a b) -> p a b", a=2), func=AF.Copy), reads=[r_tb], writes=[r_V])
            w, rw = wtile()
            proj8("A", n, w, rw)
            P.op("act", lambda e: e.activation(out=lfT[:, :n], in_=bank["A"][:16, :n], func=AF.Exp, bias=nbf[:16, :], scale=-1.0), reads=[r_b["A"]] + CR, writes=[r_lf])
            P.op("act", lambda e: e.activation(out=lfT[:, :n], in_=lfT[:, :n], func=AF.Ln, bias=onec[:16, :], scale=1.0), reads=[r_lf], writes=[r_lf])
            P.op("dve", lambda e: e.tensor_scalar(out=lfT[:, :n], in0=lfT[:, :n], scalar1=-1.0, scalar2=None, op0=ALU.mult), reads=[r_lf], writes=[r_lf])
            if not light:
                P.dma("sp", lambda e: e.dma_start(out=lo, in_=lfT[:, :n]), None, reads=[r_lf])
            if not sample:
                if b == 0:
                    P.op("dve", lambda e: e.memset(ccarry[:], 0.0), writes=[r_cc])
                P.op("dve", lambda e: e.tensor_tensor_scan(out=cT[:, :n], data0=ones16[:, :n], data1=lfT[:, :n], initial=ccarry[:, 0:1], op0=ALU.mult, op1=ALU.add),
                     reads=[r_lf, r_cc], writes=[r_cT])
                P.op("dve", lambda e: e.tensor_copy(out=ccarry[:], in_=cT[:, n - 1:n]), reads=[r_cT], writes=[r_cc])
                for (off, L, sq_) in segs:
                    ti = (tb + off) // 128
                    mm(bank["B"][:L, 0:16], cT[:, off:off + L], identf[:16, :16], True, True, [r_cT] + CR, [r_b["B"]])
                    P.op("dve", lambda e, ti=ti, L=L: e.tensor_copy(out=cks[:L, ti, :], in_=bank["B"][:L, 0:16]), reads=[r_b["B"]], writes=[r_ck])
            if light:
                wskip(8)
            for c in range(0 if light else 8):
                w, rw = wtile(); bk = "AB"[c % 2]
                proj8(bk, n, w, rw)
                qknorm(bk, n, CV_QN, 0.125, qT[:, c, :n], r_qT, si=c % 2)
            for c in range(24):
                w, rw = wtile(); bk = "AB"[c % 2]; ci = c % 2
                proj8(bk, n, w, rw)
                cs = cstage[ci]
                nsq = len(segs) if sample else 1
                Ls = n // nsq
                csv = cs[:, :nsq * (Ls + 3)].rearrange("p (s l) -> p s l", s=nsq)
                if sample:
                    P.op("pool", lambda e, c=c, csv=csv, nsq=nsq: e.tensor_copy(out=csv[:, :, 0:3], in_=ccar[:, c, :3 * nsq].rearrange("p (s l) -> p s l", s=nsq)), reads=[r_ccar[c]], writes=[r_cst[ci]])
                elif b == 0:
                    P.op("pool", lambda e, csv=csv: e.memset(csv[:, :, 0:3], 0.0), writes=[r_cst[ci]])
                else:
                    P.op("pool", lambda e, c=c, csv=csv: e.tensor_copy(out=csv[:, 0, 0:3], in_=ccar[:, c, 0:3]), reads=[r_ccar[c]], writes=[r_cst[ci]])
                P.op("act", lambda e, bk=bk, csv=csv, nsq=nsq, Ls=Ls: e.activation(out=csv[:, :, 3:3 + Ls], in_=bank[bk][:, :n].rearrange("p (s l) -> p s l", s=nsq), func=AF.Copy),
                     reads=[r_b[bk]], writes=[r_cst[ci]])
                P.op("pool", lambda e, c=c, csv=csv, nsq=nsq, Ls=Ls: e.tensor_copy(out=ccar[:, c, :3 * nsq].rearrange("p (s l) -> p s l", s=nsq), in_=csv[:, :, Ls:Ls + 3]),
                     reads=[r_cst[ci]], writes=[r_ccar[c]])
                ca = cacc[ci][:, :n].rearrange("p (s l) -> p s l", s=nsq)
                wc = CV_CW + 4 * c
                P.op("dve", lambda e, c=c, ca=ca, csv=csv, Ls=Ls, wc=wc: e.tensor_scalar(out=ca, in0=csv[:, :, 3:3 + Ls], scalar1=cvec[:, wc + 3:wc + 4], scalar2=cvec[:, CV_CB + c:CV_CB + c + 1], op0=ALU.mult, op1=ALU.add),
                     reads=[r_cst[ci]] + CR, writes=[r_cacc[ci]])
                for j in range(3):
                    P.op("dve", lambda e, j=j, ca=ca, csv=csv, Ls=Ls, wc=wc: e.scalar_tensor_tensor(out=ca, in0=csv[:, :, j:j + Ls], scalar=cvec[:, wc + j:wc + j + 1], in1=ca, op0=ALU.mult, op1=ALU.add),
                         reads=[r_cst[ci], r_cacc[ci]] + CR, writes=[r_cacc[ci]])
                P.op("act", lambda e, c=c, ci=ci: e.activation(out=xc[:, c, :n], in_=cacc[ci][:, :n], func=AF.Silu), reads=[r_cacc[ci]], writes=[r_xc[c]])
            if sample:
                for si_ in range(NSEQ_S):
                    P.dma("sp", lambda e, si_=si_: e.dma_start(out=convs_o[si_].rearrange("p (c l) -> p c l", l=3), in_=ccar[:, :, 3 * si_:3 * si_ + 3]), None, reads=r_ccar)
            elif b == NB - 1:
                P.dma("sp", lambda e: e.dma_start(out=conv_o.rearrange("p (c l) -> p c l", l=3), in_=ccar[:, :, 0:3]), None, reads=r_ccar)

            if SSTOP <= 2:
                return
            for (off, L, sq_) in segs:
                first = (b == 0 and off == 0) if not sample else True
                if sample:
                    P.dma("sp", lambda e, sq_=sq_: e.dma_start(out=hst[:], in_=ssm0_d[sq_]), None, writes=[r_hst])
                elif first:
                    P.op("pool", lambda e: e.memset(hst[:], 0.0), writes=[r_hst])
                if not light:
                    P.op("act", lambda e: e.activation(out=hbf, in_=hst[:], func=AF.Copy), reads=[r_hst], writes=[r_hbf])
                for c in range(8):
                    mm(bank["E"][:L, 0:32], xn[:, c, off:off + L], wdt[:, c, :], c == 0, c == 7, [r_xn] + CR, [r_b["E"]])
                P.op("dve", lambda e, L=L: e.tensor_tensor(out=dtt[:L, :], in0=bank["E"][:L, 0:32], in1=rowc[:L, 0:32], op=ALU.add), reads=[r_b["E"]] + CR, writes=[r_ss])
                P.op("act", lambda e, L=L: e.activation(out=dtt[:L, :], in_=dtt[:L, :], func=AF.Exp), reads=[r_ss], writes=[r_ss])
                P.op("act", lambda e, L=L: e.activation(out=dtt[:L, :], in_=dtt[:L, :], func=AF.Ln, bias=onec[:L, :], scale=1.0), reads=[r_ss] + CR, writes=[r_ss])
                P.op("dve", lambda e, L=L: e.tensor_tensor(out=at[:L, :], in0=dtt[:L, :], in1=A_bc[:L, :], op=ALU.mult), reads=[r_ss] + CR, writes=[r_ss])
                mm(bank["F"][:L, 0:32], tri_f[:L, :L], at[:L, :], True, True, [r_ss] + CR, [r_b["F"]])
                mm(bank["E"][:, 32:64], onesf[:L, :], at[:L, :], True, True, [r_ss] + CR, [r_b["E"]])
                P.op("dve", lambda e, L=L: e.tensor_copy(out=acs[:L, :], in_=bank["F"][:L, 0:32]), reads=[r_b["F"]], writes=[r_ss])
                P.op("dve", lambda e: e.tensor_copy(out=tot[:], in_=bank["E"][:, 32:64]), reads=[r_b["E"]], writes=[r_ss])
                P.op("dve", lambda e, L=L: e.tensor_tensor(out=dte[:L, :], in0=tot[:L, :], in1=acs[:L, :], op=ALU.subtract), reads=[r_ss], writes=[r_ss])
                P.op("act", lambda e, L=L: e.activation(out=dte[:L, :], in_=dte[:L, :], func=AF.Exp), reads=[r_ss], writes=[r_ss])
                P.op("act", lambda e, L=L: e.activation(out=eacs[:L, :], in_=acs[:L, :], func=AF.Exp), reads=[r_ss], writes=[r_ss])
                P.op("act", lambda e: e.activation(out=cdec[:], in_=tot[:], func=AF.Exp), reads=[r_ss], writes=[r_ss])
                for c in range(16):
                    P.op("pe", lambda e, c=c, off=off, L=L: e.transpose(pTB[:L, c * 128:(c + 1) * 128], xc[:, c, off:off + L], ident[:]), reads=[r_xc[c]] + CR, writes=[r_tb])
                x3 = pTB[:L, :].rearrange("p (h d) -> p h d", d=64)
                P.op("dve", lambda e, L=L, x3=x3: e.tensor_tensor(out=xdt[:L, :].rearrange("p (h d) -> p h d", d=64), in0=x3, in1=dtt[:L, :].unsqueeze(2).to_broadcast([L, 32, 64]), op=ALU.mult),
                     reads=[r_tb, r_ss], writes=[r_xdt])
                P.op("dve", lambda e, L=L: e.tensor_tensor(out=xw[:L, :].rearrange("p (h d) -> p h d", d=64), in0=xdt[:L, :].rearrange("p (h d) -> p h d", d=64), in1=dte[:L, :].unsqueeze(2).to_broadcast([L, 32, 64]), op=ALU.mult),
                     reads=[r_xdt, r_ss], writes=[r_xw])
                for g in range(4):
                    P.op("pe", lambda e, g=g, off=off, L=L: e.transpose(pTB[:L, g * 128:(g + 1) * 128], xc[:, 16 + g, off:off + L], ident[:]), reads=[r_xc[16 + g], r_xdt] + CR, writes=[r_tb])
                P.op("act", lambda e, L=L: e.activation(out=Btok[:L, :], in_=pTB[:L, 0:512], func=AF.Copy), reads=[r_tb], writes=[r_Btok])
                for g in range(4):
                    Bt = xc[:, 16 + g, off:off + L]; Ct = xc[:, 20 + g, off:off + L]
                    rB, rC = r_xc[16 + g], r_xc[20 + g]
                    if light:
                        mm(bank["E"][:, :], Btok[:L, g * 128:(g + 1) * 128], xw[:L, g * 512:(g + 1) * 512], True, True, [r_Btok, r_xw], [r_b["E"]])
                        hs3 = hst[:, g * 512:(g + 1) * 512].rearrange("p (h d) -> p h d", d=64)
                        P.op("pool", lambda e, g=g, hs3=hs3: e.tensor_tensor(out=hs3, in0=hs3, in1=cdec[:, g * 8:(g + 1) * 8].unsqueeze(2).to_broadcast([128, 8, 64]), op=ALU.mult),
                             reads=[r_ss], writes=[r_hst])
                        P.op("dve", lambda e, g=g: e.tensor_tensor(out=hst[:, g * 512:(g + 1) * 512], in0=hst[:, g * 512:(g + 1) * 512], in1=bank["E"][:, :], op=ALU.add),
                             reads=[r_b["E"]], writes=[r_hst])
                        continue
                    mm(bank["F"][:L, :L], Bt, Ct, True, True, [rB, rC], [r_b["F"]])
                    P.op("dve", lambda e, L=L: e.tensor_tensor(out=cbm[:L, :L], in0=bank["F"][:L, :L], in1=mask01[:L, :L], op=ALU.mult), reads=[r_b["F"]] + CR, writes=[r_cbm])
                    P.op("pool", lambda e, g=g, L=L: e.tensor_tensor(out=Dg[:L, :, :L], in0=at[:L, g * 8:(g + 1) * 8].unsqueeze(2).to_broadcast([L, 8, L]), in1=tri_f[:L, :L].unsqueeze(1).to_broadcast([L, 8, L]), op=ALU.mult),
                         reads=[r_ss] + CR, writes=[r_Dg])
                    segp = pAB[:L, :].rearrange("p (h l) -> p h l", l=128)
                    if L == 128:
                        for hh in range(2):
                            P.op("pe", lambda e, hh=hh, L=L, segp=segp: e.matmul(segp[:, hh * 4:(hh + 1) * 4, :L], lhsT=ustr_f[:L, :L], rhs=Dg[:L, hh * 4:(hh + 1) * 4, :L], start=True, stop=True),
                                 reads=[r_Dg] + CR, writes=[r_b["AB"[hh]]])
                    else:
                        for hh in range(8):
                            P.op("pe", lambda e, hh=hh, L=L, segp=segp: e.matmul(segp[:, hh, :L], lhsT=ustr_f[:L, :L], rhs=Dg[:L, hh, :L], start=True, stop=True),
                                 reads=[r_Dg] + CR, writes=[r_b["AB"[hh // 4]]])
                    MT3 = MT[:L, :].rearrange("p (h l) -> p h l", l=128)
                    MTb3 = MTb[:L, :].rearrange("p (h l) -> p h l", l=128)
                    P.op("act", lambda e, L=L, segp=segp, MT3=MT3: e.activation(out=MT3[:, :, :L], in_=segp[:, :, :L], func=AF.Exp), reads=[r_b["A"], r_b["B"]], writes=[r_MT])
                    P.op("dve", lambda e, L=L, MT3=MT3, MTb3=MTb3: e.tensor_tensor(out=MTb3[:, :, :L], in0=MT3[:, :, :L], in1=cbm[:L, :L].unsqueeze(1).to_broadcast([L, 8, L]), op=ALU.mult),
                         reads=[r_MT, r_cbm], writes=[r_MTb])
                    for hh in range(8):
                        h = g * 8 + hh
                        mm(bank["C"][:L, hh * 64:(hh + 1) * 64], MTb3[:, hh, :L], xdt[:L, h * 64:(h + 1) * 64], True, True, [r_MTb, r_xdt], [r_b["C"]])
                    mm(bank["D"][:L, :], Ct, hbf[:, g * 512:(g + 1) * 512], True, True, [rC, r_hbf], [r_b["D"]])
                    P.op("dve", lambda e, g=g, L=L: e.tensor_tensor(out=tmpf[:L, :].rearrange("p (h d) -> p h d", d=64), in0=bank["D"][:L, :].rearrange("p (h d) -> p h d", d=64),
                                                                   in1=eacs[:L, g * 8:(g + 1) * 8].unsqueeze(2).to_broadcast([L, 8, 64]), op=ALU.mult),
                         reads=[r_b["D"], r_ss], writes=[r_tmpf])
                    P.op("dve", lambda e, g=g, L=L: e.tensor_tensor(out=ytok[:L, g * 512:(g + 1) * 512], in0=bank["C"][:L, :], in1=tmpf[:L, :], op=ALU.add),
                         reads=[r_b["C"], r_tmpf], writes=[r_ytok])
                    mm(bank["E"][:, :], Btok[:L, g * 128:(g + 1) * 128], xw[:L, g * 512:(g + 1) * 512], True, True, [r_Btok, r_xw], [r_b["E"]])
                    hs3 = hst[:, g * 512:(g + 1) * 512].rearrange("p (h d) -> p h d", d=64)
                    P.op("pool", lambda e, g=g, hs3=hs3: e.tensor_tensor(out=hs3, in0=hs3, in1=cdec[:, g * 8:(g + 1) * 8].unsqueeze(2).to_broadcast([128, 8, 64]), op=ALU.mult),
                         reads=[r_ss, r_hbf], writes=[r_hst])
                    P.op("dve", lambda e, g=g: e.tensor_tensor(out=hst[:, g * 512:(g + 1) * 512], in0=hst[:, g * 512:(g + 1) * 512], in1=bank["E"][:, :], op=ALU.add),
                         reads=[r_b["E"]], writes=[r_hst])
                for c in range(0 if light else 16):
                    P.op("pe", lambda e, c=c, L=L: e.transpose(pTB[:, c * 128:c * 128 + L], ytok[:L, c * 128:(c + 1) * 128], ident[:L, :L]), reads=[r_ytok] + CR, writes=[r_tb])
                    P.op("dve", lambda e, c=c, off=off, L=L: e.scalar_tensor_tensor(out=yT[:, c, off:off + L], in0=xc[:, c, off:off + L], scalar=cvec[:, CV_DS + c:CV_DS + c + 1],
                                                                                 in1=pTB[:, c * 128:c * 128 + L], op0=ALU.mult, op1=ALU.add),
                         reads=[r_tb, r_xc[c]] + CR, writes=[r_yT[c]])
                if sample:
                    P.dma("sp", lambda e, sq_=sq_: e.dma_start(out=ssms_o[sq_], in_=hst[:]), None, reads=[r_hst])
                elif b == NB - 1 and off + L == n:
                    P.dma("sp", lambda e: e.dma_start(out=ssm_o, in_=hst[:]), None, reads=[r_hst])
            if SSTOP <= 3:
                return
            if light:
                wskip(16 + 48 + 8 + 68)
                return
            for c in range(16):
                w, rw = wtile(); bk = "AB"[c % 2]; si = c % 2
                proj8(bk, n, w, rw)
                P.op("act", lambda e, bk=bk, si=si: e.activation(out=stg[si][:, :n], in_=bank[bk][:, :n], func=AF.Silu), reads=[r_b[bk]], writes=[r_stg[si]])
                P.op("dve", lambda e, c=c, si=si: e.tensor_tensor(out=yT[:, c, :n], in0=yT[:, c, :n], in1=stg[si][:, :n], op=ALU.mult), reads=[r_stg[si], r_yT[c]], writes=[r_yT[c]])
                P.op("act", lambda e, c=c: e.activation(out=sqb[:, c % 4, :n], in_=yT[:, c, :n], func=AF.Square), reads=[r_yT[c]], writes=[r_sq[c % 4]])
                mm(bank["E"][:, :n], ones_g[:], sqb[:, c % 4, :n], c % 4 == 0, c % 4 == 3, [r_sq[c % 4]] + CR, [r_b["E"]])
                if c % 4 == 3:
                    rms_rstd(bank["E"][:, :n], r_b["E"], n)
                    for cc in range(c - 3, c + 1):
                        P.op("dve", lambda e, cc=cc: e.scalar_tensor_tensor(out=yT[:, cc, :n], in0=yT[:, cc, :n], scalar=cvec[:, CV_SN + cc:CV_SN + cc + 1], in1=rstd[:, :n], op0=ALU.mult, op1=ALU.mult),
                             reads=[r_rstd, r_yT[cc]] + CR, writes=[r_yT[cc]])

            if SSTOP <= 4:
                return
            attention(n, segs, sample, b)

            if SSTOP <= 5:
                return
            oT3 = oT[0:64, 0:16, :]
            for m in range(8):
                w0, r0 = wtile(); w1, r1 = wtile()
                for c in range(16):
                    w, r = (w0, r0) if c < 8 else (w1, r1)
                    mm(bank["A"][:, :n], w[:, (c % 8) * 128:(c % 8 + 1) * 128], yT[:, c, :n], c == 0, c == 15, [r, r_yT[c]], [r_b["A"]])
                wg_, rg_ = wtile()
                proj8("C", n, wg_, rg_)
                P.op("act", lambda e: e.activation(out=stg[0][:, :n], in_=bank["C"][:, :n], func=AF.Sigmoid), reads=[r_b["C"]], writes=[r_stg[0]])
                P.op("dve", lambda e: e.tensor_tensor(out=stg[2][:, :n], in0=bank["A"][:, :n], in1=stg[0][:, :n], op=ALU.mult), reads=[r_b["A"], r_stg[0]], writes=[r_stg[2]])
                w0, r0 = wtile(); w1, r1 = wtile()
                for h in range(16):
                    w, r = (w0, r0) if h < 8 else (w1, r1)
                    mm(bank["B"][:, :n], w[0:64, (h % 8) * 128:(h % 8 + 1) * 128], oT3[:, h, :n], h == 0, h == 15, [r, r_oT], [r_b["B"]])
                wg_, rg_ = wtile()
                proj8("D", n, wg_, rg_)
                P.op("act", lambda e: e.activation(out=stg[1][:, :n], in_=bank["D"][:, :n], func=AF.Sigmoid), reads=[r_b["D"]], writes=[r_stg[1]])
                P.op("dve", lambda e: e.tensor_tensor(out=stg[1][:, :n], in0=bank["B"][:, :n], in1=stg[1][:, :n], op=ALU.mult), reads=[r_b["B"], r_stg[1]], writes=[r_stg[1]])
                P.op("dve", lambda e, m=m: e.tensor_tensor(out=merged[:, m, :n], in0=stg[1][:, :n], in1=stg[2][:, :n], op=ALU.add), reads=[r_stg[1], r_stg[2], r_qT], writes=[r_mg])
            for m in range(8):
                w, rw = wtile(); bk = "AB"[m % 2]
                for c in range(8):
                    mm(bank[bk][:, :n], w[:, c * 128:(c + 1) * 128], merged[:, c, :n], c == 0, c == 7, [rw, r_mg], [r_b[bk]])
                P.op("dve", lambda e, m=m, bk=bk: e.tensor_tensor(out=hT[:, m, :n], in0=hT[:, m, :n], in1=bank[bk][:, :n], op=ALU.add), reads=[r_b[bk], r_hT], writes=[r_hT])
            if SSTOP <= 6:
                return
            norm_to_xn(n, CV_N2)
            ffn(n)
            ydst = ysT_o if sample else yT_o[:, ob_:ob_ + n]
            P.dma("sp", lambda e: e.dma_start(out=ydst.rearrange("(c p) t -> p c t", p=128), in_=hT[:, :, :n]), None, reads=[r_hT])

        def attn_tile(kv, qcols_ap, nqc, kt_ap, nk, v_ap, bias_ap, first, last, diag, acc_bank, rd):
            pass

        sT_bufs = [(sTt, r_sTt), (stg[0], r_stg[0]), (stg[1], r_stg[1])]
        PT_bufs = [(PT, r_PT), (asl(0, 1), Res("PT1")), (asl(1, 2), Res("PT2")), (asl(2, 3), Res("PT3"))]
        att_it = [0]

        def attention(n, segs, sample, b):
            tb = b * T
            for (off, L, sq_) in segs:
                if not sample:
                    qi = (tb + off) // 128
                    P.op("dve", lambda e, off=off, L=L: e.tensor_scalar(out=dgl[:], in0=identf[:16, :16], scalar1=cT[:, off + L - 1:off + L], scalar2=None, op0=ALU.mult), reads=[r_cT] + CR, writes=[r_dgl])
                    mm(bank["F"][:, 0:16], onesf[:16, :], dgl[:], True, True, [r_dgl] + CR, [r_b["F"]])
                    P.op("dve", lambda e: e.tensor_copy(out=cref[:], in_=bank["F"][:, 0:16]), reads=[r_b["F"]], writes=[r_cref])
                    nkt = qi + 1
                    P.op("dve", lambda e, nkt=nkt: e.tensor_tensor(out=biasb[:, :nkt, :], in0=cref[:].unsqueeze(1).to_broadcast([128, nkt, 16]), in1=cks[:, :nkt, :], op=ALU.subtract),
                         reads=[r_cref, r_ck], writes=[r_bias])
                    if NL > 0:
                        P.op("dve", lambda e: e.tensor_scalar(out=biasb[:, :NL * 4, :], in0=biasb[:, :NL * 4, :], scalar1=flg[:, 1:2], scalar2=None, op0=ALU.add),
                             reads=[r_bias] + CR, writes=[r_bias])
                    for kv in range(4):
                        half = kv % 2; cp = kv // 2
                        ob = "CD"[kv % 2]
                        qap = qT[half * 64:(half + 1) * 64, 4 * cp:4 * cp + 4, off:off + L]
                        def emit_st(kt, half=half, cp=cp, qap=qap, kv=kv, L=L, qi=qi):
                            sbk = "AB"[kt % 2]
                            dg = (kt == qi)
                            P.op("pe", lambda e: e.matmul(bank[sbk][:, :4 * L].rearrange("p (g t) -> p g t", g=4), lhsT=kTs[half * 64:(half + 1) * 64, cp, kt * 128:(kt + 1) * 128], rhs=qap, start=True, stop=True),
                                 reads=[r_kT, r_qT], writes=[r_b[sbk]])
                            it = att_it[0]; att_it[0] += 1
                            sTb, r_sTb = sT_bufs[it % 3]
                            PTb, r_PTb = PT_bufs[it % 4]
                            P.op("dve", lambda e: e.scalar_tensor_tensor(out=sTb[:, :4 * L].rearrange("p (g t) -> p g t", g=4), in0=bank[sbk][:, :4 * L].rearrange("p (g t) -> p g t", g=4), scalar=1.0,
                                                                         in1=biasb[:, kt, 4 * kv:4 * kv + 4].unsqueeze(2).to_broadcast([128, 4, L]), op0=ALU.mult, op1=ALU.add),
                                 reads=[r_b[sbk], r_bias], writes=[r_sTb])
                            P.op("act", lambda e: e.activation(out=PTb[:, :4 * L], in_=sTb[:, :4 * L], func=AF.Exp), reads=[r_sTb], writes=[r_PTb])
                            if dg:
                                P.op("pool", lambda e: e.affine_select(out=PTb[:, :4 * L].rearrange("p (g t) -> p g t", g=4), in_=PTb[:, :4 * L].rearrange("p (g t) -> p g t", g=4), pattern=[[0, 4], [1, L]], compare_op=ALU.is_ge, fill=0.0, base=0, channel_multiplier=-1),
                                     reads=[r_PTb], writes=[r_PTb])
                            return PTb, r_PTb
                        pend = emit_st(0)
                        for kt in range(nkt):
                            cur = pend
                            if kt + 1 < nkt:
                                pend = emit_st(kt + 1)
                            mm(bank[ob][0:65, :4 * L], Vs[:, kt, kv, :], cur[0][:, :4 * L], kt == 0, kt == nkt - 1, [r_V, cur[1]], [r_b[ob]])
                        finish_head(kv, ob, off, L, 4 * L)
                else:
                    sample_attention(off, L, sq_)

        def finish_head(kv, ob, off, L, ncol):
            P.op("dve", lambda e: e.reciprocal(out=rec[64:65, :ncol], in_=bank[ob][64:65, :ncol]), reads=[r_b[ob]], writes=[r_rec])
            mm(bank["E"][0:64, :ncol], onesf[64:65, 0:64], rec[64:65, :ncol], True, True, [r_rec] + CR, [r_b["E"]])
            P.op("act", lambda e: e.activation(out=bcs[:, :ncol], in_=bank["E"][0:64, :ncol], func=AF.Copy), reads=[r_b["E"]], writes=[r_bcs])
            P.op("dve", lambda e: e.tensor_tensor(out=oT[0:64, 4 * kv:4 * kv + 4, off:off + L], in0=bank[ob][0:64, :ncol].rearrange("p (g t) -> p g t", g=4), in1=bcs[:, :ncol].rearrange("p (g t) -> p g t", g=4), op=ALU.mult),
                 reads=[r_b[ob], r_bcs] + r_xc, writes=[r_oT])

        if do_sample:
            qS = sb("qS", [128, NSEQ_S, 8, LS], BF16); r_qS = Res("qS")
            ksb = sb("ksb", [128, 2, NSEQ_S, 128], BF16)
            P.op("pool", lambda e: e.memset(ksb[:].rearrange("p a b c -> p (a b c)"), 0.0), writes=[r_kT])
            Vsm = sb("Vsm", [128, NSEQ_S, 4, 65], BF16)
            P.op("pool", lambda e: e.memset(Vsm[:].rearrange("p a b c -> p (a b c)"), 1.0), writes=[r_V])
            ptab = sb("ptab", [128, NSEQ_S * NPG], I32); r_pt = Res("ptab")
            ridx = ptab
            piota = sb("piota", [128, 1], I32)
            P.dma("sp", lambda e: e.dma_start(out=ptab[:], in_=pt_d.partition_broadcast(128)), None, writes=[r_pt])
            P.op("pool", lambda e: e.iota(piota[:], pattern=[[0, 1]], base=0, channel_multiplier=1), writes=[r_pt])
            P.op("pool", lambda e: e.tensor_scalar(out=ridx[:], in0=ptab[:], scalar1=128, scalar2=piota[:, 0:1], op0=ALU.mult, op1=ALU.add), reads=[r_pt], writes=[r_pt])
            import os as _os
            NPB = int(_DBG.get("K_NPB", "4"))
            r_kvr = [Res(f"kvr{i}") for i in range(2)]; r_vpg = [Res(f"vpg{i}") for i in range(2)]
            r_lpg = [Res(f"lpg{i}") for i in range(2)]; r_ktp = Res("ktp")
            PGE = NPB * 256; VGE = NPB * 4 * 65
            if 2 * SEQ >= 5 * PGE + 2 * VGE:
                kflat = kTs[:].rearrange("p a b -> p (a b)")
                kvr = [kflat[:, 2 * i * PGE:2 * (i + 1) * PGE].rearrange("p (a b) -> p a b", a=NPB) for i in range(2)]
                ktp = kflat[:, 4 * PGE:5 * PGE].rearrange("p (a b c) -> p a b c", a=NPB, b=2)
                vpg = [kflat[:, 5 * PGE + i * VGE:5 * PGE + (i + 1) * VGE].rearrange("p (a b c) -> p a b c", a=NPB, b=4) for i in range(2)]
            else:
                kvr = [sb(f"kvr{i}", [128, NPB, 512], BF16) for i in range(2)]
                vpg = [sb(f"vpg{i}", [128, NPB, 4, 65], BF16) for i in range(2)]
                ktp = sb("ktp", [128, NPB, 2, 128], BF16)
            lpg = [sb(f"lpg{i}", [128, NPB, 16]) for i in range(2)]
            s_pg = [P.slot(f"pg{i}") for i in range(2)]
            Racc = sb("Racc", [128, 16]); r_R = Res("Racc")
            bpg = sb("bpg", [128, NPB, 16]); r_bpg = Res("bpg")
            lftok = sb("lftok", [128, 16]); r_lftok = Res("lftok")
            oacc = sb("oacc", [128, 16 * LS]); r_oacc = Res("oacc")
            bph = sb("bph", [128, 2, NPB * 2, 4]); r_bph = Res("bph")

        def sample_attention(off, L, sq_):
            nq = 4 * L
            if sq_ == 0:
                for i in range(2):
                    P.op("pool", lambda e, i=i: e.memset(vpg[i][:].rearrange("p a b c -> p (a b c)"), 1.0), writes=[r_vpg[i], r_kT])
                P.op("dve", lambda e: e.tensor_copy(out=qS[:].rearrange("p s c t -> p c s t"), in_=qT[:, :, :TS].rearrange("p c (s t) -> p c s t", s=NSEQ_S)), reads=[r_qT], writes=[r_qS])
            import os as _os
            nb = 0 if _DBG.get('K_NOPAGES') else NPG // NPB
            P.op("pool", lambda e: e.memset(Racc[:], 0.0), writes=[r_R])
            mm(bank["F"][:L, 0:16], lfT[:, off:off + L], identf[:16, :16], True, True, [r_lf] + CR, [r_b["F"]])
            P.op("dve", lambda e: e.tensor_copy(out=lftok[:L, :], in_=bank["F"][:L, 0:16]), reads=[r_b["F"]], writes=[r_lftok])
            P.op("dve", lambda e: e.tensor_copy(out=Racc[:L, :], in_=bank["F"][:L, 0:16]), reads=[r_b["F"], r_R], writes=[r_R])
            mm(bank["F"][:, 16:32], ustr_f[:L, :], lftok[:L, :], True, True, [r_lftok] + CR, [r_b["F"]])
            P.op("dve", lambda e: e.tensor_copy(out=bpg[:, 0, :], in_=bank["F"][:, 16:32]), reads=[r_b["F"]], writes=[r_bpg])
            import os as _os
            ASTOP = float(_DBG.get("K_ASTOP", "99"))
            if ASTOP <= 1:
                return
            for kv in range(4):
                half = kv % 2; cp = kv // 2; bk = "AB"[half]
                qap = qS[half * 64:(half + 1) * 64, sq_, 4 * cp:4 * cp + 4, :].rearrange("p c t -> p (c t)")
                P.op("pe", lambda e, half=half, cp=cp, qap=qap, bk=bk: e.matmul(bank[bk][:, cp * nq:(cp + 1) * nq], lhsT=ksb[half * 64:(half + 1) * 64, cp, sq_, :], rhs=qap, start=True, stop=True),
                     reads=[r_kT, r_qS], writes=[r_b[bk]])
            if ASTOP <= 1.2:
                return
            for half in range(2):
                bk = "AB"[half]
                P.op("dve", lambda e, half=half: e.tensor_copy(out=bph[:, half, 0:2, :], in_=bpg[:, 0, :].rearrange("p (c h g) -> p c h g", c=2, h=2)[:, :, half, :]), reads=[r_bpg], writes=[r_bph])
                o3 = sTt[:, half * 2 * nq:(half + 1) * 2 * nq].rearrange("p (a t) -> p a t", t=L)
                i3 = bank[bk][:, :2 * nq].rearrange("p (a t) -> p a t", t=L)
                b3 = bph[:, half, 0:2, :].rearrange("p c g -> p (c g)").unsqueeze(2).to_broadcast([128, 8, L])
                P.op("dve", lambda e, o3=o3, i3=i3, b3=b3: e.scalar_tensor_tensor(out=o3, in0=i3, scalar=1.0, in1=b3, op0=ALU.mult, op1=ALU.add),
                     reads=[r_b[bk], r_bph], writes=[r_sTt])
            if ASTOP <= 1.4:
                return
            P.op("act", lambda e: e.activation(out=PT[:, :4 * nq], in_=sTt[:, :4 * nq], func=AF.Exp), reads=[r_sTt], writes=[r_PT])
            if ASTOP <= 1.6:
                return
            P.op("dve", lambda e: e.tensor_tensor(out=PT[:, :4 * nq].rearrange("p (h t) -> p h t", h=16), in0=PT[:, :4 * nq].rearrange("p (h t) -> p h t", h=16), in1=mask01[:, :L].unsqueeze(1).to_broadcast([128, 16, L]), op=ALU.mult),
                 reads=[r_PT] + CR, writes=[r_PT])
            if ASTOP <= 2:
                return
            for kv in range(4):
                pc = ((kv % 2) * 2 + kv // 2) * nq
                mm(bank["C"][0:65, kv * nq:(kv + 1) * nq], Vsm[:, sq_, kv, :], PT[:, pc:pc + nq], True, True, [r_V, r_PT], [r_b["C"]])
            P.op("dve", lambda e: e.tensor_copy(out=oacc[0:65, :4 * nq], in_=bank["C"][0:65, :4 * nq]), reads=[r_b["C"]], writes=[r_oacc])
            if ASTOP <= 3:
                return
            for bi in range(nb):
                pb = nb - 1 - bi
                i2 = bi % 2
                for pp in range(NPB):
                    pg = pb * NPB + pp
                    col = sq_ * NPG + pg
                    P.dma("pool", lambda e, i2=i2, pp=pp, col=col: e.indirect_dma_start(out=kvr[i2][:, pp, :], out_offset=None, in_=ckv_d, in_offset=bass.IndirectOffsetOnAxis(ap=ridx[:, col:col + 1], axis=0)),
                          None, reads=[r_pt], writes=[r_kvr[i2]])
                    P.dma("pool", lambda e, i2=i2, pp=pp, col=col: e.indirect_dma_start(out=lpg[i2][:, pp, :], out_offset=None, in_=clf_d, in_offset=bass.IndirectOffsetOnAxis(ap=ridx[:, col:col + 1], axis=0)),
                          None, reads=[r_pt], writes=[r_lpg[i2]])
                P.op("act", lambda e, i2=i2: e.activation(out=vpg[i2][:, :, :, 0:64], in_=kvr[i2][:, :, 256:512].rearrange("p a (b c) -> p a b c", b=4), func=AF.Copy), reads=[r_kvr[i2]], writes=[r_vpg[i2]])
                for pp in range(NPB):
                    for c in range(2):
                        P.op("pe", lambda e, i2=i2, pp=pp, c=c: e.transpose(pTB[:, (pp * 2 + c) * 128:(pp * 2 + c + 1) * 128], kvr[i2][:, pp, c * 128:(c + 1) * 128], ident[:]),
                             reads=[r_kvr[i2]] + CR, writes=[r_tb])
                P.op("act", lambda e: e.activation(out=ktp[:].rearrange("p a b c -> p (a b c)"), in_=pTB[:, 0:NPB * 256], func=AF.Copy), reads=[r_tb], writes=[r_ktp])
                for pp in reversed(range(NPB)):
                    mm(bank["F"][:, 0:16], ustr_f[:], lpg[i2][:, pp, :], True, False, [r_lpg[i2]] + CR, [r_b["F"]])
                    mm(bank["F"][:, 0:16], onesf[:], Racc[:], False, True, [r_R] + CR, [r_b["F"]])
                    P.op("dve", lambda e, pp=pp: e.tensor_copy(out=bpg[:, pp, :], in_=bank["F"][:, 0:16]), reads=[r_b["F"]], writes=[r_bpg])
                    P.op("dve", lambda e, pp=pp, i2=i2: e.tensor_tensor(out=Racc[:], in0=Racc[:], in1=lpg[i2][:, pp, :], op=ALU.add), reads=[r_lpg[i2], r_R], writes=[r_R])
                for pp in range(NPB):
                    for kv in range(4):
                        half = kv % 2; cp = kv // 2; bk = "AB"[half]
                        qap = qS[half * 64:(half + 1) * 64, sq_, 4 * cp:4 * cp + 4, :].rearrange("p c t -> p (c t)")
                        c0 = (pp * 2 + cp) * nq
                        P.op("pe", lambda e, half=half, cp=cp, qap=qap, pp=pp, c0=c0, bk=bk: e.matmul(bank[bk][:, c0:c0 + nq], lhsT=ktp[half * 64:(half + 1) * 64, pp, cp, :], rhs=qap, start=True, stop=True),
                             reads=[r_ktp, r_qS], writes=[r_b[bk]])
                tot_c = NPB * 4 * nq
                hc = NPB * 2 * nq
                for half in range(2):
                    bk = "AB"[half]
                    P.op("dve", lambda e, half=half: e.tensor_copy(out=bph[:, half, :, :], in_=bpg[:].rearrange("p n (c h g) -> p (n c) h g", c=2, h=2)[:, :, half, :]), reads=[r_bpg], writes=[r_bph])
                    o3 = sTt[:, half * hc:(half + 1) * hc].rearrange("p (a t) -> p a t", t=L)
                    i3 = bank[bk][:, :hc].rearrange("p (a t) -> p a t", t=L)
                    b3 = bph[:, half, :, :].rearrange("p a g -> p (a g)").unsqueeze(2).to_broadcast([128, NPB * 8, L])
                    P.op("dve", lambda e, o3=o3, i3=i3, b3=b3: e.scalar_tensor_tensor(out=o3, in0=i3, scalar=1.0, in1=b3, op0=ALU.mult, op1=ALU.add),
                         reads=[r_b[bk], r_bph], writes=[r_sTt])
                P.op("act", lambda e: e.activation(out=PT[:, :tot_c], in_=sTt[:, :tot_c], func=AF.Exp), reads=[r_sTt], writes=[r_PT])
                for pp in range(NPB):
                    for kv in range(4):
                        c0 = (pp * 4 + kv) * nq
                        pc = ((kv % 2) * NPB * 2 + pp * 2 + kv // 2) * nq
                        mm(bank["C"][0:65, c0:c0 + nq], vpg[i2][:, pp, kv, :], PT[:, pc:pc + nq], True, True, [r_vpg[i2], r_PT], [r_b["C"]])
                for pp in range(NPB):
                    P.op("dve", lambda e, pp=pp: e.tensor_tensor(out=oacc[0:65, :4 * nq], in0=oacc[0:65, :4 * nq], in1=bank["C"][0:65, pp * 4 * nq:(pp + 1) * 4 * nq], op=ALU.add),
                         reads=[r_b["C"], r_oacc], writes=[r_oacc])
            ncol = 4 * nq
            P.op("dve", lambda e: e.reciprocal(out=rec[64:65, :ncol], in_=oacc[64:65, :ncol]), reads=[r_oacc], writes=[r_rec])
            mm(bank["E"][0:64, :ncol], onesf[64:65, 0:64], rec[64:65, :ncol], True, True, [r_rec] + CR, [r_b["E"]])
            P.op("dve", lambda e: e.tensor_tensor(out=oT[0:64, 0:16, off:off + L], in0=oacc[0:64, :ncol].rearrange("p (h t) -> p h t", h=16), in1=bank["E"][0:64, :ncol].rearrange("p (h t) -> p h t", h=16), op=ALU.mult),
                 reads=[r_b["E"], r_oacc] + r_xc, writes=[r_oT])

        import os as _os
        for b in range(0 if _DBG.get("K_NOPROMPT") else NB):
            block(T, [(i * 128, 128, 0) for i in range(4)], False, b)
        if do_sample:
            block(TS, [(i * LS, LS, i) for i in range(NSEQ_S)], True, 0)
        print("sbuf bytes remaining", nc.sbuf_bytes_remaining, flush=True)
        n, ns = P.emit(nc)
        print("ops", n, "signals", ns, flush=True)
    return nc


_PROG_CACHE = {}


def kernel(x_prompt, x_sample, cache_k, cache_v, cache_logf, state_ssm, state_conv, page_table,
           ffn1_norm, ffn1_w_gate, ffn1_w_up, ffn1_w_down, mix_norm, w_in, conv_w, conv_b,
           dt_bias, a_log, d_skip, ssd_norm, q_norm, k_norm, b_f, w_ssd_proj, w_attn_proj, w_out,
           ffn2_norm, ffn2_w_gate, ffn2_w_up, ffn2_w_down):
    f = lambda a: np.asarray(a, dtype=np.float32)
    B, SEQ, _ = x_prompt.shape
    DB, DS, _ = x_sample.shape
    NPOOL = cache_k.shape[1]
    NPG = page_table.shape[1]
    p = dict(ffn1_norm=f(ffn1_norm)[0], ffn1_w_gate=f(ffn1_w_gate)[0], ffn1_w_up=f(ffn1_w_up)[0], ffn1_w_down=f(ffn1_w_down)[0],
             mix_norm=f(mix_norm)[0], w_in=f(w_in)[0], conv_w=f(conv_w)[0], conv_b=f(conv_b)[0], dt_bias=f(dt_bias)[0],
             a_log=f(a_log)[0], d_skip=f(d_skip)[0], ssd_norm=f(ssd_norm)[0], q_norm=f(q_norm)[0], k_norm=f(k_norm)[0],
             b_f=f(b_f)[0], w_ssd_proj=f(w_ssd_proj)[0], w_attn_proj=f(w_attn_proj)[0], w_out=f(w_out)[0],
             ffn2_norm=f(ffn2_norm)[0], ffn2_w_gate=f(ffn2_w_gate)[0], ffn2_w_up=f(ffn2_w_up)[0], ffn2_w_down=f(ffn2_w_down)[0])
    wst = build_weight_stream(p)
    cvec = build_cvec(p)
    rowc = np.concatenate([p["dt_bias"], p["a_log"], p["d_skip"]]).reshape(1, 96).astype(np.float32)
    wdt = np.ascontiguousarray(p["w_in"][:, O_DT:O_DT + 32].reshape(8, 128, 32).transpose(1, 0, 2).reshape(128, 256))
    ckv = np.concatenate([f(cache_k)[0].reshape(NPOOL * 128, 256), f(cache_v)[0].reshape(NPOOL * 128, 256)], axis=1)
    cl2 = np.ascontiguousarray(f(cache_logf)[0].reshape(NPOOL * 128, 16))
    xp = f(x_prompt); xs = f(x_sample)
    ssm = f(state_ssm)[0]; cst = f(state_conv)[0]
    pt = np.asarray(page_table, dtype=np.int32)
    NLh = (SEQ // 512) // 2
    key = (SEQ, NPG, NPOOL)
    if key not in _PROG_CACHE:
        import os as _os
        _PROG_CACHE[key] = build_program(SEQ, NPG, NPOOL, do_sample=(_DBG.get('K_NOSAMPLE') is None))
    nc = _PROG_CACHE[key]
    in_maps = []
    for c in range(NCORES):
        sq = slice(NSEQ_S * c, NSEQ_S * (c + 1))
        in_maps.append({
            "xT": np.ascontiguousarray((xp[c % B] if (c // B == 1 or NLh == 0) else np.concatenate([xp[c % B][:SEQ // 2], xp[c % B][:SEQ // 2]])).T),
            "flg": np.tile(np.array([[1.0, 0.0]] if (c // B == 1 or NLh == 0) else [[0.0, -30000.0]], np.float32), (128, 1)),
            "xsT": np.ascontiguousarray(xs[sq].reshape(NSEQ_S * DS, D).T),
            "wst": wst, "cvec": cvec, "rowc": rowc, "wdt": wdt,
            "ssm0": np.ascontiguousarray(ssm[sq].reshape(NSEQ_S, 2048, 128).transpose(0, 2, 1)),
            "conv0": np.ascontiguousarray(cst[sq].reshape(NSEQ_S, 3, 24, 128).transpose(0, 3, 2, 1).reshape(NSEQ_S, 128, 72)),
            "cache_kv": ckv, "cache_lf": cl2,
            "ptab": np.ascontiguousarray(pt[sq].reshape(1, NSEQ_S * NPG)),
        })
    import os as _os
    if _DBG.get("K_TRACE"):
        _r = run_bass_kernel_spmd(nc, in_maps, core_ids=list(range(NCORES)), trace=True)
        print("EXEC_NS", _r.exec_time_ns, flush=True)
        res = _r.results
    else:
        res = run_bass_kernel_spmd(nc, in_maps, core_ids=list(range(NCORES))).results
    def cat(name, b):
        if NLh == 0:
            return res[b][name].T
        return np.concatenate([res[b][name].T, res[b + B][name].T], axis=0)
    fb = 0 if NLh == 0 else B
    yp = np.stack([cat("yT", b) for b in range(B)])
    kp = np.stack([cat("kT", b).reshape(SEQ, 4, 64) for b in range(B)])[None]
    vp = np.stack([cat("vT", b).reshape(SEQ, 4, 64) for b in range(B)])[None]
    lp = np.stack([cat("lfT", b) for b in range(B)])[None]
    sp = np.stack([res[b + fb]["ssm"].T.reshape(32, 64, 128) for b in range(B)])[None]
    cp = np.stack([res[b + fb]["conv"].reshape(128, 24, 3).transpose(2, 1, 0).reshape(3, 3072) for b in range(B)])[None]
    ys = np.concatenate([res[c]["ysT"].T.reshape(NSEQ_S, DS, D) for c in range(NCORES)])
    ks = np.concatenate([res[c]["ksT"].T.reshape(NSEQ_S, DS, 4, 64) for c in range(NCORES)])[None]
    vs = np.concatenate([res[c]["vsT"].T.reshape(NSEQ_S, DS, 4, 64) for c in range(NCORES)])[None]
    ls = np.concatenate([res[c]["lfsT"].T.reshape(NSEQ_S, DS, 16) for c in range(NCORES)])[None]
    ss = np.concatenate([res[c]["ssms"].transpose(0, 2, 1).reshape(NSEQ_S, 32, 64, 128) for c in range(NCORES)])[None]
    cs = np.concatenate([res[c]["convs"].reshape(NSEQ_S, 128, 24, 3).transpose(0, 3, 2, 1).reshape(NSEQ_S, 3, 3072) for c in range(NCORES)])[None]
    o = (yp, ys, kp, vp, lp, sp, cp, ks, vs, ls, ss, cs)
    return tuple(np.ascontiguousarray(a, dtype=np.float32) for a in o)
```
